# Optimizing a Trainium2 kernel written in Bass

```python
import math
import jax
import jax.numpy as jnp
from jax import lax
import numpy as np

D_MODEL = 1024
BATCH = 16
SEQ = 2048
DEPTH = 2

HEAD_DIM = 64
ATTN_HEADS = 8
DILATED_GROUPS = ((128, 1), (512, 4), (2048, 16))
N_GROUPS = len(DILATED_GROUPS)
ATTN_WIDTH = ATTN_HEADS * HEAD_DIM
ROPE_THETA = 10000.0
NEG_INF = -1e30
HYENA_WIDTH = D_MODEL // 2
SHORT_CONV = 3
FILTER_BANDS = 16
FILTER_EMB_DIM = 1 + 2 * FILTER_BANDS
FILTER_HIDDEN = 64
FILTER_INNER = 2
DECAY_TARGET = 1e-2
FAST_DECAY_PCT = 0.3
SLOW_DECAY_PCT = 1.5
N_BRANCHES = 2
HY_IN_WIDTH = 3 * HYENA_WIDTH
QKV_WIDTH = N_GROUPS * 3 * ATTN_WIDTH
IN_WIDTH = HY_IN_WIDTH + QKV_WIDTH + N_BRANCHES * D_MODEL
D_FF = 4 * D_MODEL
RMS_EPS = 1e-6

kernel_name = "hyena_dilated_attn_gated_hybrid"


def rmsnorm(x, gain):
    xf = x.astype(jnp.float32)
    y = xf * lax.rsqrt(jnp.mean(jnp.square(xf), axis=-1, keepdims=True) + RMS_EPS)
    return (y * gain.astype(jnp.float32)).astype(x.dtype)


def rotary(t, positions):
    half = t.shape[-1] // 2
    inv_freq = ROPE_THETA ** (-jnp.arange(half, dtype=jnp.float32) / half)
    ang = positions.astype(jnp.float32)[:, None] * inv_freq[None, :]
    cos = jnp.cos(ang)[None, :, None, :]
    sin = jnp.sin(ang)[None, :, None, :]
    tf = t.astype(jnp.float32)
    t1, t2 = tf[..., :half], tf[..., half:]
    return jnp.concatenate([t1 * cos - t2 * sin, t2 * cos + t1 * sin], axis=-1).astype(t.dtype)


def short_conv_centred(u, w, b):
    k_width = w.shape[0]
    s = u.shape[1]
    pad = k_width // 2
    up = jnp.pad(u, ((0, 0), (pad, k_width - 1 - pad), (0, 0)))
    out = b
    for tap in range(k_width):
        out = out + up[:, tap:tap + s] * w[tap]
    return out


def hyena_filters(length, w1, b1, w_inner, b_inner, w_out, freq):
    f32 = lambda a: a.astype(jnp.float32)
    n = jnp.arange(length, dtype=jnp.float32)
    t = n / max(length - 1, 1)
    bands = jnp.linspace(1e-4, FILTER_BANDS - 1, FILTER_BANDS, dtype=jnp.float32)
    ang = (2.0 * math.pi / length) * n[:, None] * bands[None, :]
    z = jnp.concatenate([t[:, None], jnp.cos(ang), -jnp.sin(ang)], axis=-1)
    fr = f32(freq)
    hid = jnp.sin(fr * (z @ f32(w1) + f32(b1)))
    for i in range(FILTER_INNER):
        hid = jnp.sin(fr * (hid @ f32(w_inner[i]) + f32(b_inner[i])))
    filt = (hid @ f32(w_out)).reshape(length, 2, HYENA_WIDTH)
    max_decay = math.log(DECAY_TARGET) / FAST_DECAY_PCT
    min_decay = math.log(DECAY_TARGET) / SLOW_DECAY_PCT
    deltas = jnp.abs(jnp.linspace(min_decay, max_decay, HYENA_WIDTH, dtype=jnp.float32))
    window = jnp.exp(-t[:, None] * deltas[None, :])
    filt = filt * window[:, None, :]
    return filt[:, 0], filt[:, 1]


def bidir_fftconv(u, h_fwd, h_bwd, d_skip):
    length, chans = u.shape[1], u.shape[2]
    n_fft = 2 * length
    kern = jnp.concatenate([h_fwd, jnp.zeros((1, chans), jnp.float32), h_bwd[:0:-1]], axis=0)
    uf = u.astype(jnp.float32)
    u_f = jnp.fft.rfft(uf, n=n_fft, axis=1)
    k_f = jnp.fft.rfft(kern, n=n_fft, axis=0)
    y = jnp.fft.irfft(u_f * k_f[None], n=n_fft, axis=1)[:, :length]
    return (y + uf * d_skip.astype(jnp.float32)).astype(u.dtype)


def dilated_window_attention(q, k, v, dilation, radius):
    b, s, h, e = q.shape
    n = s // dilation
    blk = radius
    nb = -(-n // blk)
    n_pad = nb * blk

    def to_sub(t):
        return t.reshape(b, n, dilation, h, e).transpose(0, 2, 1, 3, 4)

    def band(t):
        tp = jnp.pad(t, ((0, 0), (0, 0), (blk, n_pad - n + blk), (0, 0), (0, 0)))
        tp = tp.reshape(b, dilation, nb + 2, blk, h, e)
        return jnp.concatenate([tp[:, :, :-2], tp[:, :, 1:-1], tp[:, :, 2:]], axis=3)

    qs = jnp.pad(to_sub(q) * (HEAD_DIM ** -0.5), ((0, 0), (0, 0), (0, n_pad - n), (0, 0), (0, 0)))
    qb = qs.reshape(b, dilation, nb, blk, h, e)
    kb = band(to_sub(k))
    vb = band(to_sub(v))
    q_idx = jnp.arange(nb)[:, None] * blk + jnp.arange(blk)[None, :]
    k_idx = (jnp.arange(nb)[:, None] - 1) * blk + jnp.arange(3 * blk)[None, :]
    rel = k_idx[:, None, :] - q_idx[:, :, None]
    valid = (jnp.abs(rel) <= radius) & (k_idx[:, None, :] >= 0) & (k_idx[:, None, :] < n)
    scores = jnp.einsum('bdnqhe,bdnkhe->bdnhqk', qb, kb).astype(jnp.float32)
    scores = jnp.where(valid[None, None, :, None], scores, NEG_INF)
    lse = jax.nn.logsumexp(scores, axis=-1)
    probs = jnp.exp(scores - lse[..., None]).astype(v.dtype)
    out = jnp.einsum('bdnhqk,bdnkhe->bdnqhe', probs, vb)
    out = out.reshape(b, dilation, n_pad, h, e)[:, :, :n].transpose(0, 2, 1, 3, 4).reshape(b, s, h, e)
    lse = lse.transpose(0, 1, 2, 4, 3).reshape(b, dilation, n_pad, h)[:, :, :n]
    lse = lse.transpose(0, 2, 1, 3).reshape(b, s, h)
    return out, lse


def hybrid_mixer(hn, w_in, conv_w, conv_b, filt_w1, filt_b1, filt_w_inner, filt_b_inner,
                 filt_w_out, filt_freq, hy_skip, p_hy, p_att, w_o):
    b, s, _ = hn.shape
    proj = hn @ w_in
    hy_in = proj[..., :HY_IN_WIDTH]
    qkv = proj[..., HY_IN_WIDTH:HY_IN_WIDTH + QKV_WIDTH]
    gate_logits = proj[..., HY_IN_WIDTH + QKV_WIDTH:]

    u = short_conv_centred(hy_in, conv_w, conv_b)
    x0, x1, hv = jnp.split(u, 3, axis=-1)
    h_fwd, h_bwd = hyena_filters(s, filt_w1, filt_b1, filt_w_inner, filt_b_inner, filt_w_out, filt_freq)
    y_hy = x0 * bidir_fftconv(x1 * hv, h_fwd, h_bwd, hy_skip)

    qkv = qkv.reshape(b, s, N_GROUPS, 3, ATTN_HEADS, HEAD_DIM)
    positions = jnp.arange(s)
    outs, lses = [], []
    for g, (window, dilation) in enumerate(DILATED_GROUPS):
        q = rotary(qkv[:, :, g, 0], positions)
        k = rotary(qkv[:, :, g, 1], positions)
        o_g, lse_g = dilated_window_attention(q, k, qkv[:, :, g, 2], dilation, window // (2 * dilation))
        outs.append(o_g)
        lses.append(lse_g)
    group_w = jax.nn.softmax(jnp.stack(lses, axis=0), axis=0).astype(hn.dtype)
    y_att = jnp.einsum('gbsh,gbshe->bshe', group_w, jnp.stack(outs, axis=0)).reshape(b, s, ATTN_WIDTH)

    gates = jax.nn.sigmoid(gate_logits)
    g_hy, g_att = gates[..., :D_MODEL], gates[..., D_MODEL:]
    merged = g_hy * (y_hy @ p_hy) + g_att * (y_att @ p_att)
    return merged @ w_o


def setup_inputs(seed: int = 0) -> dict:
    key = jax.random.key(seed)
    ks = jax.random.split(key, 20)

    def normal(k, shape, scale):
        return jax.random.normal(k, shape, jnp.float32) * scale

    return {
        "x": normal(ks[0], (BATCH, SEQ, D_MODEL), 1.0),
        "norm_mix": 1.0 + normal(ks[1], (DEPTH, D_MODEL), 0.02),
        "w_in": normal(ks[2], (DEPTH, D_MODEL, IN_WIDTH), D_MODEL ** -0.5),
        "conv_w": normal(ks[3], (DEPTH, SHORT_CONV, HY_IN_WIDTH), SHORT_CONV ** -0.5),
        "conv_b": normal(ks[4], (DEPTH, HY_IN_WIDTH), 0.02),
        "filt_w1": normal(ks[5], (DEPTH, FILTER_EMB_DIM, FILTER_HIDDEN), FILTER_EMB_DIM ** -0.5),
        "filt_b1": normal(ks[6], (DEPTH, FILTER_HIDDEN), 0.02),
        "filt_w_inner": normal(ks[7], (DEPTH, FILTER_INNER, FILTER_HIDDEN, FILTER_HIDDEN), FILTER_HIDDEN ** -0.5),
        "filt_b_inner": normal(ks[8], (DEPTH, FILTER_INNER, FILTER_HIDDEN), 0.02),
        "filt_w_out": normal(ks[9], (DEPTH, FILTER_HIDDEN, 2 * HYENA_WIDTH), 0.1 * FILTER_HIDDEN ** -0.5),
        "filt_freq": 1.0 + normal(ks[10], (DEPTH, FILTER_HIDDEN), 0.02),
        "hy_skip": normal(ks[11], (DEPTH, HYENA_WIDTH), 0.5),
        "p_hy": normal(ks[12], (DEPTH, HYENA_WIDTH, D_MODEL), HYENA_WIDTH ** -0.5),
        "p_att": normal(ks[13], (DEPTH, ATTN_WIDTH, D_MODEL), ATTN_WIDTH ** -0.5),
        "w_o": normal(ks[14], (DEPTH, D_MODEL, D_MODEL), D_MODEL ** -0.5),
        "norm_ffn": 1.0 + normal(ks[15], (DEPTH, D_MODEL), 0.02),
        "w_ff1": normal(ks[16], (DEPTH, D_MODEL, D_FF), D_MODEL ** -0.5),
        "w_ff2": normal(ks[17], (DEPTH, D_FF, D_MODEL), D_FF ** -0.5),
        "norm_final": 1.0 + normal(ks[18], (D_MODEL,), 0.02),
    }


def reference(x, norm_mix, w_in, conv_w, conv_b, filt_w1, filt_b1, filt_w_inner, filt_b_inner,
              filt_w_out, filt_freq, hy_skip, p_hy, p_att, w_o, norm_ffn, w_ff1, w_ff2, norm_final):
    for layer in range(DEPTH):
        hn = rmsnorm(x, norm_mix[layer])
        x = x + hybrid_mixer(hn, w_in[layer], conv_w[layer], conv_b[layer], filt_w1[layer], filt_b1[layer],
                             filt_w_inner[layer], filt_b_inner[layer], filt_w_out[layer], filt_freq[layer],
                             hy_skip[layer], p_hy[layer], p_att[layer], w_o[layer])
        hn = rmsnorm(x, norm_ffn[layer])
        x = x + jnp.square(jax.nn.relu(hn @ w_ff1[layer])) @ w_ff2[layer]
    return rmsnorm(x, norm_final)
```

```python
import math
from contextlib import ExitStack

import numpy as np
import ml_dtypes
import concourse.bass as bass
import concourse.mybir as mybir
from concourse.bass_utils import run_bass_kernel_spmd

F32 = mybir.dt.float32
BF16 = mybir.dt.bfloat16
AF = mybir.ActivationFunctionType
ALU = mybir.AluOpType

D = 1024
T = 2048
DEPTH = 2
NCORE = 8
NSEQ = 2
INW = 8192
HW = 512
DFF = 4096
NFFT = 4096
GROUPS = ((128, 1), (512, 4), (2048, 16))
EPS = 1e-6

AW = 52224
C_IDF, C_ONF, C_IDB, C_EPS = 0, 128, 256, 608
C_GMIX, C_GFFN, C_GFIN, C_CONV, C_SF, C_SKIP = 616, 632, 648, 656, 752, 768
O_STG = 2048
O_WB = 6144
NWB = 8
O_HT = 14336
O_X = 22528
O_U = 26624
O_TMP = 38912


class Sched:
    NDS = 24

    def __init__(self, nc, es):
        self.nc = nc
        self.eng = {'pe': nc.tensor, 'dve': nc.vector, 'act': nc.scalar, 'pool': nc.gpsimd, 'sp': nc.sync}
        self.sem = {e: es.enter_context(nc.semaphore('s_' + e)) for e in ('pe', 'dve', 'act', 'pool')}
        self.cnt = {e: 0 for e in self.sem}
        self.dsem = {q: [es.enter_context(nc.semaphore('d_%s_%d' % (q, i))) for i in range(self.NDS)]
                     for q in ('sp', 'act')}
        self.dcnt = {q: 0 for q in self.dsem}
        self.waited = {e: {} for e in self.eng}
        self.lastw = {}
        self.readers = {}

    def _wait(self, stream, tok):
        ename, sem, val = tok
        if ename == 'pe' and stream == 'pe':
            return
        w = self.waited[stream]
        k = id(sem)
        if w.get(k, 0) >= val:
            return
        w[k] = val
        self.eng[stream].wait_ge(sem, val)

    def _deps(self, stream, reads, writes):
        for k in reads:
            t = self.lastw.get(k)
            if t is not None:
                self._wait(stream, t)
        for k in writes:
            t = self.lastw.get(k)
            if t is not None:
                self._wait(stream, t)
            for t in self.readers.get(k, {}).values():
                self._wait(stream, t)

    def _commit(self, tok, reads, writes):
        for k in reads:
            if k not in writes:
                r = self.readers.setdefault(k, {})
                o = r.get(id(tok[1]))
                if o is None or o[2] < tok[2]:
                    r[id(tok[1])] = tok
        for k in writes:
            self.lastw[k] = tok
            self.readers[k] = {}

    def op(self, e, fn, reads=(), writes=(), inc=True):
        self._deps(e, reads, writes)
        ins = fn(self.eng[e])
        if inc:
            ins.then_inc(self.sem[e], 1)
            self.cnt[e] += 1
            tok = (e, self.sem[e], self.cnt[e])
        else:
            tok = (e, self.sem[e], self.cnt[e] + 1)
        self._commit(tok, reads, writes)
        return tok

    def dma(self, q, out, in_, reads=(), writes=(), **kw):
        j = self.dcnt[q]
        self.dcnt[q] += 1
        sem = self.dsem[q][j % self.NDS]
        rnd = j // self.NDS
        if rnd > 0:
            self._wait(q, ('dma', sem, 16 * rnd))
        self._deps(q, reads, writes)
        self.eng[q].dma_start(out=out, in_=in_, **kw).then_inc(sem, 16)
        tok = ('dma', sem, 16 * (rnd + 1))
        self._commit(tok, reads, writes)
        return tok

    def barrier_all(self):
        toks = []
        for e in self.sem:
            if self.cnt[e]:
                toks.append((e, self.sem[e], self.cnt[e]))
        for q in self.dsem:
            n = self.dcnt[q]
            for i in range(min(n, self.NDS)):
                last_j = ((n - 1 - i) // self.NDS) * self.NDS + i
                toks.append(('dma', self.dsem[q][i], 16 * (last_j // self.NDS + 1)))
        for s in self.eng:
            for t in toks:
                if t[0] == s and s != 'pe':
                    pass
                self._wait(s, t)


def perm_view(ap2d, d, j0, ln):
    if d == 1:
        return ap2d[:, j0:j0 + ln]
    n = T // d
    v = ap2d.rearrange("p (m r) -> p r m", r=d)
    if ln <= n:
        r = j0 // n
        m0 = j0 % n
        assert m0 + ln <= n
        return v[:, r, m0:m0 + ln]
    assert j0 % n == 0 and ln % n == 0
    return v[:, j0 // n:j0 // n + ln // n, :]


def like(ap_contig, ref):
    if len(ref.shape) == 3:
        return ap_contig.rearrange("p (a b) -> p a b", b=ref.shape[2])
    return ap_contig


class Prog:
    def __init__(self, nc, es, nseq, depth, dbg=None):
        self.nc = nc
        self.nseq = nseq
        self.depth = depth
        self.dbg = dbg or {}
        S = self.S = Sched(nc, es)
        dt = nc.dram_tensor
        self.x = dt("x", [nseq, T, D], F32, kind="ExternalInput").ap()
        self.w = {}
        for name, shp in [("norm_mix", [DEPTH, D]), ("w_in", [DEPTH, D, INW]), ("conv_w", [DEPTH, 3, 1536]),
                          ("conv_b", [DEPTH, 1536]), ("filt_w1", [DEPTH, 33, 64]), ("filt_b1", [DEPTH, 64]),
                          ("filt_w_inner", [DEPTH, 2, 64, 64]), ("filt_b_inner", [DEPTH, 2, 64]),
                          ("filt_w_out", [DEPTH, 64, 1024]), ("filt_freq", [DEPTH, 64]), ("hy_skip", [DEPTH, 512]),
                          ("p_hy", [DEPTH, 512, D]), ("p_att", [DEPTH, 512, D]), ("w_o", [DEPTH, D, D]),
                          ("norm_ffn", [DEPTH, D]), ("w_ff1", [DEPTH, D, DFF]), ("w_ff2", [DEPTH, DFF, D]),
                          ("norm_final", [D])]:
            self.w[name] = dt(name, shp, F32, kind="ExternalInput").ap()
        self.c_f32 = dt("c_f32", [128, 256], F32, kind="ExternalInput").ap()
        self.c_bf = dt("c_bf", [128, 128 + 64 + 128 + 384], BF16, kind="ExternalInput").ap()
        self.c_cs = dt("c_cs", [128, 2 * T], F32, kind="ExternalInput").ap()
        self.c_mf = dt("c_mf", [32, 128, 2048], BF16, kind="ExternalInput").ap()
        self.c_mi = dt("c_mi", [4, 8, 128, 2048], BF16, kind="ExternalInput").ap()
        self.c_zt = dt("c_zt", [33, T], F32, kind="ExternalInput").ap()
        self.c_win = dt("c_win", [16, 128, 512], F32, kind="ExternalInput").ap()
        self.c_sf = dt("c_sf", [128, 16], F32, kind="ExternalInput").ap()
        self.out = dt("out", [nseq, T, D], F32, kind="ExternalOutput").ap()
        self.kspec = dt("kspec", [DEPTH, 16, 128, 1024], F32, kind="Internal").ap()
        self.xs = dt("xs", [128, 8 * T], F32, kind="Internal").ap()
        self.dbg_t = {}
        for name, shp in self.dbg.items():
            self.dbg_t[name] = dt("dbg_" + name, list(shp), F32, kind="ExternalOutput").ap()
        self.A = es.enter_context(nc.sbuf_tensor("arena", [128, AW], F32))
        self.ps = [es.enter_context(nc.psum_tensor("ps%d" % i, [128, 512], F32)) for i in range(8)]
        self.psr = {}
        self.pre = {}
        self.bank_rng = (0, 8)
        self.wbi = 0
        self.stgi = 0
        self.uid = 0

    def fv(self, off, n, parts=128):
        return self.A[0:parts, off:off + n]

    def bv(self, off, nwords, parts=128):
        return self.A[0:parts, off:off + nwords].bitcast(BF16)

    def bank(self, lo=None, hi=None):
        if lo is None:
            lo, hi = self.bank_rng
        k = (lo, hi)
        i = self.psr.get(k, lo)
        self.psr[k] = lo + (i + 1 - lo) % (hi - lo)
        return self.ps[i], ('ps', i)

    def key(self, name):
        self.uid += 1
        return (name, self.uid)

    def load_w(self, src, kcs, ncols, scale=None, ceng='pool'):
        S = self.S
        si = self.stgi
        self.stgi = (self.stgi + 1) % 2
        wi = self.wbi
        self.wbi = (self.wbi + 1) % NWB
        n = kcs * ncols
        assert n <= 2048
        stg = self.fv(O_STG + si * 2048, n).rearrange("p (k n) -> p k n", n=ncols)
        wb = self.bv(O_WB + wi * 1024, n // 2).rearrange("p (k n) -> p k n", n=ncols)
        S.dma('sp', stg, src.rearrange("(k p) n -> p k n", p=128), writes=[('stg', si)])
        if scale is None:
            if ceng == 'act':
                S.op('act', lambda e: e.activation(out=wb, in_=stg, func=AF.Copy), reads=[('stg', si)],
                     writes=[('wb', wi)])
            else:
                S.op(ceng, lambda e: e.tensor_copy(out=wb, in_=stg), reads=[('stg', si)], writes=[('wb', wi)])
        else:
            for k in range(kcs):
                S.op('pool', lambda e, k=k: e.tensor_scalar(out=wb[:, k, :], in0=stg[:, k, :],
                                                             scalar1=scale[:, k:k + 1], scalar2=None, op0=ALU.mult),
                     reads=[('stg', si)], writes=[('wb', wi)])
        return wb, ('wb', wi)

    def load_w_dma(self, src, kcs, ncols):
        si = self.stgi
        self.stgi = (self.stgi + 1) % 2
        n = kcs * ncols
        stg = self.fv(O_STG + si * 2048, n).rearrange("p (k n) -> p k n", n=ncols)
        self.S.dma('sp', stg, src.rearrange("(k p) n -> p k n", p=128), writes=[('stg', si)])
        return (stg, si, n, ncols)

    def load_w_cast(self, h, ceng='act'):
        stg, si, n, ncols = h
        wi = self.wbi
        self.wbi = (self.wbi + 1) % NWB
        wb = self.bv(O_WB + wi * 1024, n // 2).rearrange("p (k n) -> p k n", n=ncols)
        if ceng == 'act':
            self.S.op('act', lambda e: e.activation(out=wb, in_=stg, func=AF.Copy), reads=[('stg', si)],
                      writes=[('wb', wi)])
        else:
            self.S.op(ceng, lambda e: e.tensor_copy(out=wb, in_=stg), reads=[('stg', si)], writes=[('wb', wi)])
        return wb, ('wb', wi)

    def pf(self, tag, loader):
        if tag not in self.pre:
            self.pre[tag] = loader()

    def take(self, tag, loader):
        if tag in self.pre:
            return self.pre.pop(tag)
        return loader()

    def load_bf(self, src, shape3):
        S = self.S
        wi = self.wbi
        self.wbi = (self.wbi + 1) % NWB
        a, b = shape3
        wb = self.bv(O_WB + wi * 1024, a * b // 2)
        S.dma('sp', wb, src, writes=[('wb', wi)])
        return wb.rearrange("p (a b) -> p a b", b=b), ('wb', wi)

    def dump(self, name, src_ap, reads):
        if name in self.dbg_t:
            self.S.dma('sp', self.dbg_t[name], src_ap, reads=reads, writes=[('dbg', name)])

    def load_consts(self):
        S = self.S
        A = self.A
        S.dma('sp', self.fv(C_IDF, 256), self.c_f32, writes=['consts'])
        S.dma('sp', self.bv(C_IDB, 352), self.c_bf, writes=['consts'])
        S.dma('sp', self.fv(C_SF, 16), self.c_sf, writes=['consts'])
        S.op('pool', lambda e: e.memset(self.fv(C_EPS, 1), EPS), writes=['consts'])
        with self.nc.allow_non_contiguous_dma(reason="tiny param loads"):
            for l in range(DEPTH):
                S.dma('sp', self.fv(C_GMIX + 8 * l, 8), self.w["norm_mix"][l].rearrange("(k p) -> p k", p=128),
                      writes=['consts'])
                S.dma('sp', self.fv(C_GFFN + 8 * l, 8), self.w["norm_ffn"][l].rearrange("(k p) -> p k", p=128),
                      writes=['consts'])
                cv = self.fv(C_CONV + 48 * l, 48).rearrange("p (c t) -> p c t", t=4)
                for t in range(3):
                    S.dma('sp', cv[:, :, t], self.w["conv_w"][l, t].rearrange("(c p) -> p c", p=128),
                          writes=['consts'])
                S.dma('sp', cv[:, :, 3], self.w["conv_b"][l].rearrange("(c p) -> p c", p=128), writes=['consts'])
            S.dma('sp', self.fv(C_GFIN, 8), self.w["norm_final"].rearrange("(k p) -> p k", p=128), writes=['consts'])
        for l in range(DEPTH):
            S.dma('sp', self.fv(C_SKIP + 512 * l, 512, parts=1), self.w["hy_skip"][l:l + 1, :], writes=['consts'])
        self.idf = self.fv(C_IDF, 128)
        self.onf = self.fv(C_ONF, 128)
        cb = self.bv(C_IDB, 352)
        self.idb = cb[:, 0:128]
        self.onb = cb[:, 128:192]
        self.rot = cb[:, 192:320]
        self.mask = cb[:, 320:704]
        self.eps = self.fv(C_EPS, 1)
        S.barrier_all()

    def filter_prologue(self, l):
        S = self.S
        W = self.w
        o = O_HT
        zt = self.fv(o, T, parts=33); o += T
        hA = self.fv(o, T, parts=64); o += T
        hB = self.fv(o, T, parts=64); o += T
        w1 = self.fv(o, 64, parts=33); o += 64
        wi0 = self.fv(o, 64, parts=64); o += 64
        wi1 = self.fv(o, 64, parts=64); o += 64
        wo = self.fv(o, 1024, parts=64); o += 1024
        sm = self.fv(o, 16, parts=64); o += 16
        wrps = [self.fv(o + 512 * i, 512, parts=64) for i in range(2)]; o += 1024
        hf = [self.fv(o + 512 * i, 512) for i in range(2)]; o += 1024
        hb = [self.fv(o + 512 * i, 512) for i in range(2)]; o += 1024
        wn = [self.fv(o + 512 * i, 512) for i in range(2)]; o += 1024
        hsum = self.bv(o, 4096).rearrange("p (a b) -> p a b", b=512); o += 4096
        hdif = self.bv(o, 4096).rearrange("p (a b) -> p a b", b=512); o += 4096
        kout = [self.fv(o + 1024 * i, 1024) for i in range(2)]; o += 2048
        assert o <= AW
        S.dma('sp', zt, self.c_zt, writes=['f_zt'])
        S.dma('sp', w1, W["filt_w1"][l], writes=['f_w'])
        S.dma('sp', wi0, W["filt_w_inner"][l, 0], writes=['f_w'])
        S.dma('sp', wi1, W["filt_w_inner"][l, 1], writes=['f_w'])
        S.dma('sp', wo, W["filt_w_out"][l], writes=['f_w'])
        with self.nc.allow_non_contiguous_dma(reason="tiny param loads"):
            S.dma('sp', sm[:, 0:1], W["filt_b1"][l].rearrange("(p o) -> p o", o=1), writes=['f_sm'])
            S.dma('sp', sm[:, 1:3], W["filt_b_inner"][l].rearrange("i p -> p i"), writes=['f_sm'])
            S.dma('sp', sm[:, 3:4], W["filt_freq"][l].rearrange("(p o) -> p o", o=1), writes=['f_sm'])
        for i in range(3):
            S.op('dve', lambda e, i=i: e.tensor_tensor(out=sm[:, 4 + i:5 + i], in0=sm[:, i:i + 1], in1=sm[:, 3:4],
                                                      op=ALU.mult), reads=['f_sm'], writes=['f_sm'])
        lay = [(w1, 33, zt, 'f_zt'), (wi0, 64, hA, 'hA'), (wi1, 64, hB, 'hB')]
        outs = [(hA, 'hA'), (hB, 'hB'), (hA, 'hA')]
        wi_ = 0
        for li, (wt, kk, src, skey) in enumerate(lay):
            dst, dkey = outs[li]
            for tb in range(4):
                sl = slice(tb * 512, (tb + 1) * 512)
                b, bk = self.bank()
                rk = [(skey, tb)] if li > 0 else ['f_zt']
                S.op('pe', lambda e: e.matmul(b[0:64, :], lhsT=wt[0:kk, :], rhs=src[0:kk, sl], start=True, stop=True),
                     reads=['f_w'] + rk, writes=[bk])
                S.op('dve', lambda e: e.tensor_scalar(out=dst[:, sl], in0=b[0:64, :], scalar1=sm[:, 3:4],
                                                      scalar2=sm[:, 4 + li:5 + li], op0=ALU.mult, op1=ALU.add),
                     reads=['f_sm'], writes=[bk, (dkey, tb)])
                for cmp_op, thr, shift in ((ALU.is_gt, math.pi, -2 * math.pi), (ALU.is_lt, -math.pi, 2 * math.pi)):
                    wr = wrps[wi_ % 2]
                    wkey = ('wrp', wi_ % 2)
                    wi_ += 1
                    S.op('dve', lambda e: e.tensor_scalar(out=wr[:, :], in0=dst[:, sl], scalar1=thr, scalar2=shift,
                                                          op0=cmp_op, op1=ALU.mult),
                         reads=[(dkey, tb)], writes=[wkey])
                    S.op('dve', lambda e: e.tensor_tensor(out=dst[:, sl], in0=dst[:, sl], in1=wr[:, :], op=ALU.add),
                         reads=[wkey], writes=[(dkey, tb)])
                S.op('act', lambda e: e.activation(out=dst[:, sl], in_=dst[:, sl], func=AF.Sin),
                     writes=[(dkey, tb)])
        hid = hA
        skip = self.fv(C_SKIP + 512 * l, 512, parts=1)
        for mc in range(16):
            i = mc % 2
            S.dma('sp', wn[i], self.c_win[mc], writes=[('f_wn', i)])
            for half, dst in ((0, hf[i]), (1, hb[i])):
                b, bk = self.bank()
                S.op('pe', lambda e: e.matmul(b[:, :], lhsT=hid[:, mc * 128:(mc + 1) * 128],
                                              rhs=wo[:, half * 512:(half + 1) * 512], start=True, stop=True),
                     reads=['f_w', ('hA', mc // 4)], writes=[bk])
                S.op('dve', lambda e: e.tensor_tensor(out=dst, in0=b[:, :], in1=wn[i], op=ALU.mult),
                     reads=[('f_wn', i)], writes=[bk, ('f_h', half, i)])
            if mc == 0:
                S.op('dve', lambda e: e.memset(hb[i][0:1, :], 0.0), writes=[('f_h', 1, i)])
                S.op('dve', lambda e: e.tensor_tensor(out=hf[i][0:1, :], in0=hf[i][0:1, :], in1=skip, op=ALU.add),
                     writes=[('f_h', 0, i)])
            S.op('pool', lambda e: e.tensor_tensor(out=hsum[:, mc, :], in0=hf[i], in1=hb[i], op=ALU.add),
                 reads=[('f_h', 0, i), ('f_h', 1, i)], writes=['f_hs'])
            S.op('pool', lambda e: e.tensor_tensor(out=hdif[:, mc, :], in0=hb[i], in1=hf[i], op=ALU.subtract),
                 reads=[('f_h', 0, i), ('f_h', 1, i)], writes=['f_hs'])
        for fc in range(16):
            i = fc % 2
            ko = kout[i]
            mre, kre = self.load_bf(self.c_mf[fc], (16, 128))
            mim, kim = self.load_bf(self.c_mf[16 + fc], (16, 128))
            bre, bkre = self.bank()
            bim, bkim = self.bank()
            for mc in range(16):
                S.op('pe', lambda e: e.matmul(bre[:, :], lhsT=mre[:, mc, :], rhs=hsum[:, mc, :], start=(mc == 0),
                                              stop=(mc == 15)), reads=[kre, 'f_hs'], writes=[bkre], inc=(mc == 15))
            for mc in range(16):
                S.op('pe', lambda e: e.matmul(bim[:, :], lhsT=mim[:, mc, :], rhs=hdif[:, mc, :], start=(mc == 0),
                                              stop=(mc == 15)), reads=[kim, 'f_hs'], writes=[bkim], inc=(mc == 15))
            sf = self.fv(C_SF + fc, 1)
            S.op('act', lambda e: e.activation(out=ko[:, 0:512], in_=bre[:, :], func=AF.Identity, scale=sf),
                 reads=['consts'], writes=[bkre, ('f_ko', i)])
            S.op('act', lambda e: e.activation(out=ko[:, 512:1024], in_=bim[:, :], func=AF.Identity, scale=sf),
                 reads=['consts'], writes=[bkim, ('f_ko', i)])
            if fc == 0:
                bn, bkn = self.bank()
                for mc in range(16):
                    S.op('pe', lambda e: e.matmul(bn[0:1, :], lhsT=mim[:, mc, 0:1], rhs=hsum[:, mc, :],
                                                  start=(mc == 0), stop=(mc == 15)),
                         reads=[kim, 'f_hs'], writes=[bkn], inc=(mc == 15))
                S.op('act', lambda e: e.activation(out=ko[0:1, 512:1024], in_=bn[0:1, :], func=AF.Identity,
                                                   scale=1.0 / NFFT), writes=[bkn, ('f_ko', i)])
            S.dma('act', self.kspec[l, fc], ko, reads=[('f_ko', i)], writes=['kspec'])
        S.barrier_all()

    def xT(self):
        return self.fv(O_X, 8 * T).rearrange("p (c t) -> p c t", t=T)

    def hT(self):
        return self.bv(O_HT, 8192).rearrange("p (c t) -> p c t", t=T)

    def load_x(self, s):
        S = self.S
        xT = self.xT()
        xin = self.fv(O_TMP, 4096).rearrange("p (a d) -> p a d", d=D)
        for tb in range(4):
            S.dma('sp', xin, self.x[s, tb * 512:(tb + 1) * 512, :].rearrange("(a p) d -> p a d", p=128),
                  writes=['xin'])
            for c in range(8):
                b, bk = self.bank()
                for a in range(4):
                    S.op('pe', lambda e: e.transpose(out=b[:, a * 128:(a + 1) * 128],
                                                     in_=xin[:, a, c * 128:(c + 1) * 128], identity=self.idf),
                         reads=['xin', 'consts'], writes=[bk], inc=(a == 3))
                S.op('act', lambda e: e.activation(out=xT[:, c, tb * 512:(tb + 1) * 512], in_=b[:, :], func=AF.Copy),
                     writes=[bk, ('xT', c, tb)])
        S.barrier_all()

    def norm(self, tmp_off, gain):
        S = self.S
        xT, hT = self.xT(), self.hT()
        sq = [self.fv(tmp_off + 512 * i, 512) for i in range(2)]
        rs_ = [self.fv(tmp_off + 1024 + 512 * i, 512) for i in range(4)]
        banks = []
        for tb in range(4):
            sl = slice(tb * 512, (tb + 1) * 512)
            b, bk = self.bank()
            banks.append((b, bk))
            for c in range(8):
                i = c % 2
                S.op('act', lambda e: e.activation(out=sq[i], in_=xT[:, c, sl], func=AF.Square),
                     reads=[('xT', c, tb)], writes=[('sq', i)])
                S.op('pe', lambda e: e.matmul(b[:, :], lhsT=self.onf, rhs=sq[i], start=(c == 0), stop=(c == 7)),
                     reads=[('sq', i), 'consts'], writes=[bk])
        for tb in range(4):
            b, bk = banks[tb]
            S.op('act', lambda e: e.activation(out=rs_[tb], in_=b[:, :], func=AF.Ln, bias=self.eps, scale=1.0 / D),
                 reads=['consts'], writes=[bk, ('rs', tb)])
        for tb in range(4):
            S.op('act', lambda e: e.activation(out=rs_[tb], in_=rs_[tb], func=AF.Exp, scale=-0.5),
                 writes=[('rs', tb)])
        for tb in range(4):
            sl = slice(tb * 512, (tb + 1) * 512)
            for c in range(8):
                S.op('dve', lambda e: e.scalar_tensor_tensor(out=hT[:, c, sl], in0=xT[:, c, sl], scalar=gain[:, c:c + 1],
                                                             in1=rs_[tb], op0=ALU.mult, op1=ALU.mult),
                     reads=[('xT', c, tb), ('rs', tb), 'consts'], writes=[('hT', tb)])

    def spill_x(self):
        S = self.S
        for c in (2, 3, 4, 5, 6, 7, 0, 1):
            S.dma('sp', self.xs[:, c * T:(c + 1) * T], self.fv(O_X + c * T, T),
                  reads=[('xT', c, tb) for tb in range(4)], writes=[('xs', c)])

    def prefetch_hyena(self, l):
        W = self.w["w_in"][l]
        for j in range(2):
            self.pf(('hy', l, j), lambda: self.load_w(W[:, 256 * j:256 * j + 256], 8, 256))
        for j in range(2):
            self.pf(('hyA', l, j), lambda: self.load_w(W[:, 512 + 256 * j:512 + 256 * j + 256], 8, 256))
            self.pf(('hyB', l, j), lambda: self.load_w(W[:, 1024 + 256 * j:1024 + 256 * j + 256], 8, 256))

    def prefetch_attention(self, l):
        W = self.w["w_in"][l]
        base = 1536
        for which in range(3):
            self.pf(('att', l, 0, 0, which),
                    lambda: self.load_w(W[:, base + 512 * which: base + 512 * which + 128], 8, 128))

    def prefetch_merge(self, l):
        W = self.w["w_in"][l]
        self.pf(('mg', l, 0, 0), lambda: self.load_w(W[:, 6144: 6144 + 128], 8, 128))
        self.pf(('mg', l, 0, 1), lambda: self.load_w(W[:, 7168: 7168 + 128], 8, 128))
        self.pf(('mg', l, 0, 2), lambda: self.load_w(self.w["p_hy"][l][:, 0:128], 4, 128))
        self.pf(('mg', l, 0, 3), lambda: self.load_w(self.w["p_att"][l][:, 0:128], 4, 128))

    def prefetch_wo(self, l):
        for j in range(2):
            self.pf(('wo', l, j), lambda: self.load_w(self.w["w_o"][l][:, j * 256:(j + 1) * 256], 8, 256))

    def prefetch_ffn(self, l):
        W1 = self.w["w_ff1"][l]
        self.pf(('w1', l, 0), lambda: [self.load_w(W1[:, 256 * i: 256 * i + 256], 8, 256) for i in range(2)])

    def hyena(self, l):
        S = self.S
        hT = self.hT()
        W = self.w["w_in"][l]
        gmix = self.fv(C_GMIX + 8 * l, 8)
        cv = self.fv(C_CONV + 48 * l, 48).rearrange("p (c t) -> p c t", t=4)
        o = O_U
        x0 = self.fv(o, 4 * T).rearrange("p (c t) -> p c t", t=T); o += 4 * T
        u = self.bv(o, 4096).rearrange("p (c t) -> p c t", t=T); o += 4096
        o_b = o
        raws = [self.fv(o, T), self.fv(o + T, T)]; o += 2 * T
        rawi = [0]
        cA = self.fv(o, T); o += T
        cB = self.fv(o, T); o += T
        assert o <= AW

        def conv_chunk(wt, wk, col, ch, dst, dkey, extra=()):
            raw = raws[rawi[0] % 2]
            rkey = ('raw', rawi[0] % 2)
            rawi[0] += 1
            for tb in range(4):
                b, bk = self.bank()
                for kc in range(8):
                    S.op('pe', lambda e: e.matmul(b[:, :], lhsT=wt[:, kc, col:col + 128],
                                                  rhs=hT[:, kc, tb * 512:(tb + 1) * 512], start=(kc == 0),
                                                  stop=(kc == 7)), reads=[wk, ('hT', tb)], writes=[bk], inc=(kc == 7))
                S.op('act', lambda e: e.activation(out=raw[:, tb * 512:(tb + 1) * 512], in_=b[:, :], func=AF.Copy),
                     writes=[bk, rkey])
                S.op('act', lambda e: e.activation(out=dst[:, tb * 512:(tb + 1) * 512], in_=b[:, :], func=AF.Identity,
                                                   scale=cv[:, ch, 1:2], bias=cv[:, ch, 3:4]),
                     reads=['consts'], writes=[bk, dkey] + list(extra))
            S.op('dve', lambda e: e.scalar_tensor_tensor(out=dst[:, 1:T], in0=raw[:, 0:T - 1], scalar=cv[:, ch, 0:1],
                                                         in1=dst[:, 1:T], op0=ALU.mult, op1=ALU.add),
                 reads=[rkey, 'consts'], writes=[dkey])
            S.op('dve', lambda e: e.scalar_tensor_tensor(out=dst[:, 0:T - 1], in0=raw[:, 1:T], scalar=cv[:, ch, 2:3],
                                                         in1=dst[:, 0:T - 1], op0=ALU.mult, op1=ALU.add),
                 reads=[rkey, 'consts'], writes=[dkey])

        for j in range(2):
            wt, wk = self.take(('hy', l, j), lambda: self.load_w(W[:, 256 * j:256 * j + 256], 8, 256))
            for i in range(2):
                ch = 2 * j + i
                conv_chunk(wt, wk, 128 * i, ch, x0[:, ch, :], ('x0', ch), [('xT', 2 + ch, t) for t in range(4)])
        for j in range(2):
            wa, wak = self.take(('hyA', l, j), lambda: self.load_w(W[:, 512 + 256 * j:512 + 256 * j + 256], 8, 256))
            wb_, wbk = self.take(('hyB', l, j), lambda: self.load_w(W[:, 1024 + 256 * j:1024 + 256 * j + 256], 8, 256))
            for i in range(2):
                ch = 2 * j + i
                conv_chunk(wa, wak, 128 * i, 4 + ch, cA, 'cA')
                conv_chunk(wb_, wbk, 128 * i, 8 + ch, cB, 'cB')
                S.op('dve', lambda e: e.tensor_tensor(out=u[:, ch, :], in0=cA, in1=cB, op=ALU.mult),
                     reads=['cA', 'cB'], writes=[('u', ch)] + [('xT', 6 + ch // 2, t) for t in range(4)])
        uT = self.bv(O_U + 20480, 4096).rearrange("p (a c) -> p a c", c=512)
        Y = self.bv(O_U + 12288, 8192).rearrange("p (f c) -> p f c", c=512)
        ksp = [self.fv(O_U + 8192 + 1024 * i, 1024) for i in range(2)]
        tt_ = [self.fv(O_U + 10240 + 512 * i, 512) for i in range(4)]
        for tt in range(16):
            b, bk = self.bank()
            bb = b[:, :].bitcast(BF16)
            for cc in range(4):
                S.op('pe', lambda e: e.transpose(out=bb[:, cc * 128:(cc + 1) * 128],
                                                 in_=u[:, cc, tt * 128:(tt + 1) * 128], identity=self.idb),
                     reads=[('u', cc), 'consts'], writes=[bk], inc=(cc == 3))
            S.op('act', lambda e: e.activation(out=uT[:, tt, :], in_=bb[:, 0:512], func=AF.Copy),
                 writes=[bk, 'uT'])
        S.op('dve', lambda e: e.memset(tt_[0][0:1, 0:2], 0.0),
             writes=[('u', c_) for c_ in range(4)] + [('raw', 0), ('raw', 1), 'cA', 'cB', ('ksp', 0), ('ksp', 1),
                                                      't1', 't2', 't3', 't4', 'Y'])
        for fc in range(16):
            i = fc % 2
            mre, kre = self.load_bf(self.c_mf[fc], (16, 128))
            mim, kim = self.load_bf(self.c_mf[16 + fc], (16, 128))
            S.dma('sp', ksp[i], self.kspec[l, fc], reads=['kspec'], writes=[('ksp', i)])
            bre, bkre = self.bank()
            bim, bkim = self.bank()
            for mc in range(16):
                S.op('pe', lambda e: e.matmul(bre[:, :], lhsT=mre[:, mc, :], rhs=uT[:, mc, :], start=(mc == 0),
                                              stop=(mc == 15)), reads=[kre, 'uT'], writes=[bkre], inc=(mc == 15))
            for mc in range(16):
                S.op('pe', lambda e: e.matmul(bim[:, :], lhsT=mim[:, mc, :], rhs=uT[:, mc, :], start=(mc == 0),
                                              stop=(mc == 15)), reads=[kim, 'uT'], writes=[bkim], inc=(mc == 15))
            Ka = ksp[i][:, 0:512]
            Kb = ksp[i][:, 512:1024]
            t1, t2, t3, t4 = tt_
            S.op('dve', lambda e: e.tensor_tensor(out=t1, in0=bre[:, :], in1=Ka, op=ALU.mult),
                 reads=[('ksp', i)], writes=[bkre, 't1'])
            S.op('dve', lambda e: e.tensor_tensor(out=t2, in0=bim[:, :], in1=Kb, op=ALU.mult),
                 reads=[('ksp', i)], writes=[bkim, 't2'])
            S.op('dve', lambda e: e.tensor_tensor(out=t3, in0=bre[:, :], in1=Kb, op=ALU.mult),
                 reads=[('ksp', i)], writes=[bkre, 't3'])
            S.op('dve', lambda e: e.tensor_tensor(out=t4, in0=bim[:, :], in1=Ka, op=ALU.mult),
                 reads=[('ksp', i)], writes=[bkim, 't4'])
            S.op('dve', lambda e: e.tensor_tensor(out=Y[:, fc, :], in0=t1, in1=t2, op=ALU.add),
                 reads=['t1', 't2'], writes=['Y'])
            S.op('dve', lambda e: e.tensor_tensor(out=Y[:, 16 + fc, :], in0=t4, in1=t3, op=ALU.subtract),
                 reads=['t3', 't4'], writes=['Y'])
            if fc == 0:
                S.op('dve', lambda e: e.tensor_copy(out=Y[0:1, 0, :], in_=t1[0:1, :]), reads=['t1'], writes=['Y'])
                S.op('dve', lambda e: e.tensor_copy(out=Y[0:1, 16, :], in_=t2[0:1, :]), reads=['t2'], writes=['Y'])
        yhy = self.bv(O_X, 4096).rearrange("p (c t) -> p c t", t=T)
        for nb in range(4):
            banks = [self.bank() for _ in range(4)]
            for fg in range(8):
                mi, mik = self.load_bf(self.c_mi[nb, fg], (4, 512))
                for j in range(4):
                    fch = fg * 4 + j
                    for cc in range(4):
                        b, bk = banks[cc]
                        S.op('pe', lambda e: e.matmul(b[:, :], lhsT=Y[:, fch, cc * 128:(cc + 1) * 128],
                                                      rhs=mi[:, j, :], start=(fch == 0), stop=(fch == 31)),
                             reads=[mik, 'Y'], writes=[bk], inc=(fch == 31 or j == 3))
            for cc in range(4):
                b, bk = banks[cc]
                S.op('dve', lambda e: e.tensor_tensor(out=yhy[:, cc, nb * 512:(nb + 1) * 512], in0=b[:, :],
                                                      in1=x0[:, cc, nb * 512:(nb + 1) * 512], op=ALU.mult),
                     reads=[('x0', cc)], writes=[bk, 'yhy'] + [('xT', c_, t) for c_ in (0, 1) for t in range(4)])
        self.dump('yhy', self.fv(O_X, 4096), ['yhy'])
        self.prefetch_attention(l)
        S.barrier_all()

    def attention(self, l):
        S = self.S
        hT = self.hT()
        W = self.w["w_in"][l]
        gmix = self.fv(C_GMIX + 8 * l, 8)
        yatt = self.bv(O_U, 4096).rearrange("p (c t) -> p c t", t=T)
        o = O_U + 4096
        cs = self.fv(o, 2 * T); o += 2 * T
        cosT, sinT = cs[:, 0:T], cs[:, T:2 * T]
        accN = self.fv(o, T); o += T
        accD = self.fv(o, T); o += T
        qk = [[self.bv(o + 1024 * (2 * i + j), 1024) for j in range(2)] for i in range(2)]; o += 4096
        vch = [self.bv(o + 2112 * i, 2112).rearrange("p (n f) -> p n f", f=128) for i in range(2)]; o += 4224
        NPT = 6
        pT = [self.bv(o + 128 * i, 128) for i in range(NPT)]; o += 128 * NPT
        qb16 = [self.bv(o + 256 * i, 256) for i in range(2)]; o += 512
        t12 = [[self.fv(o + 512 * (2 * i + j), 512) for j in range(2)] for i in range(2)]; o += 2048
        etmp = [self.fv(o + 512 * i, 512) for i in range(2)]; o += 1024
        eti = [0]
        assert o <= AW, o
        self.bank_rng = (4, 8)
        S.dma('sp', cs, self.c_cs, writes=['cs'])
        it = 0
        pti = 0
        for hp in range(4):
            for g, (window, d) in enumerate(GROUPS):
                n = T // d
                par = it % 2
                it += 1
                qr, kr = qk[par]
                vc = vch[par]
                base = 1536 + g * 1536 + hp * 128
                qkw = [self.take(('att', l, hp, g, which), lambda: self.load_w(
                    W[:, base + 512 * which: base + 512 * which + 128], 8, 128)) for which in range(2)]
                qitems = [(which, tb) for which in range(2) for tb in range(4)]
                qst = {}

                def q_proj(k):
                    which, tb = qitems[k]
                    wt, wk = qkw[which]
                    tp = k % 2
                    j0 = tb * 512
                    bA, bkA = self.bank()
                    for kc in range(8):
                        S.op('pe', lambda e: e.matmul(bA[:, :], lhsT=wt[:, kc, :], rhs=hT[:, kc, j0:j0 + 512],
                                                      start=(kc == 0), stop=(kc == 7)),
                             reads=[wk, ('hT', tb)], writes=[bkA], inc=(kc == 7))
                    S.op('act', lambda e: e.activation(out=qb16[tp], in_=bA[:, :], func=AF.Copy),
                         writes=[bkA, ('qb16', tp)])
                    qst[k] = (bA, bkA)

                def q_rot(k):
                    which, tb = qitems[k]
                    dst = (qr, kr)[which]
                    tp = k % 2
                    j0 = tb * 512
                    bA, bkA = qst.pop(k)
                    bB, bkB = self.bank()
                    S.op('pe', lambda e: e.matmul(bB[:, :], lhsT=self.rot, rhs=qb16[tp], start=True, stop=True),
                         reads=[('qb16', tp), 'consts'], writes=[bkB])
                    t1, t2 = t12[tp]
                    S.op('dve', lambda e: e.tensor_tensor(out=t1, in0=bA[:, :], in1=cosT[:, j0:j0 + 512],
                                                          op=ALU.mult), reads=['cs'], writes=[bkA, ('t1', tp)])
                    S.op('dve', lambda e: e.tensor_tensor(out=t2, in0=bB[:, :], in1=sinT[:, j0:j0 + 512],
                                                          op=ALU.mult), reads=['cs'], writes=[bkB, ('t2', tp)])
                    if d == 1:
                        dv, a1, a2 = dst[:, j0:j0 + 512], t1, t2
                    else:
                        m0, ml = j0 // d, 512 // d
                        dv = dst.rearrange("p (r m) -> p r m", r=d)[:, :, m0:m0 + ml]
                        a1 = t1.rearrange("p (m r) -> p r m", r=d)
                        a2 = t2.rearrange("p (m r) -> p r m", r=d)
                    S.op('pool', lambda e: e.tensor_tensor(out=dv, in0=a1, in1=a2, op=ALU.add),
                         reads=[('t1', tp), ('t2', tp)], writes=[('qk', par, which)])

                q_proj(0)
                for k in range(len(qitems)):
                    if k + 1 < len(qitems):
                        q_proj(k + 1)
                    q_rot(k)
                wt, wk = self.take(('att', l, hp, g, 2), lambda: self.load_w(W[:, base + 1024: base + 1024 + 128], 8, 128))
                nj = n // 128 + 1
                chunks = []
                for r in range(d):
                    if n == 128:
                        chunks.append((r, -1, 0, 128, 0, len(chunks)))
                        continue
                    for j in range(nj):
                        k0 = max(0, 128 * j - 64)
                        k1 = min(n, 128 * j + 64)
                        pb = 64 if j == 0 else 0
                        chunks.append((r, j, k0, k1 - k0, pb, len(chunks)))
                def v_proj(lo, hi, rng):
                    self.bank_rng = rng
                    for c0 in range(lo, hi, 4):
                        b, bk = self.bank()
                        grp = chunks[c0:min(c0 + 4, hi)]
                        for gi, (r, j, k0, nk, pb, idx) in enumerate(grp):
                            for kc in range(8):
                                lhsT = perm_view(hT[:, kc, :], d, r * n + k0, nk)
                                S.op('pe', lambda e: e.matmul(b[pb:pb + nk, gi * 128:(gi + 1) * 128], lhsT=lhsT,
                                                              rhs=wt[:, kc, :], start=(kc == 0), stop=(kc == 7)),
                                     reads=[wk] + [('hT', t) for t in range(4)], writes=[bk],
                                     inc=(kc == 7 and gi == len(grp) - 1))
                        for gi, (r, j, k0, nk, pb, idx) in enumerate(grp):
                            S.op('act', lambda e: e.activation(out=vc[pb:pb + nk, idx, :],
                                                               in_=b[pb:pb + nk, gi * 128:(gi + 1) * 128],
                                                               func=AF.Copy), writes=[bk, ('vc', par)])
                    self.bank_rng = (4, 8)

                nqb = n // 128
                segbanks = {}
                stA = {}

                def stage_a(ci):
                    nonlocal pti
                    (r, j, k0, nk, pb, idx) = chunks[ci]
                    if j == -1:
                        qbs, mcol0 = [0], 256
                    else:
                        qbs = [qb for qb in (j - 1, j) if 0 <= qb < nqb]
                        mcol0 = 0 if qbs[0] == j - 1 else 128
                    q0 = r * n + 128 * qbs[0]
                    nq = 128 * len(qbs)
                    res = []
                    for h in range(2):
                        hs = slice(64 * h, 64 * h + 64)
                        bS, bkS = self.bank()
                        p_ = pT[pti % NPT]
                        pk = ('pT', pti % NPT)
                        pti += 1
                        S.op('pe', lambda e: e.matmul(bS[pb:pb + nk, 0:nq], lhsT=kr[hs, r * n + k0: r * n + k0 + nk],
                                                      rhs=qr[hs, q0:q0 + nq], start=True, stop=True),
                             reads=[('qk', par, 0), ('qk', par, 1)], writes=[bkS])
                        res.append((bS, bkS, p_, pk))
                    for h in range(2):
                        bS, bkS, p_, pk = res[h]
                        S.op('act', lambda e: e.activation(out=p_[pb:pb + nk, 0:nq], in_=bS[pb:pb + nk, 0:nq],
                                                           func=AF.Exp, scale=0.125), writes=[bkS, pk])
                        S.op('dve' if h == 0 else 'pool', lambda e: e.tensor_tensor(out=p_[pb:pb + nk, 0:nq], in0=p_[pb:pb + nk, 0:nq],
                                                              in1=self.mask[pb:pb + nk, mcol0:mcol0 + nq],
                                                              op=ALU.mult), reads=['consts'], writes=[pk])
                    stA[ci] = (qbs, res)

                def stage_b(ci):
                    (r, j, k0, nk, pb, idx) = chunks[ci]
                    qbs, res = stA.pop(ci)
                    for qi, qb in enumerate(qbs):
                        gq = (r * n) // 128 + qb
                        seg = gq // 4
                        if seg not in segbanks:
                            sb = 2 * (seg % 2)
                            segbanks[seg] = ((self.ps[sb], ('ps', sb)), (self.ps[sb + 1], ('ps', sb + 1)))
                        (bN, bkN), (bD, bkD) = segbanks[seg]
                        col = (gq % 4) * 128
                        first = (qb == j) or j == -1
                        last = (qb != j) or j == -1
                        for h in range(2):
                            hs = slice(64 * h, 64 * h + 64)
                            bS, bkS, p_, pk = res[h]
                            S.op('pe', lambda e: e.matmul(bN[hs, col:col + 128], lhsT=vc[pb:pb + nk, idx, hs],
                                                          rhs=p_[pb:pb + nk, qi * 128:(qi + 1) * 128],
                                                          start=first, stop=last),
                                 reads=[pk, ('vc', par)], writes=[bkN])
                        for h in range(2):
                            hs = slice(64 * h, 64 * h + 64)
                            bS, bkS, p_, pk = res[h]
                            S.op('pe', lambda e: e.matmul(bD[hs, col:col + 128], lhsT=self.onb[pb:pb + nk, :],
                                                          rhs=p_[pb:pb + nk, qi * 128:(qi + 1) * 128],
                                                          start=first, stop=last),
                                 reads=[pk, 'consts'], writes=[bkD])
                    if j >= 1 or j == -1:
                        gq = (r * n) // 128 + (j - 1 if j >= 1 else 0)
                        if gq % 4 == 3:
                            seg = gq // 4
                            (bN, bkN), (bD, bkD) = segbanks.pop(seg)
                            for bsrc, bks, acc, ak in ((bN, bkN, accN, 'accN'), (bD, bkD, accD, 'accD')):
                                av = perm_view(acc, d, 512 * seg, 512)
                                src = like(bsrc[:, :], av)
                                if g == 0 and ak == 'accN':
                                    S.op('act', lambda e: e.activation(out=av, in_=src, func=AF.Copy),
                                         writes=[bks, ak])
                                elif g == 0:
                                    S.op('dve', lambda e: e.tensor_copy(out=av, in_=src), writes=[bks, ak])
                                else:
                                    S.op('dve', lambda e: e.tensor_tensor(out=av, in0=src, in1=av, op=ALU.add),
                                         writes=[bks, ak])

                nxt = [(hp_, g_) for hp_ in range(4) for g_ in range(3)]
                ni = nxt.index((hp, g)) + 1
                pend = {}

                def prefetch_step(ci):
                    if ni >= len(nxt):
                        return
                    nhp, ng = nxt[ni]
                    nbase = 1536 + ng * 1536 + nhp * 128

                    def src(which):
                        return W[:, nbase + 512 * which: nbase + 512 * which + 128]
                    if ci == 1:
                        pend[0] = self.load_w_dma(src(0), 8, 128)
                        pend[1] = self.load_w_dma(src(1), 8, 128)
                    elif ci == 6:
                        for which in range(2):
                            self.pre[('att', l, nhp, ng, which)] = self.load_w_cast(pend.pop(which), 'act')
                        pend[2] = self.load_w_dma(src(2), 8, 128)
                    elif ci == 11:
                        self.pre[('att', l, nhp, ng, 2)] = self.load_w_cast(pend.pop(2), 'act')

                vsplit = 8 if len(chunks) > 16 else 4
                v_proj(0, vsplit, (0, 8))
                stage_a(0)
                stage_a(1)
                v_proj(vsplit, len(chunks), (0, 4))
                for ci in range(len(chunks)):
                    if ci + 2 < len(chunks):
                        stage_a(ci + 2)
                    stage_b(ci)
                    prefetch_step(ci)
                assert not segbanks
            S.op('act', lambda e: e.activation(out=accD, in_=accD, func=AF.Ln), writes=['accD'])
            S.op('act', lambda e: e.activation(out=accD, in_=accD, func=AF.Exp, scale=-1.0), writes=['accD'])
            S.op('dve', lambda e: e.tensor_tensor(out=yatt[:, hp, :], in0=accN, in1=accD, op=ALU.mult),
                 reads=['accN', 'accD'], writes=['yatt'])
        self.dump('yatt', self.fv(O_U, 4096), ['yatt'])
        self.bank_rng = (0, 8)
        self.prefetch_merge(l)
        S.barrier_all()

    def merge(self, l):
        S = self.S
        hT = self.hT()
        W = self.w["w_in"][l]
        gmix = self.fv(C_GMIX + 8 * l, 8)
        yhy = self.bv(O_X, 4096).rearrange("p (c t) -> p c t", t=T)
        yatt = self.bv(O_U, 4096).rearrange("p (c t) -> p c t", t=T)
        merged = self.bv(O_U + 17408, 8192).rearrange("p (c t) -> p c t", t=T)
        tmp = [[self.fv(O_U + 13312 + 512 * (4 * i + j), 512) for j in range(4)] for i in range(2)]
        it = 0
        for fc in range(8):
            wg1, k1 = self.take(('mg', l, fc, 0), lambda: self.load_w(W[:, 6144 + fc * 128: 6144 + fc * 128 + 128], 8, 128))
            wg2, k2 = self.take(('mg', l, fc, 1), lambda: self.load_w(W[:, 7168 + fc * 128: 7168 + fc * 128 + 128], 8, 128))
            wp1, k3 = self.take(('mg', l, fc, 2), lambda: self.load_w(self.w["p_hy"][l][:, fc * 128:(fc + 1) * 128], 4, 128))
            wp2, k4 = self.take(('mg', l, fc, 3), lambda: self.load_w(self.w["p_att"][l][:, fc * 128:(fc + 1) * 128], 4, 128))
            for tb in range(4):
                sl = slice(tb * 512, (tb + 1) * 512)
                s1, s2, m1, m2 = tmp[it % 2]
                tk = it % 2
                it += 1
                b1, bk1 = self.bank()
                b2, bk2 = self.bank()
                b3, bk3 = self.bank()
                b4, bk4 = self.bank()
                for kc in range(8):
                    S.op('pe', lambda e: e.matmul(b1[:, :], lhsT=wg1[:, kc, :], rhs=hT[:, kc, sl], start=(kc == 0),
                                                  stop=(kc == 7)), reads=[k1, ('hT', tb)], writes=[bk1], inc=(kc == 7))
                for kc in range(8):
                    S.op('pe', lambda e: e.matmul(b2[:, :], lhsT=wg2[:, kc, :], rhs=hT[:, kc, sl], start=(kc == 0),
                                                  stop=(kc == 7)), reads=[k2, ('hT', tb)], writes=[bk2], inc=(kc == 7))
                for kc in range(4):
                    S.op('pe', lambda e: e.matmul(b3[:, :], lhsT=wp1[:, kc, :], rhs=yhy[:, kc, sl], start=(kc == 0),
                                                  stop=(kc == 3)), reads=[k3, 'yhy'], writes=[bk3], inc=(kc == 3))
                for kc in range(4):
                    S.op('pe', lambda e: e.matmul(b4[:, :], lhsT=wp2[:, kc, :], rhs=yatt[:, kc, sl], start=(kc == 0),
                                                  stop=(kc == 3)), reads=[k4, 'yatt'], writes=[bk4], inc=(kc == 3))
                S.op('act', lambda e: e.activation(out=s1, in_=b1[:, :], func=AF.Sigmoid), writes=[bk1, ('s1', tk)])
                S.op('act', lambda e: e.activation(out=s2, in_=b2[:, :], func=AF.Sigmoid), writes=[bk2, ('s2', tk)])
                S.op('dve', lambda e: e.tensor_tensor(out=m1, in0=b3[:, :], in1=s1, op=ALU.mult),
                     reads=[('s1', tk)], writes=[bk3, ('m1', tk)])
                S.op('dve', lambda e: e.tensor_tensor(out=m2, in0=b4[:, :], in1=s2, op=ALU.mult),
                     reads=[('s2', tk)], writes=[bk4, ('m2', tk)])
                S.op('dve', lambda e: e.tensor_tensor(out=merged[:, fc, sl], in0=m1, in1=m2, op=ALU.add),
                     reads=[('m1', tk), ('m2', tk)], writes=[('merged', tb)])
        self.prefetch_wo(l)
        S.barrier_all()
        xT = self.xT()
        for c in range(8):
            S.dma('sp', self.fv(O_X + c * T, T), self.xs[:, c * T:(c + 1) * T], reads=[('xs', c)],
                  writes=[('xT', c, tb) for tb in range(4)])
        wos = [self.take(('wo', l, j), lambda: self.load_w(self.w["w_o"][l][:, j * 256:(j + 1) * 256], 8, 256))
               for j in range(4)]
        gffn = self.fv(C_GFFN + 8 * l, 8)
        hT = self.hT()
        sq = [self.fv(O_TMP + 512 * i, 512) for i in range(2)]
        rs_ = [self.fv(O_TMP + 1024 + 512 * i, 512) for i in range(4)]
        def stats(tb):
            sl = slice(tb * 512, (tb + 1) * 512)
            sb, sbk = self.ps[4 + tb], ('ps', 4 + tb)
            for c in range(8):
                i = c % 2
                S.op('act', lambda e: e.activation(out=sq[i], in_=xT[:, c, sl], func=AF.Square),
                     reads=[('xT', c, tb)], writes=[('sq', i)])
                S.op('pe', lambda e: e.matmul(sb[:, :], lhsT=self.onf, rhs=sq[i], start=(c == 0), stop=(c == 7)),
                     reads=[('sq', i), 'consts'], writes=[sbk])

        self.bank_rng = (0, 4)
        for tb in range(4):
            sl = slice(tb * 512, (tb + 1) * 512)
            for oc in range(8):
                wo, ko = wos[oc // 2]
                b, bk = self.bank()
                for kc in range(8):
                    S.op('pe', lambda e: e.matmul(b[:, :], lhsT=wo[:, kc, (oc % 2) * 128:(oc % 2) * 128 + 128],
                                                  rhs=merged[:, kc, sl], start=(kc == 0), stop=(kc == 7)),
                         reads=[ko, ('merged', tb)], writes=[bk], inc=(kc == 7))
                S.op('dve', lambda e: e.tensor_tensor(out=xT[:, oc, sl], in0=b[:, :], in1=xT[:, oc, sl], op=ALU.add),
                     writes=[bk, ('xT', oc, tb)])
            if tb >= 1:
                stats(tb - 1)
        stats(3)
        self.bank_rng = (0, 8)
        for tb in range(4):
            sb, sbk = self.ps[4 + tb], ('ps', 4 + tb)
            S.op('act', lambda e: e.activation(out=rs_[tb], in_=sb[:, :], func=AF.Ln, bias=self.eps, scale=1.0 / D),
                 reads=['consts'], writes=[sbk, ('rs', tb)])
        for tb in range(4):
            S.op('act', lambda e: e.activation(out=rs_[tb], in_=rs_[tb], func=AF.Exp, scale=-0.5),
                 writes=[('rs', tb)])
        for tb in range(4):
            sl = slice(tb * 512, (tb + 1) * 512)
            for c in range(8):
                S.op('dve', lambda e: e.scalar_tensor_tensor(out=hT[:, c, sl], in0=xT[:, c, sl], scalar=gffn[:, c:c + 1],
                                                             in1=rs_[tb], op0=ALU.mult, op1=ALU.mult),
                     reads=[('xT', c, tb), ('rs', tb), 'consts'], writes=[('hT', tb)])
        self.prefetch_ffn(l)

    def ffn(self, l):
        S = self.S
        xT, hT = self.xT(), self.hT()
        gffn = self.fv(C_GFFN + 8 * l, 8)
        a_ = [self.bv(O_TMP + 3072 + 1024 * i, 1024).rearrange("p (f t) -> p f t", t=512) for i in range(2)]
        r_ = [self.fv(O_TMP + 5120 + 512 * i, 512) for i in range(2)]
        W1 = self.w["w_ff1"][l]
        W2 = self.w["w_ff2"][l]
        S.op('dve', lambda e: e.memset(r_[0][0:1, 0:2], 0.0),
             writes=[('r', 0), ('r', 1)] + [('merged', t) for t in range(4)])
        ri = [0]
        wts = {}

        def get_w1(fg):
            if ('w1', fg) not in wts:
                wts[('w1', fg)] = self.take(('w1', l, fg), lambda: [
                    self.load_w(W1[:, fg * 512 + 256 * i: fg * 512 + 256 * i + 256], 8, 256) for i in range(2)])
            return wts[('w1', fg)]

        def get_w2(fg):
            if ('w2', fg) not in wts:
                wts[('w2', fg)] = [self.load_w(W2[fg * 512:(fg + 1) * 512, 512 * i: 512 * i + 512], 4, 512)
                                   for i in range(2)]
            return wts[('w2', fg)]

        items = [(fg, tb) for fg in range(8) for tb in range(4)]

        def stage1(k):
            fg, tb = items[k]
            sl = slice(tb * 512, (tb + 1) * 512)
            w1t = get_w1(fg)
            if tb == 0:
                get_w2(fg)
            if tb == 2 and fg + 1 < 8:
                get_w1(fg + 1)
            a = a_[k % 2]
            for f in range(4):
                wt, wk = w1t[f // 2]
                b, bk = self.bank()
                rr = r_[ri[0] % 2]
                rk = ('r', ri[0] % 2)
                ri[0] += 1
                for kc in range(8):
                    S.op('pe', lambda e: e.matmul(b[:, :], lhsT=wt[:, kc, (f % 2) * 128:(f % 2) * 128 + 128],
                                                  rhs=hT[:, kc, sl], start=(kc == 0), stop=(kc == 7)),
                         reads=[wk, ('hT', tb)], writes=[bk], inc=(kc == 7))
                S.op('act', lambda e: e.activation(out=rr, in_=b[:, :], func=AF.Relu), writes=[bk, rk])
                S.op('dve', lambda e: e.tensor_tensor(out=a[:, f, :], in0=rr, in1=rr, op=ALU.mult),
                     reads=[rk], writes=[('a', k % 2)])

        def stage2(k):
            fg, tb = items[k]
            sl = slice(tb * 512, (tb + 1) * 512)
            w2t = get_w2(fg)
            a = a_[k % 2]
            for dc in range(8):
                wt, wk = w2t[dc // 4]
                b, bk = self.bank()
                for f in range(4):
                    S.op('pe', lambda e: e.matmul(b[:, :], lhsT=wt[:, f, (dc % 4) * 128:(dc % 4) * 128 + 128],
                                                  rhs=a[:, f, :], start=(f == 0), stop=(f == 3)),
                         reads=[wk, ('a', k % 2)], writes=[bk], inc=(f == 3))
                S.op('dve', lambda e: e.tensor_tensor(out=xT[:, dc, sl], in0=b[:, :], in1=xT[:, dc, sl],
                                                      op=ALU.add), writes=[bk, ('xT', dc, tb)])

        stage1(0)
        for k in range(len(items)):
            if k + 1 < len(items):
                stage1(k + 1)
            stage2(k)
        S.barrier_all()

    def final(self, s):
        S = self.S
        xT = self.xT()
        gfin = self.fv(C_GFIN, 8)
        tmp = O_TMP
        sq = [self.fv(tmp + 512 * i, 512) for i in range(2)]
        rst = [self.fv(tmp + 1024 + 512 * i, 512) for i in range(4)]
        yn_ = [self.fv(tmp + 3072 + 4096 * i, 4096).rearrange("p (c t) -> p c t", t=512) for i in range(2)]
        yo = [self.fv(tmp + 11264 + 1024 * i, 1024) for i in range(2)]
        oi = [0]

        def stats_all():
            banks = []
            for tb in range(4):
                sl = slice(tb * 512, (tb + 1) * 512)
                b, bk = self.bank(0, 4)
                banks.append((b, bk))
                for c in range(8):
                    i = c % 2
                    S.op('act', lambda e: e.activation(out=sq[i], in_=xT[:, c, sl], func=AF.Square),
                         reads=[('xT', c, tb)], writes=[('sq', i)])
                    S.op('pe', lambda e: e.matmul(b[:, :], lhsT=self.onf, rhs=sq[i], start=(c == 0), stop=(c == 7)),
                         reads=[('sq', i), 'consts'], writes=[bk])
            for tb in range(4):
                b, bk = banks[tb]
                S.op('act', lambda e: e.activation(out=rst[tb], in_=b[:, :], func=AF.Ln, bias=self.eps,
                                                   scale=1.0 / D), reads=['consts'], writes=[bk, ('rs', tb)])
            for tb in range(4):
                S.op('act', lambda e: e.activation(out=rst[tb], in_=rst[tb], func=AF.Exp, scale=-0.5),
                     writes=[('rs', tb)])

        def scale(tb):
            sl = slice(tb * 512, (tb + 1) * 512)
            yn = yn_[tb % 2]
            ynk = ('yn', tb % 2)
            for c in range(8):
                S.op('dve', lambda e: e.scalar_tensor_tensor(out=yn[:, c, :], in0=xT[:, c, sl], scalar=gfin[:, c:c + 1],
                                                             in1=rst[tb], op0=ALU.mult, op1=ALU.mult),
                     reads=[('xT', c, tb), ('rs', tb), 'consts'], writes=[ynk])

        def store(tb):
            yn = yn_[tb % 2]
            ynk = ('yn', tb % 2)
            for a in range(4):
                y = yo[oi[0] % 2]
                yk = ('yo', oi[0] % 2)
                oi[0] += 1
                for half in range(2):
                    b2, bk2 = self.bank()
                    for cc in range(4):
                        c = half * 4 + cc
                        S.op('pe', lambda e: e.transpose(out=b2[:, cc * 128:(cc + 1) * 128],
                                                         in_=yn[:, c, a * 128:(a + 1) * 128], identity=self.idf),
                             reads=[ynk, 'consts'], writes=[bk2], inc=(cc == 3))
                    S.op('act', lambda e: e.activation(out=y[:, half * 512:(half + 1) * 512], in_=b2[:, :],
                                                       func=AF.Copy), writes=[bk2, yk])
                t0 = tb * 512 + a * 128
                S.dma('act', self.out[s, t0:t0 + 128, :], y, reads=[yk], writes=['out'])

        stats_all()
        scale(0)
        self.bank_rng = (4, 8)
        for tb in range(4):
            if tb + 1 < 4:
                scale(tb + 1)
            store(tb)
        self.bank_rng = (0, 8)
        S.barrier_all()

    def build(self):
        S = self.S
        self.load_consts()
        for l in range(self.depth):
            self.filter_prologue(l)
        for s in range(self.nseq):
            self.load_x(s)
            for l in range(self.depth):
                self.norm(AW - 3072, self.fv(C_GMIX + 8 * l, 8))
                self.prefetch_hyena(l)
                self.spill_x()
                self.hyena(l)
                self.attention(l)
                self.merge(l)
                self.ffn(l)
            self.final(s)
        S.barrier_all()


_CONSTS = None


def make_consts():
    global _CONSTS
    if _CONSTS is not None:
        return _CONSTS
    bf = ml_dtypes.bfloat16
    c = {}
    ident = np.eye(128, dtype=np.float32)
    c["c_f32"] = np.concatenate([ident, np.ones((128, 128), np.float32)], axis=1)
    rot = np.zeros((128, 128), np.float32)
    for hb in (0, 64):
        for e in range(32):
            rot[hb + e + 32, hb + e] = -1.0
            rot[hb + e, hb + e + 32] = 1.0
    k = np.arange(128)[:, None]
    q = np.arange(128)[None, :]
    mask = np.concatenate([(k <= q), (k >= q), (np.abs(k - q) <= 64)], axis=1).astype(np.float32)
    c["c_bf"] = np.concatenate([ident, np.ones((128, 64), np.float32), rot, mask], axis=1).astype(bf)
    half = 32
    inv_freq = (10000.0 ** (-np.arange(half, dtype=np.float32) / half)).astype(np.float32)
    pos = np.arange(T, dtype=np.float32)
    ang = (pos[None, :] * inv_freq[:, None]).astype(np.float32)
    ang128 = np.tile(ang, (4, 1))
    c["c_cs"] = np.concatenate([np.cos(ang128), np.sin(ang128)], axis=1).astype(np.float32)
    a = np.arange(T, dtype=np.int64)[:, None]
    f = np.arange(T, dtype=np.int64)[None, :]
    ph = ((a * f) % NFFT).astype(np.float64) * (2 * np.pi / NFFT)
    M = np.concatenate([np.cos(ph), np.sin(ph)], axis=1)
    M[:, 2048] = np.where(np.arange(T) % 2 == 0, 1.0, -1.0)
    M = M.astype(np.float32)
    mf = M.reshape(16, 128, 32, 128).transpose(2, 1, 0, 3).reshape(32, 128, 2048)
    c["c_mf"] = np.ascontiguousarray(mf).astype(bf)
    MT = M.T.reshape(8, 4, 128, 4, 512).transpose(3, 0, 2, 1, 4).reshape(4, 8, 128, 2048)
    c["c_mi"] = np.ascontiguousarray(MT).astype(bf)
    L = T
    nn = np.arange(L, dtype=np.float32)
    t = (nn / np.float32(L - 1)).astype(np.float32)
    bands = np.linspace(1e-4, 15, 16, dtype=np.float32)
    angf = (np.float32(2.0 * math.pi / L) * nn[:, None] * bands[None, :]).astype(np.float32)
    z = np.concatenate([t[:, None], np.cos(angf), -np.sin(angf)], axis=-1).astype(np.float32)
    c["c_zt"] = np.ascontiguousarray(z.T)
    max_decay = math.log(1e-2) / 0.3
    min_decay = math.log(1e-2) / 1.5
    deltas = np.abs(np.linspace(min_decay, max_decay, HW, dtype=np.float32))
    window = np.exp(-t[:, None] * deltas[None, :]).astype(np.float32)
    c["c_win"] = np.ascontiguousarray(window.reshape(16, 128, 512))
    sf = np.full((128, 16), 2.0 / NFFT, np.float32)
    sf[0, 0] = 1.0 / NFFT
    c["c_sf"] = sf
    _CONSTS = c
    return c


_NC_CACHE = {}


def get_nc(nseq, depth, dbg=None):
    key = (nseq, depth, tuple(sorted((dbg or {}).items())))
    if key not in _NC_CACHE:
        nc = bass.Bass("TRN2", target_bir_lowering=False)
        with ExitStack() as es:
            p = Prog(nc, es, nseq, depth, dbg)
            p.build()
        _NC_CACHE[key] = nc
    return _NC_CACHE[key]


WNAMES = ["norm_mix", "w_in", "conv_w", "conv_b", "filt_w1", "filt_b1", "filt_w_inner", "filt_b_inner",
          "filt_w_out", "filt_freq", "hy_skip", "p_hy", "p_att", "w_o", "norm_ffn", "w_ff1", "w_ff2", "norm_final"]


def kernel(**inputs):
    x = np.ascontiguousarray(np.asarray(inputs["x"], dtype=np.float32))
    consts = make_consts()
    base = {k: np.ascontiguousarray(np.asarray(inputs[k], dtype=np.float32)) for k in WNAMES}
    base.update(consts)
    nc = get_nc(NSEQ, DEPTH)
    in_maps = []
    for c in range(NCORE):
        m = dict(base)
        m["x"] = x[c * NSEQ:(c + 1) * NSEQ]
        in_maps.append(m)
    res = run_bass_kernel_spmd(nc, in_maps, core_ids=list(range(NCORE)))
    out = np.concatenate([res.results[c]["out"] for c in range(NCORE)], axis=0)
    return out.astype(np.float32)
```

```python
import math
from contextlib import ExitStack

import numpy as np
import ml_dtypes
import concourse.bass as bass
import concourse.mybir as mybir
from concourse.bass_utils import run_bass_kernel_spmd

F32 = mybir.dt.float32
BF16 = mybir.dt.bfloat16
AF = mybir.ActivationFunctionType
ALU = mybir.AluOpType

D = 1024
T = 2048
DEPTH = 2
NCORE = 8
NSEQ = 2
INW = 8192
HW = 512
DFF = 4096
NFFT = 4096
GROUPS = ((128, 1), (512, 4), (2048, 16))
EPS = 1e-6

AW = 52224
C_IDF, C_ONF, C_IDB, C_EPS = 0, 128, 256, 608
C_GMIX, C_GFFN, C_GFIN, C_CONV, C_SF, C_SKIP = 616, 632, 648, 656, 752, 768
O_STG = 2048
O_WB = 6144
NWB = 8
O_HT = 14336
O_X = 22528
O_U = 26624
O_TMP = 38912


class Sched:
    NDS = 24

    def __init__(self, nc, es):
        self.nc = nc
        self.eng = {'pe': nc.tensor, 'dve': nc.vector, 'act': nc.scalar, 'pool': nc.gpsimd, 'sp': nc.sync}
        self.sem = {e: es.enter_context(nc.semaphore('s_' + e)) for e in ('pe', 'dve', 'act', 'pool')}
        self.cnt = {e: 0 for e in self.sem}
        self.dsem = {q: [es.enter_context(nc.semaphore('d_%s_%d' % (q, i))) for i in range(self.NDS)]
                     for q in ('sp', 'act')}
        self.dcnt = {q: 0 for q in self.dsem}
        self.waited = {e: {} for e in self.eng}
        self.lastw = {}
        self.readers = {}

    def _wait(self, stream, tok):
        ename, sem, val = tok
        if ename == 'pe' and stream == 'pe':
            return
        w = self.waited[stream]
        k = id(sem)
        if w.get(k, 0) >= val:
            return
        w[k] = val
        self.eng[stream].wait_ge(sem, val)

    def _deps(self, stream, reads, writes):
        for k in reads:
            t = self.lastw.get(k)
            if t is not None:
                self._wait(stream, t)
        for k in writes:
            t = self.lastw.get(k)
            if t is not None:
                self._wait(stream, t)
            for t in self.readers.get(k, {}).values():
                self._wait(stream, t)

    def _commit(self, tok, reads, writes):
        for k in reads:
            if k not in writes:
                r = self.readers.setdefault(k, {})
                o = r.get(id(tok[1]))
                if o is None or o[2] < tok[2]:
                    r[id(tok[1])] = tok
        for k in writes:
            self.lastw[k] = tok
            self.readers[k] = {}

    def op(self, e, fn, reads=(), writes=(), inc=True):
        self._deps(e, reads, writes)
        ins = fn(self.eng[e])
        if inc:
            ins.then_inc(self.sem[e], 1)
            self.cnt[e] += 1
            tok = (e, self.sem[e], self.cnt[e])
        else:
            tok = (e, self.sem[e], self.cnt[e] + 1)
        self._commit(tok, reads, writes)
        return tok

    def dma(self, q, out, in_, reads=(), writes=(), **kw):
        j = self.dcnt[q]
        self.dcnt[q] += 1
        sem = self.dsem[q][j % self.NDS]
        rnd = j // self.NDS
        if rnd > 0:
            self._wait(q, ('dma', sem, 16 * rnd))
        self._deps(q, reads, writes)
        self.eng[q].dma_start(out=out, in_=in_, **kw).then_inc(sem, 16)
        tok = ('dma', sem, 16 * (rnd + 1))
        self._commit(tok, reads, writes)
        return tok

    def barrier_all(self):
        toks = []
        for e in self.sem:
            if self.cnt[e]:
                toks.append((e, self.sem[e], self.cnt[e]))
        for q in self.dsem:
            n = self.dcnt[q]
            for i in range(min(n, self.NDS)):
                last_j = ((n - 1 - i) // self.NDS) * self.NDS + i
                toks.append(('dma', self.dsem[q][i], 16 * (last_j // self.NDS + 1)))
        for s in self.eng:
            for t in toks:
                if t[0] == s and s != 'pe':
                    pass
                self._wait(s, t)


def perm_view(ap2d, d, j0, ln):
    if d == 1:
        return ap2d[:, j0:j0 + ln]
    n = T // d
    v = ap2d.rearrange("p (m r) -> p r m", r=d)
    if ln <= n:
        r = j0 // n
        m0 = j0 % n
        assert m0 + ln <= n
        return v[:, r, m0:m0 + ln]
    assert j0 % n == 0 and ln % n == 0
    return v[:, j0 // n:j0 // n + ln // n, :]


def like(ap_contig, ref):
    if len(ref.shape) == 3:
        return ap_contig.rearrange("p (a b) -> p a b", b=ref.shape[2])
    return ap_contig


class Prog:
    def __init__(self, nc, es, nseq, depth, dbg=None):
        self.nc = nc
        self.nseq = nseq
        self.depth = depth
        self.dbg = dbg or {}
        S = self.S = Sched(nc, es)
        dt = nc.dram_tensor
        self.x = dt("x", [nseq, T, D], F32, kind="ExternalInput").ap()
        self.w = {}
        for name, shp in [("norm_mix", [DEPTH, D]), ("w_in", [DEPTH, D, INW]), ("conv_w", [DEPTH, 3, 1536]),
                          ("conv_b", [DEPTH, 1536]), ("filt_w1", [DEPTH, 33, 64]), ("filt_b1", [DEPTH, 64]),
                          ("filt_w_inner", [DEPTH, 2, 64, 64]), ("filt_b_inner", [DEPTH, 2, 64]),
                          ("filt_w_out", [DEPTH, 64, 1024]), ("filt_freq", [DEPTH, 64]), ("hy_skip", [DEPTH, 512]),
                          ("p_hy", [DEPTH, 512, D]), ("p_att", [DEPTH, 512, D]), ("w_o", [DEPTH, D, D]),
                          ("norm_ffn", [DEPTH, D]), ("w_ff1", [DEPTH, D, DFF]), ("w_ff2", [DEPTH, DFF, D]),
                          ("norm_final", [D])]:
            self.w[name] = dt(name, shp, F32, kind="ExternalInput").ap()
        self.c_f32 = dt("c_f32", [128, 256], F32, kind="ExternalInput").ap()
        self.c_bf = dt("c_bf", [128, 128 + 64 + 128 + 384], BF16, kind="ExternalInput").ap()
        self.c_cs = dt("c_cs", [128, 2 * T], F32, kind="ExternalInput").ap()
        self.c_mf = dt("c_mf", [32, 128, 2048], BF16, kind="ExternalInput").ap()
        self.c_mi = dt("c_mi", [4, 8, 128, 2048], BF16, kind="ExternalInput").ap()
        self.c_zt = dt("c_zt", [33, T], F32, kind="ExternalInput").ap()
        self.c_win = dt("c_win", [16, 128, 512], F32, kind="ExternalInput").ap()
        self.c_sf = dt("c_sf", [128, 16], F32, kind="ExternalInput").ap()
        self.out = dt("out", [nseq, T, D], F32, kind="ExternalOutput").ap()
        self.kspec = dt("kspec", [DEPTH, 16, 128, 1024], F32, kind="Internal").ap()
        self.xs = dt("xs", [128, 8 * T], F32, kind="Internal").ap()
        self.dbg_t = {}
        for name, shp in self.dbg.items():
            self.dbg_t[name] = dt("dbg_" + name, list(shp), F32, kind="ExternalOutput").ap()
        self.A = es.enter_context(nc.sbuf_tensor("arena", [128, AW], F32))
        self.ps = [es.enter_context(nc.psum_tensor("ps%d" % i, [128, 512], F32)) for i in range(8)]
        self.psr = {}
        self.pre = {}
        self.bank_rng = (0, 8)
        self.wbi = 0
        self.stgi = 0
        self.uid = 0

    def fv(self, off, n, parts=128):
        return self.A[0:parts, off:off + n]

    def bv(self, off, nwords, parts=128):
        return self.A[0:parts, off:off + nwords].bitcast(BF16)

    def bank(self, lo=None, hi=None):
        if lo is None:
            lo, hi = self.bank_rng
        k = (lo, hi)
        i = self.psr.get(k, lo)
        self.psr[k] = lo + (i + 1 - lo) % (hi - lo)
        return self.ps[i], ('ps', i)

    def key(self, name):
        self.uid += 1
        return (name, self.uid)

    def load_w(self, src, kcs, ncols, scale=None, ceng='pool'):
        S = self.S
        si = self.stgi
        self.stgi = (self.stgi + 1) % 2
        wi = self.wbi
        self.wbi = (self.wbi + 1) % NWB
        n = kcs * ncols
        assert n <= 2048
        stg = self.fv(O_STG + si * 2048, n).rearrange("p (k n) -> p k n", n=ncols)
        wb = self.bv(O_WB + wi * 1024, n // 2).rearrange("p (k n) -> p k n", n=ncols)
        S.dma('sp', stg, src.rearrange("(k p) n -> p k n", p=128), writes=[('stg', si)])
        if scale is None:
            if ceng == 'act':
                S.op('act', lambda e: e.activation(out=wb, in_=stg, func=AF.Copy), reads=[('stg', si)],
                     writes=[('wb', wi)])
            else:
                S.op(ceng, lambda e: e.tensor_copy(out=wb, in_=stg), reads=[('stg', si)], writes=[('wb', wi)])
        else:
            for k in range(kcs):
                S.op('pool', lambda e, k=k: e.tensor_scalar(out=wb[:, k, :], in0=stg[:, k, :],
                                                             scalar1=scale[:, k:k + 1], scalar2=None, op0=ALU.mult),
                     reads=[('stg', si)], writes=[('wb', wi)])
        return wb, ('wb', wi)

    def load_w_dma(self, src, kcs, ncols):
        si = self.stgi
        self.stgi = (self.stgi + 1) % 2
        n = kcs * ncols
        stg = self.fv(O_STG + si * 2048, n).rearrange("p (k n) -> p k n", n=ncols)
        self.S.dma('sp', stg, src.rearrange("(k p) n -> p k n", p=128), writes=[('stg', si)])
        return (stg, si, n, ncols)

    def load_w_cast(self, h, ceng='act'):
        stg, si, n, ncols = h
        wi = self.wbi
        self.wbi = (self.wbi + 1) % NWB
        wb = self.bv(O_WB + wi * 1024, n // 2).rearrange("p (k n) -> p k n", n=ncols)
        if ceng == 'act':
            self.S.op('act', lambda e: e.activation(out=wb, in_=stg, func=AF.Copy), reads=[('stg', si)],
                      writes=[('wb', wi)])
        else:
            self.S.op(ceng, lambda e: e.tensor_copy(out=wb, in_=stg), reads=[('stg', si)], writes=[('wb', wi)])
        return wb, ('wb', wi)

    def pf(self, tag, loader):
        if tag not in self.pre:
            self.pre[tag] = loader()

    def take(self, tag, loader):
        if tag in self.pre:
            return self.pre.pop(tag)
        return loader()

    def load_bf(self, src, shape3):
        S = self.S
        wi = self.wbi
        self.wbi = (self.wbi + 1) % NWB
        a, b = shape3
        wb = self.bv(O_WB + wi * 1024, a * b // 2)
        S.dma('sp', wb, src, writes=[('wb', wi)])
        return wb.rearrange("p (a b) -> p a b", b=b), ('wb', wi)

    def dump(self, name, src_ap, reads):
        if name in self.dbg_t:
            self.S.dma('sp', self.dbg_t[name], src_ap, reads=reads, writes=[('dbg', name)])

    def load_consts(self):
        S = self.S
        A = self.A
        S.dma('sp', self.fv(C_IDF, 256), self.c_f32, writes=['consts'])
        S.dma('sp', self.bv(C_IDB, 352), self.c_bf, writes=['consts'])
        S.dma('sp', self.fv(C_SF, 16), self.c_sf, writes=['consts'])
        S.op('pool', lambda e: e.memset(self.fv(C_EPS, 1), EPS), writes=['consts'])
        with self.nc.allow_non_contiguous_dma(reason="tiny param loads"):
            for l in range(DEPTH):
                S.dma('sp', self.fv(C_GMIX + 8 * l, 8), self.w["norm_mix"][l].rearrange("(k p) -> p k", p=128),
                      writes=['consts'])
                S.dma('sp', self.fv(C_GFFN + 8 * l, 8), self.w["norm_ffn"][l].rearrange("(k p) -> p k", p=128),
                      writes=['consts'])
                cv = self.fv(C_CONV + 48 * l, 48).rearrange("p (c t) -> p c t", t=4)
                for t in range(3):
                    S.dma('sp', cv[:, :, t], self.w["conv_w"][l, t].rearrange("(c p) -> p c", p=128),
                          writes=['consts'])
                S.dma('sp', cv[:, :, 3], self.w["conv_b"][l].rearrange("(c p) -> p c", p=128), writes=['consts'])
            S.dma('sp', self.fv(C_GFIN, 8), self.w["norm_final"].rearrange("(k p) -> p k", p=128), writes=['consts'])
        for l in range(DEPTH):
            S.dma('sp', self.fv(C_SKIP + 512 * l, 512, parts=1), self.w["hy_skip"][l:l + 1, :], writes=['consts'])
        self.idf = self.fv(C_IDF, 128)
        self.onf = self.fv(C_ONF, 128)
        cb = self.bv(C_IDB, 352)
        self.idb = cb[:, 0:128]
        self.onb = cb[:, 128:192]
        self.rot = cb[:, 192:320]
        self.mask = cb[:, 320:704]
        self.eps = self.fv(C_EPS, 1)
        S.barrier_all()

    def filter_prologue(self, l):
        S = self.S
        W = self.w
        o = O_HT
        zt = self.fv(o, T, parts=33); o += T
        hA = self.fv(o, T, parts=64); o += T
        hB = self.fv(o, T, parts=64); o += T
        w1 = self.fv(o, 64, parts=33); o += 64
        wi0 = self.fv(o, 64, parts=64); o += 64
        wi1 = self.fv(o, 64, parts=64); o += 64
        wo = self.fv(o, 1024, parts=64); o += 1024
        sm = self.fv(o, 16, parts=64); o += 16
        wrps = [self.fv(o + 512 * i, 512, parts=64) for i in range(2)]; o += 1024
        hf = [self.fv(o + 512 * i, 512) for i in range(2)]; o += 1024
        hb = [self.fv(o + 512 * i, 512) for i in range(2)]; o += 1024
        wn = [self.fv(o + 512 * i, 512) for i in range(2)]; o += 1024
        hsum = self.bv(o, 4096).rearrange("p (a b) -> p a b", b=512); o += 4096
        hdif = self.bv(o, 4096).rearrange("p (a b) -> p a b", b=512); o += 4096
        kout = [self.fv(o + 1024 * i, 1024) for i in range(2)]; o += 2048
        assert o <= AW
        S.dma('sp', zt, self.c_zt, writes=['f_zt'])
        S.dma('sp', w1, W["filt_w1"][l], writes=['f_w'])
        S.dma('sp', wi0, W["filt_w_inner"][l, 0], writes=['f_w'])
        S.dma('sp', wi1, W["filt_w_inner"][l, 1], writes=['f_w'])
        S.dma('sp', wo, W["filt_w_out"][l], writes=['f_w'])
        with self.nc.allow_non_contiguous_dma(reason="tiny param loads"):
            S.dma('sp', sm[:, 0:1], W["filt_b1"][l].rearrange("(p o) -> p o", o=1), writes=['f_sm'])
            S.dma('sp', sm[:, 1:3], W["filt_b_inner"][l].rearrange("i p -> p i"), writes=['f_sm'])
            S.dma('sp', sm[:, 3:4], W["filt_freq"][l].rearrange("(p o) -> p o", o=1), writes=['f_sm'])
        for i in range(3):
            S.op('dve', lambda e, i=i: e.tensor_tensor(out=sm[:, 4 + i:5 + i], in0=sm[:, i:i + 1], in1=sm[:, 3:4],
                                                      op=ALU.mult), reads=['f_sm'], writes=['f_sm'])
        lay = [(w1, 33, zt, 'f_zt'), (wi0, 64, hA, 'hA'), (wi1, 64, hB, 'hB')]
        outs = [(hA, 'hA'), (hB, 'hB'), (hA, 'hA')]
        wi_ = 0
        for li, (wt, kk, src, skey) in enumerate(lay):
            dst, dkey = outs[li]
            for tb in range(4):
                sl = slice(tb * 512, (tb + 1) * 512)
                b, bk = self.bank()
                rk = [(skey, tb)] if li > 0 else ['f_zt']
                S.op('pe', lambda e: e.matmul(b[0:64, :], lhsT=wt[0:kk, :], rhs=src[0:kk, sl], start=True, stop=True),
                     reads=['f_w'] + rk, writes=[bk])
                S.op('dve', lambda e: e.tensor_scalar(out=dst[:, sl], in0=b[0:64, :], scalar1=sm[:, 3:4],
                                                      scalar2=sm[:, 4 + li:5 + li], op0=ALU.mult, op1=ALU.add),
                     reads=['f_sm'], writes=[bk, (dkey, tb)])
                for cmp_op, thr, shift in ((ALU.is_gt, math.pi, -2 * math.pi), (ALU.is_lt, -math.pi, 2 * math.pi)):
                    wr = wrps[wi_ % 2]
                    wkey = ('wrp', wi_ % 2)
                    wi_ += 1
                    S.op('dve', lambda e: e.tensor_scalar(out=wr[:, :], in0=dst[:, sl], scalar1=thr, scalar2=shift,
                                                          op0=cmp_op, op1=ALU.mult),
                         reads=[(dkey, tb)], writes=[wkey])
                    S.op('dve', lambda e: e.tensor_tensor(out=dst[:, sl], in0=dst[:, sl], in1=wr[:, :], op=ALU.add),
                         reads=[wkey], writes=[(dkey, tb)])
                S.op('act', lambda e: e.activation(out=dst[:, sl], in_=dst[:, sl], func=AF.Sin),
                     writes=[(dkey, tb)])
        hid = hA
        skip = self.fv(C_SKIP + 512 * l, 512, parts=1)
        for mc in range(16):
            i = mc % 2
            S.dma('sp', wn[i], self.c_win[mc], writes=[('f_wn', i)])
            for half, dst in ((0, hf[i]), (1, hb[i])):
                b, bk = self.bank()
                S.op('pe', lambda e: e.matmul(b[:, :], lhsT=hid[:, mc * 128:(mc + 1) * 128],
                                              rhs=wo[:, half * 512:(half + 1) * 512], start=True, stop=True),
                     reads=['f_w', ('hA', mc // 4)], writes=[bk])
                S.op('dve', lambda e: e.tensor_tensor(out=dst, in0=b[:, :], in1=wn[i], op=ALU.mult),
                     reads=[('f_wn', i)], writes=[bk, ('f_h', half, i)])
            if mc == 0:
                S.op('dve', lambda e: e.memset(hb[i][0:1, :], 0.0), writes=[('f_h', 1, i)])
                S.op('dve', lambda e: e.tensor_tensor(out=hf[i][0:1, :], in0=hf[i][0:1, :], in1=skip, op=ALU.add),
                     writes=[('f_h', 0, i)])
            S.op('pool', lambda e: e.tensor_tensor(out=hsum[:, mc, :], in0=hf[i], in1=hb[i], op=ALU.add),
                 reads=[('f_h', 0, i), ('f_h', 1, i)], writes=['f_hs'])
            S.op('pool', lambda e: e.tensor_tensor(out=hdif[:, mc, :], in0=hb[i], in1=hf[i], op=ALU.subtract),
                 reads=[('f_h', 0, i), ('f_h', 1, i)], writes=['f_hs'])
        for fc in range(16):
            i = fc % 2
            ko = kout[i]
            mre, kre = self.load_bf(self.c_mf[fc], (16, 128))
            mim, kim = self.load_bf(self.c_mf[16 + fc], (16, 128))
            bre, bkre = self.bank()
            bim, bkim = self.bank()
            for mc in range(16):
                S.op('pe', lambda e: e.matmul(bre[:, :], lhsT=mre[:, mc, :], rhs=hsum[:, mc, :], start=(mc == 0),
                                              stop=(mc == 15)), reads=[kre, 'f_hs'], writes=[bkre], inc=(mc == 15))
            for mc in range(16):
                S.op('pe', lambda e: e.matmul(bim[:, :], lhsT=mim[:, mc, :], rhs=hdif[:, mc, :], start=(mc == 0),
                                              stop=(mc == 15)), reads=[kim, 'f_hs'], writes=[bkim], inc=(mc == 15))
            sf = self.fv(C_SF + fc, 1)
            S.op('act', lambda e: e.activation(out=ko[:, 0:512], in_=bre[:, :], func=AF.Identity, scale=sf),
                 reads=['consts'], writes=[bkre, ('f_ko', i)])
            S.op('act', lambda e: e.activation(out=ko[:, 512:1024], in_=bim[:, :], func=AF.Identity, scale=sf),
                 reads=['consts'], writes=[bkim, ('f_ko', i)])
            if fc == 0:
                bn, bkn = self.bank()
                for mc in range(16):
                    S.op('pe', lambda e: e.matmul(bn[0:1, :], lhsT=mim[:, mc, 0:1], rhs=hsum[:, mc, :],
                                                  start=(mc == 0), stop=(mc == 15)),
                         reads=[kim, 'f_hs'], writes=[bkn], inc=(mc == 15))
                S.op('act', lambda e: e.activation(out=ko[0:1, 512:1024], in_=bn[0:1, :], func=AF.Identity,
                                                   scale=1.0 / NFFT), writes=[bkn, ('f_ko', i)])
            S.dma('act', self.kspec[l, fc], ko, reads=[('f_ko', i)], writes=['kspec'])
        S.barrier_all()

    def xT(self):
        return self.fv(O_X, 8 * T).rearrange("p (c t) -> p c t", t=T)

    def hT(self):
        return self.bv(O_HT, 8192).rearrange("p (c t) -> p c t", t=T)

    def load_x(self, s):
        S = self.S
        xT = self.xT()
        xin = self.fv(O_TMP, 4096).rearrange("p (a d) -> p a d", d=D)
        for tb in range(4):
            S.dma('sp', xin, self.x[s, tb * 512:(tb + 1) * 512, :].rearrange("(a p) d -> p a d", p=128),
                  writes=['xin'])
            for c in range(8):
                b, bk = self.bank()
                for a in range(4):
                    S.op('pe', lambda e: e.transpose(out=b[:, a * 128:(a + 1) * 128],
                                                     in_=xin[:, a, c * 128:(c + 1) * 128], identity=self.idf),
                         reads=['xin', 'consts'], writes=[bk], inc=(a == 3))
                S.op('act', lambda e: e.activation(out=xT[:, c, tb * 512:(tb + 1) * 512], in_=b[:, :], func=AF.Copy),
                     writes=[bk, ('xT', c, tb)])
        S.barrier_all()

    def norm(self, tmp_off, gain):
        S = self.S
        xT, hT = self.xT(), self.hT()
        sq = [self.fv(tmp_off + 512 * i, 512) for i in range(2)]
        rs_ = [self.fv(tmp_off + 1024 + 512 * i, 512) for i in range(4)]
        banks = []
        for tb in range(4):
            sl = slice(tb * 512, (tb + 1) * 512)
            b, bk = self.bank()
            banks.append((b, bk))
            for c in range(8):
                i = c % 2
                S.op('act', lambda e: e.activation(out=sq[i], in_=xT[:, c, sl], func=AF.Square),
                     reads=[('xT', c, tb)], writes=[('sq', i)])
                S.op('pe', lambda e: e.matmul(b[:, :], lhsT=self.onf, rhs=sq[i], start=(c == 0), stop=(c == 7)),
                     reads=[('sq', i), 'consts'], writes=[bk])
        for tb in range(4):
            b, bk = banks[tb]
            S.op('act', lambda e: e.activation(out=rs_[tb], in_=b[:, :], func=AF.Ln, bias=self.eps, scale=1.0 / D),
                 reads=['consts'], writes=[bk, ('rs', tb)])
        for tb in range(4):
            S.op('act', lambda e: e.activation(out=rs_[tb], in_=rs_[tb], func=AF.Exp, scale=-0.5),
                 writes=[('rs', tb)])
        for tb in range(4):
            sl = slice(tb * 512, (tb + 1) * 512)
            for c in range(8):
                S.op('dve', lambda e: e.scalar_tensor_tensor(out=hT[:, c, sl], in0=xT[:, c, sl], scalar=gain[:, c:c + 1],
                                                             in1=rs_[tb], op0=ALU.mult, op1=ALU.mult),
                     reads=[('xT', c, tb), ('rs', tb), 'consts'], writes=[('hT', tb)])

    def spill_x(self):
        S = self.S
        for c in (2, 3, 4, 5, 6, 7, 0, 1):
            S.dma('sp', self.xs[:, c * T:(c + 1) * T], self.fv(O_X + c * T, T),
                  reads=[('xT', c, tb) for tb in range(4)], writes=[('xs', c)])

    def prefetch_hyena(self, l):
        W = self.w["w_in"][l]
        for j in range(2):
            self.pf(('hy', l, j), lambda: self.load_w(W[:, 256 * j:256 * j + 256], 8, 256))
        for j in range(2):
            self.pf(('hyA', l, j), lambda: self.load_w(W[:, 512 + 256 * j:512 + 256 * j + 256], 8, 256))
            self.pf(('hyB', l, j), lambda: self.load_w(W[:, 1024 + 256 * j:1024 + 256 * j + 256], 8, 256))

    def prefetch_attention(self, l):
        W = self.w["w_in"][l]
        base = 1536
        for which in range(3):
            self.pf(('att', l, 0, 0, which),
                    lambda: self.load_w(W[:, base + 512 * which: base + 512 * which + 128], 8, 128))

    def prefetch_merge(self, l):
        W = self.w["w_in"][l]
        self.pf(('mg', l, 0, 0), lambda: self.load_w(W[:, 6144: 6144 + 128], 8, 128))
        self.pf(('mg', l, 0, 1), lambda: self.load_w(W[:, 7168: 7168 + 128], 8, 128))
        self.pf(('mg', l, 0, 2), lambda: self.load_w(self.w["p_hy"][l][:, 0:128], 4, 128))
        self.pf(('mg', l, 0, 3), lambda: self.load_w(self.w["p_att"][l][:, 0:128], 4, 128))

    def prefetch_wo(self, l):
        for j in range(2):
            self.pf(('wo', l, j), lambda: self.load_w(self.w["w_o"][l][:, j * 256:(j + 1) * 256], 8, 256))

    def prefetch_ffn(self, l):
        W1 = self.w["w_ff1"][l]
        self.pf(('w1', l, 0), lambda: [self.load_w(W1[:, 256 * i: 256 * i + 256], 8, 256) for i in range(2)])

    def hyena(self, l):
        S = self.S
        hT = self.hT()
        W = self.w["w_in"][l]
        gmix = self.fv(C_GMIX + 8 * l, 8)
        cv = self.fv(C_CONV + 48 * l, 48).rearrange("p (c t) -> p c t", t=4)
        o = O_U
        x0 = self.fv(o, 4 * T).rearrange("p (c t) -> p c t", t=T); o += 4 * T
        u = self.bv(o, 4096).rearrange("p (c t) -> p c t", t=T); o += 4096
        o_b = o
        raws = [self.fv(o, T), self.fv(o + T, T)]; o += 2 * T
        rawi = [0]
        cA = self.fv(o, T); o += T
        cB = self.fv(o, T); o += T
        assert o <= AW

        def conv_chunk(wt, wk, col, ch, dst, dkey, extra=()):
            raw = raws[rawi[0] % 2]
            rkey = ('raw', rawi[0] % 2)
            rawi[0] += 1
            for tb in range(4):
                b, bk = self.bank()
                for kc in range(8):
                    S.op('pe', lambda e: e.matmul(b[:, :], lhsT=wt[:, kc, col:col + 128],
                                                  rhs=hT[:, kc, tb * 512:(tb + 1) * 512], start=(kc == 0),
                                                  stop=(kc == 7)), reads=[wk, ('hT', tb)], writes=[bk], inc=(kc == 7))
                S.op('act', lambda e: e.activation(out=raw[:, tb * 512:(tb + 1) * 512], in_=b[:, :], func=AF.Copy),
                     writes=[bk, rkey])
                S.op('act', lambda e: e.activation(out=dst[:, tb * 512:(tb + 1) * 512], in_=b[:, :], func=AF.Identity,
                                                   scale=cv[:, ch, 1:2], bias=cv[:, ch, 3:4]),
                     reads=['consts'], writes=[bk, dkey] + list(extra))
            S.op('dve', lambda e: e.scalar_tensor_tensor(out=dst[:, 1:T], in0=raw[:, 0:T - 1], scalar=cv[:, ch, 0:1],
                                                         in1=dst[:, 1:T], op0=ALU.mult, op1=ALU.add),
                 reads=[rkey, 'consts'], writes=[dkey])
            S.op('dve', lambda e: e.scalar_tensor_tensor(out=dst[:, 0:T - 1], in0=raw[:, 1:T], scalar=cv[:, ch, 2:3],
                                                         in1=dst[:, 0:T - 1], op0=ALU.mult, op1=ALU.add),
                 reads=[rkey, 'consts'], writes=[dkey])

        for j in range(2):
            wt, wk = self.take(('hy', l, j), lambda: self.load_w(W[:, 256 * j:256 * j + 256], 8, 256))
            for i in range(2):
                ch = 2 * j + i
                conv_chunk(wt, wk, 128 * i, ch, x0[:, ch, :], ('x0', ch), [('xT', 2 + ch, t) for t in range(4)])
        for j in range(2):
            wa, wak = self.take(('hyA', l, j), lambda: self.load_w(W[:, 512 + 256 * j:512 + 256 * j + 256], 8, 256))
            wb_, wbk = self.take(('hyB', l, j), lambda: self.load_w(W[:, 1024 + 256 * j:1024 + 256 * j + 256], 8, 256))
            for i in range(2):
                ch = 2 * j + i
                conv_chunk(wa, wak, 128 * i, 4 + ch, cA, 'cA')
                conv_chunk(wb_, wbk, 128 * i, 8 + ch, cB, 'cB')
                S.op('dve', lambda e: e.tensor_tensor(out=u[:, ch, :], in0=cA, in1=cB, op=ALU.mult),
                     reads=['cA', 'cB'], writes=[('u', ch)] + [('xT', 6 + ch // 2, t) for t in range(4)])
        uT = self.bv(O_U + 20480, 4096).rearrange("p (a c) -> p a c", c=512)
        Y = self.bv(O_U + 12288, 8192).rearrange("p (f c) -> p f c", c=512)
        ksp = [self.fv(O_U + 8192 + 1024 * i, 1024) for i in range(2)]
        tt_ = [self.fv(O_U + 10240 + 512 * i, 512) for i in range(4)]
        for tt in range(16):
            b, bk = self.bank()
            bb = b[:, :].bitcast(BF16)
            for cc in range(4):
                S.op('pe', lambda e: e.transpose(out=bb[:, cc * 128:(cc + 1) * 128],
                                                 in_=u[:, cc, tt * 128:(tt + 1) * 128], identity=self.idb),
                     reads=[('u', cc), 'consts'], writes=[bk], inc=(cc == 3))
            S.op('act', lambda e: e.activation(out=uT[:, tt, :], in_=bb[:, 0:512], func=AF.Copy),
                 writes=[bk, 'uT'])
        S.op('dve', lambda e: e.memset(tt_[0][0:1, 0:2], 0.0),
             writes=[('u', c_) for c_ in range(4)] + [('raw', 0), ('raw', 1), 'cA', 'cB', ('ksp', 0), ('ksp', 1),
                                                      't1', 't2', 't3', 't4', 'Y'])
        for fc in range(16):
            i = fc % 2
            mre, kre = self.load_bf(self.c_mf[fc], (16, 128))
            mim, kim = self.load_bf(self.c_mf[16 + fc], (16, 128))
            S.dma('sp', ksp[i], self.kspec[l, fc], reads=['kspec'], writes=[('ksp', i)])
            bre, bkre = self.bank()
            bim, bkim = self.bank()
            for mc in range(16):
                S.op('pe', lambda e: e.matmul(bre[:, :], lhsT=mre[:, mc, :], rhs=uT[:, mc, :], start=(mc == 0),
                                              stop=(mc == 15)), reads=[kre, 'uT'], writes=[bkre], inc=(mc == 15))
            for mc in range(16):
                S.op('pe', lambda e: e.matmul(bim[:, :], lhsT=mim[:, mc, :], rhs=uT[:, mc, :], start=(mc == 0),
                                              stop=(mc == 15)), reads=[kim, 'uT'], writes=[bkim], inc=(mc == 15))
            Ka = ksp[i][:, 0:512]
            Kb = ksp[i][:, 512:1024]
            t1, t2, t3, t4 = tt_
            S.op('dve', lambda e: e.tensor_tensor(out=t1, in0=bre[:, :], in1=Ka, op=ALU.mult),
                 reads=[('ksp', i)], writes=[bkre, 't1'])
            S.op('dve', lambda e: e.tensor_tensor(out=t2, in0=bim[:, :], in1=Kb, op=ALU.mult),
                 reads=[('ksp', i)], writes=[bkim, 't2'])
            S.op('dve', lambda e: e.tensor_tensor(out=t3, in0=bre[:, :], in1=Kb, op=ALU.mult),
                 reads=[('ksp', i)], writes=[bkre, 't3'])
            S.op('dve', lambda e: e.tensor_tensor(out=t4, in0=bim[:, :], in1=Ka, op=ALU.mult),
                 reads=[('ksp', i)], writes=[bkim, 't4'])
            S.op('dve', lambda e: e.tensor_tensor(out=Y[:, fc, :], in0=t1, in1=t2, op=ALU.add),
                 reads=['t1', 't2'], writes=['Y'])
            S.op('dve', lambda e: e.tensor_tensor(out=Y[:, 16 + fc, :], in0=t4, in1=t3, op=ALU.subtract),
                 reads=['t3', 't4'], writes=['Y'])
            if fc == 0:
                S.op('dve', lambda e: e.tensor_copy(out=Y[0:1, 0, :], in_=t1[0:1, :]), reads=['t1'], writes=['Y'])
                S.op('dve', lambda e: e.tensor_copy(out=Y[0:1, 16, :], in_=t2[0:1, :]), reads=['t2'], writes=['Y'])
        yhy = self.bv(O_X, 4096).rearrange("p (c t) -> p c t", t=T)
        for nb in range(4):
            banks = [self.bank() for _ in range(4)]
            for fg in range(8):
                mi, mik = self.load_bf(self.c_mi[nb, fg], (4, 512))
                for j in range(4):
                    fch = fg * 4 + j
                    for cc in range(4):
                        b, bk = banks[cc]
                        S.op('pe', lambda e: e.matmul(b[:, :], lhsT=Y[:, fch, cc * 128:(cc + 1) * 128],
                                                      rhs=mi[:, j, :], start=(fch == 0), stop=(fch == 31)),
                             reads=[mik, 'Y'], writes=[bk], inc=(fch == 31 or j == 3))
            for cc in range(4):
                b, bk = banks[cc]
                S.op('dve', lambda e: e.tensor_tensor(out=yhy[:, cc, nb * 512:(nb + 1) * 512], in0=b[:, :],
                                                      in1=x0[:, cc, nb * 512:(nb + 1) * 512], op=ALU.mult),
                     reads=[('x0', cc)], writes=[bk, 'yhy'] + [('xT', c_, t) for c_ in (0, 1) for t in range(4)])
        self.dump('yhy', self.fv(O_X, 4096), ['yhy'])
        self.prefetch_attention(l)
        S.barrier_all()

    def attention(self, l):
        S = self.S
        hT = self.hT()
        W = self.w["w_in"][l]
        gmix = self.fv(C_GMIX + 8 * l, 8)
        yatt = self.bv(O_U, 4096).rearrange("p (c t) -> p c t", t=T)
        o = O_U + 4096
        cs = self.fv(o, 2 * T); o += 2 * T
        cosT, sinT = cs[:, 0:T], cs[:, T:2 * T]
        accN = self.fv(o, T); o += T
        accD = self.fv(o, T); o += T
        qk = [[self.bv(o + 1024 * (2 * i + j), 1024) for j in range(2)] for i in range(2)]; o += 4096
        vch = [self.bv(o + 2112 * i, 2112).rearrange("p (n f) -> p n f", f=128) for i in range(2)]; o += 4224
        NPT = 6
        pT = [self.bv(o + 128 * i, 128) for i in range(NPT)]; o += 128 * NPT
        qb16 = [self.bv(o + 256 * i, 256) for i in range(2)]; o += 512
        t12 = [[self.fv(o + 512 * (2 * i + j), 512) for j in range(2)] for i in range(2)]; o += 2048
        etmp = [self.fv(o + 512 * i, 512) for i in range(2)]; o += 1024
        eti = [0]
        assert o <= AW, o
        self.bank_rng = (4, 8)
        S.dma('sp', cs, self.c_cs, writes=['cs'])
        it = 0
        pti = 0
        for hp in range(4):
            for g, (window, d) in enumerate(GROUPS):
                n = T // d
                par = it % 2
                it += 1
                qr, kr = qk[par]
                vc = vch[par]
                base = 1536 + g * 1536 + hp * 128
                qkw = [self.take(('att', l, hp, g, which), lambda: self.load_w(
                    W[:, base + 512 * which: base + 512 * which + 128], 8, 128)) for which in range(2)]
                qitems = [(which, tb) for which in range(2) for tb in range(4)]
                qst = {}

                def q_proj(k):
                    which, tb = qitems[k]
                    wt, wk = qkw[which]
                    tp = k % 2
                    j0 = tb * 512
                    bA, bkA = self.bank()
                    for kc in range(8):
                        S.op('pe', lambda e: e.matmul(bA[:, :], lhsT=wt[:, kc, :], rhs=hT[:, kc, j0:j0 + 512],
                                                      start=(kc == 0), stop=(kc == 7)),
                             reads=[wk, ('hT', tb)], writes=[bkA], inc=(kc == 7))
                    S.op('act', lambda e: e.activation(out=qb16[tp], in_=bA[:, :], func=AF.Copy),
                         writes=[bkA, ('qb16', tp)])
                    qst[k] = (bA, bkA)

                def q_rot(k):
                    which, tb = qitems[k]
                    dst = (qr, kr)[which]
                    tp = k % 2
                    j0 = tb * 512
                    bA, bkA = qst.pop(k)
                    bB, bkB = self.bank()
                    S.op('pe', lambda e: e.matmul(bB[:, :], lhsT=self.rot, rhs=qb16[tp], start=True, stop=True),
                         reads=[('qb16', tp), 'consts'], writes=[bkB])
                    t1, t2 = t12[tp]
                    S.op('dve', lambda e: e.tensor_tensor(out=t1, in0=bA[:, :], in1=cosT[:, j0:j0 + 512],
                                                          op=ALU.mult), reads=['cs'], writes=[bkA, ('t1', tp)])
                    S.op('dve', lambda e: e.tensor_tensor(out=t2, in0=bB[:, :], in1=sinT[:, j0:j0 + 512],
                                                          op=ALU.mult), reads=['cs'], writes=[bkB, ('t2', tp)])
                    if d == 1:
                        dv, a1, a2 = dst[:, j0:j0 + 512], t1, t2
                    else:
                        m0, ml = j0 // d, 512 // d
                        dv = dst.rearrange("p (r m) -> p r m", r=d)[:, :, m0:m0 + ml]
                        a1 = t1.rearrange("p (m r) -> p r m", r=d)
                        a2 = t2.rearrange("p (m r) -> p r m", r=d)
                    S.op('pool', lambda e: e.tensor_tensor(out=dv, in0=a1, in1=a2, op=ALU.add),
                         reads=[('t1', tp), ('t2', tp)], writes=[('qk', par, which)])

                q_proj(0)
                for k in range(len(qitems)):
                    if k + 1 < len(qitems):
                        q_proj(k + 1)
                    q_rot(k)
                wt, wk = self.take(('att', l, hp, g, 2), lambda: self.load_w(W[:, base + 1024: base + 1024 + 128], 8, 128))
                nj = n // 128 + 1
                chunks = []
                for r in range(d):
                    if n == 128:
                        chunks.append((r, -1, 0, 128, 0, len(chunks)))
                        continue
                    for j in range(nj):
                        k0 = max(0, 128 * j - 64)
                        k1 = min(n, 128 * j + 64)
                        pb = 64 if j == 0 else 0
                        chunks.append((r, j, k0, k1 - k0, pb, len(chunks)))
                def v_proj(lo, hi, rng):
                    self.bank_rng = rng
                    for c0 in range(lo, hi, 4):
                        b, bk = self.bank()
                        grp = chunks[c0:min(c0 + 4, hi)]
                        for gi, (r, j, k0, nk, pb, idx) in enumerate(grp):
                            for kc in range(8):
                                lhsT = perm_view(hT[:, kc, :], d, r * n + k0, nk)
                                S.op('pe', lambda e: e.matmul(b[pb:pb + nk, gi * 128:(gi + 1) * 128], lhsT=lhsT,
                                                              rhs=wt[:, kc, :], start=(kc == 0), stop=(kc == 7)),
                                     reads=[wk] + [('hT', t) for t in range(4)], writes=[bk],
                                     inc=(kc == 7 and gi == len(grp) - 1))
                        for gi, (r, j, k0, nk, pb, idx) in enumerate(grp):
                            S.op('act', lambda e: e.activation(out=vc[pb:pb + nk, idx, :],
                                                               in_=b[pb:pb + nk, gi * 128:(gi + 1) * 128],
                                                               func=AF.Copy), writes=[bk, ('vc', par)])
                    self.bank_rng = (4, 8)

                nqb = n // 128
                segbanks = {}
                stA = {}

                def stage_a(ci):
                    nonlocal pti
                    (r, j, k0, nk, pb, idx) = chunks[ci]
                    if j == -1:
                        qbs, mcol0 = [0], 256
                    else:
                        qbs = [qb for qb in (j - 1, j) if 0 <= qb < nqb]
                        mcol0 = 0 if qbs[0] == j - 1 else 128
                    q0 = r * n + 128 * qbs[0]
                    nq = 128 * len(qbs)
                    res = []
                    for h in range(2):
                        hs = slice(64 * h, 64 * h + 64)
                        bS, bkS = self.bank()
                        p_ = pT[pti % NPT]
                        pk = ('pT', pti % NPT)
                        pti += 1
                        S.op('pe', lambda e: e.matmul(bS[pb:pb + nk, 0:nq], lhsT=kr[hs, r * n + k0: r * n + k0 + nk],
                                                      rhs=qr[hs, q0:q0 + nq], start=True, stop=True),
                             reads=[('qk', par, 0), ('qk', par, 1)], writes=[bkS])
                        res.append((bS, bkS, p_, pk))
                    for h in range(2):
                        bS, bkS, p_, pk = res[h]
                        S.op('act', lambda e: e.activation(out=p_[pb:pb + nk, 0:nq], in_=bS[pb:pb + nk, 0:nq],
                                                           func=AF.Exp, scale=0.125), writes=[bkS, pk])
                        S.op('dve' if h == 0 else 'pool', lambda e: e.tensor_tensor(out=p_[pb:pb + nk, 0:nq], in0=p_[pb:pb + nk, 0:nq],
                                                              in1=self.mask[pb:pb + nk, mcol0:mcol0 + nq],
                                                              op=ALU.mult), reads=['consts'], writes=[pk])
                    stA[ci] = (qbs, res)

                def stage_b(ci):
                    (r, j, k0, nk, pb, idx) = chunks[ci]
                    qbs, res = stA.pop(ci)
                    for qi, qb in enumerate(qbs):
                        gq = (r * n) // 128 + qb
                        seg = gq // 4
                        if seg not in segbanks:
                            sb = 2 * (seg % 2)
                            segbanks[seg] = ((self.ps[sb], ('ps', sb)), (self.ps[sb + 1], ('ps', sb + 1)))
                        (bN, bkN), (bD, bkD) = segbanks[seg]
                        col = (gq % 4) * 128
                        first = (qb == j) or j == -1
                        last = (qb != j) or j == -1
                        for h in range(2):
                            hs = slice(64 * h, 64 * h + 64)
                            bS, bkS, p_, pk = res[h]
                            S.op('pe', lambda e: e.matmul(bN[hs, col:col + 128], lhsT=vc[pb:pb + nk, idx, hs],
                                                          rhs=p_[pb:pb + nk, qi * 128:(qi + 1) * 128],
                                                          start=first, stop=last),
                                 reads=[pk, ('vc', par)], writes=[bkN])
                        for h in range(2):
                            hs = slice(64 * h, 64 * h + 64)
                            bS, bkS, p_, pk = res[h]
                            S.op('pe', lambda e: e.matmul(bD[hs, col:col + 128], lhsT=self.onb[pb:pb + nk, :],
                                                          rhs=p_[pb:pb + nk, qi * 128:(qi + 1) * 128],
                                                          start=first, stop=last),
                                 reads=[pk, 'consts'], writes=[bkD])
                    if j >= 1 or j == -1:
                        gq = (r * n) // 128 + (j - 1 if j >= 1 else 0)
                        if gq % 4 == 3:
                            seg = gq // 4
                            (bN, bkN), (bD, bkD) = segbanks.pop(seg)
                            for bsrc, bks, acc, ak in ((bN, bkN, accN, 'accN'), (bD, bkD, accD, 'accD')):
                                av = perm_view(acc, d, 512 * seg, 512)
                                src = like(bsrc[:, :], av)
                                if g == 0 and ak == 'accN':
                                    S.op('act', lambda e: e.activation(out=av, in_=src, func=AF.Copy),
                                         writes=[bks, ak])
                                elif g == 0:
                                    S.op('dve', lambda e: e.tensor_copy(out=av, in_=src), writes=[bks, ak])
                                else:
                                    S.op('dve', lambda e: e.tensor_tensor(out=av, in0=src, in1=av, op=ALU.add),
                                         writes=[bks, ak])

                nxt = [(hp_, g_) for hp_ in range(4) for g_ in range(3)]
                ni = nxt.index((hp, g)) + 1
                pend = {}

                def prefetch_step(ci):
                    if ni >= len(nxt):
                        return
                    nhp, ng = nxt[ni]
                    nbase = 1536 + ng * 1536 + nhp * 128

                    def src(which):
                        return W[:, nbase + 512 * which: nbase + 512 * which + 128]
                    if ci == 1:
                        pend[0] = self.load_w_dma(src(0), 8, 128)
                        pend[1] = self.load_w_dma(src(1), 8, 128)
                    elif ci == 6:
                        for which in range(2):
                            self.pre[('att', l, nhp, ng, which)] = self.load_w_cast(pend.pop(which), 'act')
                        pend[2] = self.load_w_dma(src(2), 8, 128)
                    elif ci == 11:
                        self.pre[('att', l, nhp, ng, 2)] = self.load_w_cast(pend.pop(2), 'act')

                vsplit = 8 if len(chunks) > 16 else 4
                v_proj(0, vsplit, (0, 8))
                stage_a(0)
                stage_a(1)
                v_proj(vsplit, len(chunks), (0, 4))
                for ci in range(len(chunks)):
                    if ci + 2 < len(chunks):
                        stage_a(ci + 2)
                    stage_b(ci)
                    prefetch_step(ci)
                assert not segbanks
            S.op('act', lambda e: e.activation(out=accD, in_=accD, func=AF.Ln), writes=['accD'])
            S.op('act', lambda e: e.activation(out=accD, in_=accD, func=AF.Exp, scale=-1.0), writes=['accD'])
            S.op('dve', lambda e: e.tensor_tensor(out=yatt[:, hp, :], in0=accN, in1=accD, op=ALU.mult),
                 reads=['accN', 'accD'], writes=['yatt'])
        self.dump('yatt', self.fv(O_U, 4096), ['yatt'])
        self.bank_rng = (0, 8)
        self.prefetch_merge(l)
        S.barrier_all()

    def merge(self, l):
        S = self.S
        hT = self.hT()
        W = self.w["w_in"][l]
        gmix = self.fv(C_GMIX + 8 * l, 8)
        yhy = self.bv(O_X, 4096).rearrange("p (c t) -> p c t", t=T)
        yatt = self.bv(O_U, 4096).rearrange("p (c t) -> p c t", t=T)
        merged = self.bv(O_U + 17408, 8192).rearrange("p (c t) -> p c t", t=T)
        tmp = [[self.fv(O_U + 13312 + 512 * (4 * i + j), 512) for j in range(4)] for i in range(2)]
        it = 0
        for fc in range(8):
            wg1, k1 = self.take(('mg', l, fc, 0), lambda: self.load_w(W[:, 6144 + fc * 128: 6144 + fc * 128 + 128], 8, 128))
            wg2, k2 = self.take(('mg', l, fc, 1), lambda: self.load_w(W[:, 7168 + fc * 128: 7168 + fc * 128 + 128], 8, 128))
            wp1, k3 = self.take(('mg', l, fc, 2), lambda: self.load_w(self.w["p_hy"][l][:, fc * 128:(fc + 1) * 128], 4, 128))
            wp2, k4 = self.take(('mg', l, fc, 3), lambda: self.load_w(self.w["p_att"][l][:, fc * 128:(fc + 1) * 128], 4, 128))
            for tb in range(4):
                sl = slice(tb * 512, (tb + 1) * 512)
                s1, s2, m1, m2 = tmp[it % 2]
                tk = it % 2
                it += 1
                b1, bk1 = self.bank()
                b2, bk2 = self.bank()
                b3, bk3 = self.bank()
                b4, bk4 = self.bank()
                for kc in range(8):
                    S.op('pe', lambda e: e.matmul(b1[:, :], lhsT=wg1[:, kc, :], rhs=hT[:, kc, sl], start=(kc == 0),
                                                  stop=(kc == 7)), reads=[k1, ('hT', tb)], writes=[bk1], inc=(kc == 7))
                for kc in range(8):
                    S.op('pe', lambda e: e.matmul(b2[:, :], lhsT=wg2[:, kc, :], rhs=hT[:, kc, sl], start=(kc == 0),
                                                  stop=(kc == 7)), reads=[k2, ('hT', tb)], writes=[bk2], inc=(kc == 7))
                for kc in range(4):
                    S.op('pe', lambda e: e.matmul(b3[:, :], lhsT=wp1[:, kc, :], rhs=yhy[:, kc, sl], start=(kc == 0),
                                                  stop=(kc == 3)), reads=[k3, 'yhy'], writes=[bk3], inc=(kc == 3))
                for kc in range(4):
                    S.op('pe', lambda e: e.matmul(b4[:, :], lhsT=wp2[:, kc, :], rhs=yatt[:, kc, sl], start=(kc == 0),
                                                  stop=(kc == 3)), reads=[k4, 'yatt'], writes=[bk4], inc=(kc == 3))
                S.op('act', lambda e: e.activation(out=s1, in_=b1[:, :], func=AF.Sigmoid), writes=[bk1, ('s1', tk)])
                S.op('act', lambda e: e.activation(out=s2, in_=b2[:, :], func=AF.Sigmoid), writes=[bk2, ('s2', tk)])
                S.op('dve', lambda e: e.tensor_tensor(out=m1, in0=b3[:, :], in1=s1, op=ALU.mult),
                     reads=[('s1', tk)], writes=[bk3, ('m1', tk)])
                S.op('dve', lambda e: e.tensor_tensor(out=m2, in0=b4[:, :], in1=s2, op=ALU.mult),
                     reads=[('s2', tk)], writes=[bk4, ('m2', tk)])
                S.op('dve', lambda e: e.tensor_tensor(out=merged[:, fc, sl], in0=m1, in1=m2, op=ALU.add),
                     reads=[('m1', tk), ('m2', tk)], writes=[('merged', tb)])
        self.prefetch_wo(l)
        S.barrier_all()
        xT = self.xT()
        for c in range(8):
            S.dma('sp', self.fv(O_X + c * T, T), self.xs[:, c * T:(c + 1) * T], reads=[('xs', c)],
                  writes=[('xT', c, tb) for tb in range(4)])
        wos = [self.take(('wo', l, j), lambda: self.load_w(self.w["w_o"][l][:, j * 256:(j + 1) * 256], 8, 256))
               for j in range(4)]
        gffn = self.fv(C_GFFN + 8 * l, 8)
        hT = self.hT()
        sq = [self.fv(O_TMP + 512 * i, 512) for i in range(2)]
        rs_ = [self.fv(O_TMP + 1024 + 512 * i, 512) for i in range(4)]
        def stats(tb):
            sl = slice(tb * 512, (tb + 1) * 512)
            sb, sbk = self.ps[4 + tb], ('ps', 4 + tb)
            for c in range(8):
                i = c % 2
                S.op('act', lambda e: e.activation(out=sq[i], in_=xT[:, c, sl], func=AF.Square),
                     reads=[('xT', c, tb)], writes=[('sq', i)])
                S.op('pe', lambda e: e.matmul(sb[:, :], lhsT=self.onf, rhs=sq[i], start=(c == 0), stop=(c == 7)),
                     reads=[('sq', i), 'consts'], writes=[sbk])

        self.bank_rng = (0, 4)
        for tb in range(4):
            sl = slice(tb * 512, (tb + 1) * 512)
            for oc in range(8):
                wo, ko = wos[oc // 2]
                b, bk = self.bank()
                for kc in range(8):
                    S.op('pe', lambda e: e.matmul(b[:, :], lhsT=wo[:, kc, (oc % 2) * 128:(oc % 2) * 128 + 128],
                                                  rhs=merged[:, kc, sl], start=(kc == 0), stop=(kc == 7)),
                         reads=[ko, ('merged', tb)], writes=[bk], inc=(kc == 7))
                S.op('dve', lambda e: e.tensor_tensor(out=xT[:, oc, sl], in0=b[:, :], in1=xT[:, oc, sl], op=ALU.add),
                     writes=[bk, ('xT', oc, tb)])
            if tb >= 1:
                stats(tb - 1)
        stats(3)
        self.bank_rng = (0, 8)
        for tb in range(4):
            sb, sbk = self.ps[4 + tb], ('ps', 4 + tb)
            S.op('act', lambda e: e.activation(out=rs_[tb], in_=sb[:, :], func=AF.Ln, bias=self.eps, scale=1.0 / D),
                 reads=['consts'], writes=[sbk, ('rs', tb)])
        for tb in range(4):
            S.op('act', lambda e: e.activation(out=rs_[tb], in_=rs_[tb], func=AF.Exp, scale=-0.5),
                 writes=[('rs', tb)])
        for tb in range(4):
            sl = slice(tb * 512, (tb + 1) * 512)
            for c in range(8):
                S.op('dve', lambda e: e.scalar_tensor_tensor(out=hT[:, c, sl], in0=xT[:, c, sl], scalar=gffn[:, c:c + 1],
                                                             in1=rs_[tb], op0=ALU.mult, op1=ALU.mult),
                     reads=[('xT', c, tb), ('rs', tb), 'consts'], writes=[('hT', tb)])
        self.prefetch_ffn(l)

    def ffn(self, l):
        S = self.S
        xT, hT = self.xT(), self.hT()
        gffn = self.fv(C_GFFN + 8 * l, 8)
        a_ = [self.bv(O_TMP + 3072 + 1024 * i, 1024).rearrange("p (f t) -> p f t", t=512) for i in range(2)]
        r_ = [self.fv(O_TMP + 5120 + 512 * i, 512) for i in range(2)]
        W1 = self.w["w_ff1"][l]
        W2 = self.w["w_ff2"][l]
        S.op('dve', lambda e: e.memset(r_[0][0:1, 0:2], 0.0),
             writes=[('r', 0), ('r', 1)] + [('merged', t) for t in range(4)])
        ri = [0]
        wts = {}

        def get_w1(fg):
            if ('w1', fg) not in wts:
                wts[('w1', fg)] = self.take(('w1', l, fg), lambda: [
                    self.load_w(W1[:, fg * 512 + 256 * i: fg * 512 + 256 * i + 256], 8, 256) for i in range(2)])
            return wts[('w1', fg)]

        def get_w2(fg):
            if ('w2', fg) not in wts:
                wts[('w2', fg)] = [self.load_w(W2[fg * 512:(fg + 1) * 512, 512 * i: 512 * i + 512], 4, 512)
                                   for i in range(2)]
            return wts[('w2', fg)]

        items = [(fg, tb) for fg in range(8) for tb in range(4)]

        def stage1(k):
            fg, tb = items[k]
            sl = slice(tb * 512, (tb + 1) * 512)
            w1t = get_w1(fg)
            if tb == 0 and fg == 0:
                get_w2(fg)
            if tb == 2 and fg + 1 < 8:
                get_w1(fg + 1)
            if tb == 3 and fg + 1 < 8:
                get_w2(fg + 1)
            a = a_[k % 2]
            for f in range(4):
                wt, wk = w1t[f // 2]
                b, bk = self.bank()
                rr = r_[ri[0] % 2]
                rk = ('r', ri[0] % 2)
                ri[0] += 1
                for kc in range(8):
                    S.op('pe', lambda e: e.matmul(b[:, :], lhsT=wt[:, kc, (f % 2) * 128:(f % 2) * 128 + 128],
                                                  rhs=hT[:, kc, sl], start=(kc == 0), stop=(kc == 7)),
                         reads=[wk, ('hT', tb)], writes=[bk], inc=(kc == 7))
                S.op('act', lambda e: e.activation(out=rr, in_=b[:, :], func=AF.Relu), writes=[bk, rk])
                S.op('dve', lambda e: e.tensor_tensor(out=a[:, f, :], in0=rr, in1=rr, op=ALU.mult),
                     reads=[rk], writes=[('a', k % 2)])

        def stage2(k):
            fg, tb = items[k]
            sl = slice(tb * 512, (tb + 1) * 512)
            w2t = get_w2(fg)
            a = a_[k % 2]
            for dc in range(8):
                wt, wk = w2t[dc // 4]
                b, bk = self.bank()
                for f in range(4):
                    S.op('pe', lambda e: e.matmul(b[:, :], lhsT=wt[:, f, (dc % 4) * 128:(dc % 4) * 128 + 128],
                                                  rhs=a[:, f, :], start=(f == 0), stop=(f == 3)),
                         reads=[wk, ('a', k % 2)], writes=[bk], inc=(f == 3))
                S.op('dve', lambda e: e.tensor_tensor(out=xT[:, dc, sl], in0=b[:, :], in1=xT[:, dc, sl],
                                                      op=ALU.add), writes=[bk, ('xT', dc, tb)])

        stage1(0)
        for k in range(len(items)):
            if k + 1 < len(items):
                stage1(k + 1)
            stage2(k)
        S.barrier_all()

    def final(self, s):
        S = self.S
        xT = self.xT()
        gfin = self.fv(C_GFIN, 8)
        tmp = O_TMP
        sq = [self.fv(tmp + 512 * i, 512) for i in range(2)]
        rst = [self.fv(tmp + 1024 + 512 * i, 512) for i in range(4)]
        yn_ = [self.fv(tmp + 3072 + 4096 * i, 4096).rearrange("p (c t) -> p c t", t=512) for i in range(2)]
        yo = [self.fv(tmp + 11264 + 1024 * i, 1024) for i in range(2)]
        oi = [0]

        def stats_all():
            banks = []
            for tb in range(4):
                sl = slice(tb * 512, (tb + 1) * 512)
                b, bk = self.bank(0, 4)
                banks.append((b, bk))
                for c in range(8):
                    i = c % 2
                    S.op('act', lambda e: e.activation(out=sq[i], in_=xT[:, c, sl], func=AF.Square),
                         reads=[('xT', c, tb)], writes=[('sq', i)])
                    S.op('pe', lambda e: e.matmul(b[:, :], lhsT=self.onf, rhs=sq[i], start=(c == 0), stop=(c == 7)),
                         reads=[('sq', i), 'consts'], writes=[bk])
            for tb in range(4):
                b, bk = banks[tb]
                S.op('act', lambda e: e.activation(out=rst[tb], in_=b[:, :], func=AF.Ln, bias=self.eps,
                                                   scale=1.0 / D), reads=['consts'], writes=[bk, ('rs', tb)])
            for tb in range(4):
                S.op('act', lambda e: e.activation(out=rst[tb], in_=rst[tb], func=AF.Exp, scale=-0.5),
                     writes=[('rs', tb)])

        def scale(tb):
            sl = slice(tb * 512, (tb + 1) * 512)
            yn = yn_[tb % 2]
            ynk = ('yn', tb % 2)
            for c in range(8):
                S.op('dve', lambda e: e.scalar_tensor_tensor(out=yn[:, c, :], in0=xT[:, c, sl], scalar=gfin[:, c:c + 1],
                                                             in1=rst[tb], op0=ALU.mult, op1=ALU.mult),
                     reads=[('xT', c, tb), ('rs', tb), 'consts'], writes=[ynk])

        def store(tb):
            yn = yn_[tb % 2]
            ynk = ('yn', tb % 2)
            for a in range(4):
                y = yo[oi[0] % 2]
                yk = ('yo', oi[0] % 2)
                oi[0] += 1
                for half in range(2):
                    b2, bk2 = self.bank()
                    for cc in range(4):
                        c = half * 4 + cc
                        S.op('pe', lambda e: e.transpose(out=b2[:, cc * 128:(cc + 1) * 128],
                                                         in_=yn[:, c, a * 128:(a + 1) * 128], identity=self.idf),
                             reads=[ynk, 'consts'], writes=[bk2], inc=(cc == 3))
                    S.op('act', lambda e: e.activation(out=y[:, half * 512:(half + 1) * 512], in_=b2[:, :],
                                                       func=AF.Copy), writes=[bk2, yk])
                t0 = tb * 512 + a * 128
                S.dma('act', self.out[s, t0:t0 + 128, :], y, reads=[yk], writes=['out'])

        stats_all()
        scale(0)
        self.bank_rng = (4, 8)
        for tb in range(4):
            if tb + 1 < 4:
                scale(tb + 1)
            store(tb)
        self.bank_rng = (0, 8)
        S.barrier_all()

    def build(self):
        S = self.S
        self.load_consts()
        for l in range(self.depth):
            self.filter_prologue(l)
        for s in range(self.nseq):
            self.load_x(s)
            for l in range(self.depth):
                self.norm(AW - 3072, self.fv(C_GMIX + 8 * l, 8))
                self.prefetch_hyena(l)
                self.spill_x()
                self.hyena(l)
                self.attention(l)
                self.merge(l)
                self.ffn(l)
            self.final(s)
        S.barrier_all()


_CONSTS = None


def make_consts():
    global _CONSTS
    if _CONSTS is not None:
        return _CONSTS
    bf = ml_dtypes.bfloat16
    c = {}
    ident = np.eye(128, dtype=np.float32)
    c["c_f32"] = np.concatenate([ident, np.ones((128, 128), np.float32)], axis=1)
    rot = np.zeros((128, 128), np.float32)
    for hb in (0, 64):
        for e in range(32):
            rot[hb + e + 32, hb + e] = -1.0
            rot[hb + e, hb + e + 32] = 1.0
    k = np.arange(128)[:, None]
    q = np.arange(128)[None, :]
    mask = np.concatenate([(k <= q), (k >= q), (np.abs(k - q) <= 64)], axis=1).astype(np.float32)
    c["c_bf"] = np.concatenate([ident, np.ones((128, 64), np.float32), rot, mask], axis=1).astype(bf)
    half = 32
    inv_freq = (10000.0 ** (-np.arange(half, dtype=np.float32) / half)).astype(np.float32)
    pos = np.arange(T, dtype=np.float32)
    ang = (pos[None, :] * inv_freq[:, None]).astype(np.float32)
    ang128 = np.tile(ang, (4, 1))
    c["c_cs"] = np.concatenate([np.cos(ang128), np.sin(ang128)], axis=1).astype(np.float32)
    a = np.arange(T, dtype=np.int64)[:, None]
    f = np.arange(T, dtype=np.int64)[None, :]
    ph = ((a * f) % NFFT).astype(np.float64) * (2 * np.pi / NFFT)
    M = np.concatenate([np.cos(ph), np.sin(ph)], axis=1)
    M[:, 2048] = np.where(np.arange(T) % 2 == 0, 1.0, -1.0)
    M = M.astype(np.float32)
    mf = M.reshape(16, 128, 32, 128).transpose(2, 1, 0, 3).reshape(32, 128, 2048)
    c["c_mf"] = np.ascontiguousarray(mf).astype(bf)
    MT = M.T.reshape(8, 4, 128, 4, 512).transpose(3, 0, 2, 1, 4).reshape(4, 8, 128, 2048)
    c["c_mi"] = np.ascontiguousarray(MT).astype(bf)
    L = T
    nn = np.arange(L, dtype=np.float32)
    t = (nn / np.float32(L - 1)).astype(np.float32)
    bands = np.linspace(1e-4, 15, 16, dtype=np.float32)
    angf = (np.float32(2.0 * math.pi / L) * nn[:, None] * bands[None, :]).astype(np.float32)
    z = np.concatenate([t[:, None], np.cos(angf), -np.sin(angf)], axis=-1).astype(np.float32)
    c["c_zt"] = np.ascontiguousarray(z.T)
    max_decay = math.log(1e-2) / 0.3
    min_decay = math.log(1e-2) / 1.5
    deltas = np.abs(np.linspace(min_decay, max_decay, HW, dtype=np.float32))
    window = np.exp(-t[:, None] * deltas[None, :]).astype(np.float32)
    c["c_win"] = np.ascontiguousarray(window.reshape(16, 128, 512))
    sf = np.full((128, 16), 2.0 / NFFT, np.float32)
    sf[0, 0] = 1.0 / NFFT
    c["c_sf"] = sf
    _CONSTS = c
    return c


_NC_CACHE = {}


def get_nc(nseq, depth, dbg=None):
    key = (nseq, depth, tuple(sorted((dbg or {}).items())))
    if key not in _NC_CACHE:
        nc = bass.Bass("TRN2", target_bir_lowering=False)
        with ExitStack() as es:
            p = Prog(nc, es, nseq, depth, dbg)
            p.build()
        _NC_CACHE[key] = nc
    return _NC_CACHE[key]


WNAMES = ["norm_mix", "w_in", "conv_w", "conv_b", "filt_w1", "filt_b1", "filt_w_inner", "filt_b_inner",
          "filt_w_out", "filt_freq", "hy_skip", "p_hy", "p_att", "w_o", "norm_ffn", "w_ff1", "w_ff2", "norm_final"]


def kernel(**inputs):
    x = np.ascontiguousarray(np.asarray(inputs["x"], dtype=np.float32))
    consts = make_consts()
    base = {k: np.ascontiguousarray(np.asarray(inputs[k], dtype=np.float32)) for k in WNAMES}
    base.update(consts)
    nc = get_nc(NSEQ, DEPTH)
    in_maps = []
    for c in range(NCORE):
        m = dict(base)
        m["x"] = x[c * NSEQ:(c + 1) * NSEQ]
        in_maps.append(m)
    res = run_bass_kernel_spmd(nc, in_maps, core_ids=list(range(NCORE)))
    out = np.concatenate([res.results[c]["out"] for c in range(NCORE)], axis=0)
    return out.astype(np.float32)
```

```python
import math
from contextlib import ExitStack

import numpy as np
import ml_dtypes
import concourse.bass as bass
import concourse.mybir as mybir
from concourse.bass_utils import run_bass_kernel_spmd

F32 = mybir.dt.float32
BF16 = mybir.dt.bfloat16
AF = mybir.ActivationFunctionType
ALU = mybir.AluOpType

D = 1024
T = 2048
DEPTH = 2
NCORE = 8
NSEQ = 2
INW = 8192
HW = 512
DFF = 4096
NFFT = 4096
GROUPS = ((128, 1), (512, 4), (2048, 16))
EPS = 1e-6

AW = 52224
C_IDF, C_ONF, C_IDB, C_EPS = 0, 128, 256, 640
C_GMIX, C_GFFN, C_GFIN, C_CONV, C_SF, C_SKIP = 648, 664, 680, 688, 784, 800
O_STG = 2048
O_WB = 6144
NWB = 8
O_HT = 14336
O_X = 22528
O_U = 26624
O_TMP = 38912


class Sched:
    NDS = 24

    def __init__(self, nc, es):
        self.nc = nc
        self.eng = {'pe': nc.tensor, 'dve': nc.vector, 'act': nc.scalar, 'pool': nc.gpsimd, 'sp': nc.sync}
        self.sem = {e: es.enter_context(nc.semaphore('s_' + e)) for e in ('pe', 'dve', 'act', 'pool')}
        self.cnt = {e: 0 for e in self.sem}
        self.dsem = {q: [es.enter_context(nc.semaphore('d_%s_%d' % (q, i))) for i in range(self.NDS)]
                     for q in ('sp', 'act')}
        self.dcnt = {q: 0 for q in self.dsem}
        self.waited = {e: {} for e in self.eng}
        self.lastw = {}
        self.readers = {}

    def _wait(self, stream, tok):
        ename, sem, val = tok
        if ename == 'pe' and stream == 'pe':
            return
        w = self.waited[stream]
        k = id(sem)
        if w.get(k, 0) >= val:
            return
        w[k] = val
        self.eng[stream].wait_ge(sem, val)

    def _deps(self, stream, reads, writes):
        for k in reads:
            t = self.lastw.get(k)
            if t is not None:
                self._wait(stream, t)
        for k in writes:
            t = self.lastw.get(k)
            if t is not None:
                self._wait(stream, t)
            for t in self.readers.get(k, {}).values():
                self._wait(stream, t)

    def _commit(self, tok, reads, writes):
        for k in reads:
            if k not in writes:
                r = self.readers.setdefault(k, {})
                o = r.get(id(tok[1]))
                if o is None or o[2] < tok[2]:
                    r[id(tok[1])] = tok
        for k in writes:
            self.lastw[k] = tok
            self.readers[k] = {}

    def op(self, e, fn, reads=(), writes=(), inc=True):
        self._deps(e, reads, writes)
        ins = fn(self.eng[e])
        if inc:
            ins.then_inc(self.sem[e], 1)
            self.cnt[e] += 1
            tok = (e, self.sem[e], self.cnt[e])
        else:
            tok = (e, self.sem[e], self.cnt[e] + 1)
        self._commit(tok, reads, writes)
        return tok

    def dma(self, q, out, in_, reads=(), writes=(), **kw):
        j = self.dcnt[q]
        self.dcnt[q] += 1
        sem = self.dsem[q][j % self.NDS]
        rnd = j // self.NDS
        if rnd > 0:
            self._wait(q, ('dma', sem, 16 * rnd))
        self._deps(q, reads, writes)
        self.eng[q].dma_start(out=out, in_=in_, **kw).then_inc(sem, 16)
        tok = ('dma', sem, 16 * (rnd + 1))
        self._commit(tok, reads, writes)
        return tok

    def barrier_all(self):
        toks = []
        for e in self.sem:
            if self.cnt[e]:
                toks.append((e, self.sem[e], self.cnt[e]))
        for q in self.dsem:
            n = self.dcnt[q]
            for i in range(min(n, self.NDS)):
                last_j = ((n - 1 - i) // self.NDS) * self.NDS + i
                toks.append(('dma', self.dsem[q][i], 16 * (last_j // self.NDS + 1)))
        for s in self.eng:
            for t in toks:
                if t[0] == s and s != 'pe':
                    pass
                self._wait(s, t)


def perm_view(ap2d, d, j0, ln):
    if d == 1:
        return ap2d[:, j0:j0 + ln]
    n = T // d
    v = ap2d.rearrange("p (m r) -> p r m", r=d)
    if ln <= n:
        r = j0 // n
        m0 = j0 % n
        assert m0 + ln <= n
        return v[:, r, m0:m0 + ln]
    assert j0 % n == 0 and ln % n == 0
    return v[:, j0 // n:j0 // n + ln // n, :]


def like(ap_contig, ref):
    if len(ref.shape) == 3:
        return ap_contig.rearrange("p (a b) -> p a b", b=ref.shape[2])
    return ap_contig


class Prog:
    def __init__(self, nc, es, nseq, depth, dbg=None):
        self.nc = nc
        self.nseq = nseq
        self.depth = depth
        self.dbg = dbg or {}
        S = self.S = Sched(nc, es)
        dt = nc.dram_tensor
        self.x = dt("x", [nseq, T, D], F32, kind="ExternalInput").ap()
        self.w = {}
        for name, shp in [("norm_mix", [DEPTH, D]), ("w_in", [DEPTH, D, INW]), ("conv_w", [DEPTH, 3, 1536]),
                          ("conv_b", [DEPTH, 1536]), ("filt_w1", [DEPTH, 33, 64]), ("filt_b1", [DEPTH, 64]),
                          ("filt_w_inner", [DEPTH, 2, 64, 64]), ("filt_b_inner", [DEPTH, 2, 64]),
                          ("filt_w_out", [DEPTH, 64, 1024]), ("filt_freq", [DEPTH, 64]), ("hy_skip", [DEPTH, 512]),
                          ("p_hy", [DEPTH, 512, D]), ("p_att", [DEPTH, 512, D]), ("w_o", [DEPTH, D, D]),
                          ("norm_ffn", [DEPTH, D]), ("w_ff1", [DEPTH, D, DFF]), ("w_ff2", [DEPTH, DFF, D]),
                          ("norm_final", [D])]:
            self.w[name] = dt(name, shp, F32, kind="ExternalInput").ap()
        self.c_f32 = dt("c_f32", [128, 256], F32, kind="ExternalInput").ap()
        self.c_bf = dt("c_bf", [128, 128 + 128 + 128 + 384], BF16, kind="ExternalInput").ap()
        self.c_cs = dt("c_cs", [128, 2 * T], F32, kind="ExternalInput").ap()
        self.c_mf = dt("c_mf", [32, 128, 2048], BF16, kind="ExternalInput").ap()
        self.c_mi = dt("c_mi", [4, 8, 128, 2048], BF16, kind="ExternalInput").ap()
        self.c_zt = dt("c_zt", [33, T], F32, kind="ExternalInput").ap()
        self.c_win = dt("c_win", [16, 128, 512], F32, kind="ExternalInput").ap()
        self.c_sf = dt("c_sf", [128, 16], F32, kind="ExternalInput").ap()
        self.out = dt("out", [nseq, T, D], F32, kind="ExternalOutput").ap()
        self.kspec = dt("kspec", [DEPTH, 16, 128, 1024], F32, kind="Internal").ap()
        self.xs = dt("xs", [128, 8 * T], F32, kind="Internal").ap()
        self.dbg_t = {}
        for name, shp in self.dbg.items():
            self.dbg_t[name] = dt("dbg_" + name, list(shp), F32, kind="ExternalOutput").ap()
        self.A = es.enter_context(nc.sbuf_tensor("arena", [128, AW], F32))
        self.ps = [es.enter_context(nc.psum_tensor("ps%d" % i, [128, 512], F32)) for i in range(8)]
        self.psr = {}
        self.pre = {}
        self.bank_rng = (0, 8)
        self.wbi = 0
        self.stgi = 0
        self.uid = 0

    def fv(self, off, n, parts=128):
        return self.A[0:parts, off:off + n]

    def bv(self, off, nwords, parts=128):
        return self.A[0:parts, off:off + nwords].bitcast(BF16)

    def bank(self, lo=None, hi=None):
        if lo is None:
            lo, hi = self.bank_rng
        k = (lo, hi)
        i = self.psr.get(k, lo)
        self.psr[k] = lo + (i + 1 - lo) % (hi - lo)
        return self.ps[i], ('ps', i)

    def key(self, name):
        self.uid += 1
        return (name, self.uid)

    def load_w(self, src, kcs, ncols, scale=None, ceng='pool'):
        S = self.S
        si = self.stgi
        self.stgi = (self.stgi + 1) % 2
        wi = self.wbi
        self.wbi = (self.wbi + 1) % NWB
        n = kcs * ncols
        assert n <= 2048
        stg = self.fv(O_STG + si * 2048, n).rearrange("p (k n) -> p k n", n=ncols)
        wb = self.bv(O_WB + wi * 1024, n // 2).rearrange("p (k n) -> p k n", n=ncols)
        S.dma('sp', stg, src.rearrange("(k p) n -> p k n", p=128), writes=[('stg', si)])
        if scale is None:
            if ceng == 'act':
                S.op('act', lambda e: e.activation(out=wb, in_=stg, func=AF.Copy), reads=[('stg', si)],
                     writes=[('wb', wi)])
            else:
                S.op(ceng, lambda e: e.tensor_copy(out=wb, in_=stg), reads=[('stg', si)], writes=[('wb', wi)])
        else:
            for k in range(kcs):
                S.op('pool', lambda e, k=k: e.tensor_scalar(out=wb[:, k, :], in0=stg[:, k, :],
                                                             scalar1=scale[:, k:k + 1], scalar2=None, op0=ALU.mult),
                     reads=[('stg', si)], writes=[('wb', wi)])
        return wb, ('wb', wi)

    def load_w_dma(self, src, kcs, ncols):
        si = self.stgi
        self.stgi = (self.stgi + 1) % 2
        n = kcs * ncols
        stg = self.fv(O_STG + si * 2048, n).rearrange("p (k n) -> p k n", n=ncols)
        self.S.dma('sp', stg, src.rearrange("(k p) n -> p k n", p=128), writes=[('stg', si)])
        return (stg, si, n, ncols)

    def load_w_cast(self, h, ceng='act'):
        stg, si, n, ncols = h
        wi = self.wbi
        self.wbi = (self.wbi + 1) % NWB
        wb = self.bv(O_WB + wi * 1024, n // 2).rearrange("p (k n) -> p k n", n=ncols)
        if ceng == 'act':
            self.S.op('act', lambda e: e.activation(out=wb, in_=stg, func=AF.Copy), reads=[('stg', si)],
                      writes=[('wb', wi)])
        else:
            self.S.op(ceng, lambda e: e.tensor_copy(out=wb, in_=stg), reads=[('stg', si)], writes=[('wb', wi)])
        return wb, ('wb', wi)

    def pf(self, tag, loader):
        if tag not in self.pre:
            self.pre[tag] = loader()

    def take(self, tag, loader):
        if tag in self.pre:
            return self.pre.pop(tag)
        return loader()

    def load_bf(self, src, shape3):
        S = self.S
        wi = self.wbi
        self.wbi = (self.wbi + 1) % NWB
        a, b = shape3
        wb = self.bv(O_WB + wi * 1024, a * b // 2)
        S.dma('sp', wb, src, writes=[('wb', wi)])
        return wb.rearrange("p (a b) -> p a b", b=b), ('wb', wi)

    def dump(self, name, src_ap, reads):
        if name in self.dbg_t:
            self.S.dma('sp', self.dbg_t[name], src_ap, reads=reads, writes=[('dbg', name)])

    def load_consts(self):
        S = self.S
        A = self.A
        S.dma('sp', self.fv(C_IDF, 256), self.c_f32, writes=['consts'])
        S.dma('sp', self.bv(C_IDB, 384), self.c_bf, writes=['consts'])
        S.dma('sp', self.fv(C_SF, 16), self.c_sf, writes=['consts'])
        S.op('pool', lambda e: e.memset(self.fv(C_EPS, 1), EPS), writes=['consts'])
        with self.nc.allow_non_contiguous_dma(reason="tiny param loads"):
            for l in range(DEPTH):
                S.dma('sp', self.fv(C_GMIX + 8 * l, 8), self.w["norm_mix"][l].rearrange("(k p) -> p k", p=128),
                      writes=['consts'])
                S.dma('sp', self.fv(C_GFFN + 8 * l, 8), self.w["norm_ffn"][l].rearrange("(k p) -> p k", p=128),
                      writes=['consts'])
                cv = self.fv(C_CONV + 48 * l, 48).rearrange("p (c t) -> p c t", t=4)
                for t in range(3):
                    S.dma('sp', cv[:, :, t], self.w["conv_w"][l, t].rearrange("(c p) -> p c", p=128),
                          writes=['consts'])
                S.dma('sp', cv[:, :, 3], self.w["conv_b"][l].rearrange("(c p) -> p c", p=128), writes=['consts'])
            S.dma('sp', self.fv(C_GFIN, 8), self.w["norm_final"].rearrange("(k p) -> p k", p=128), writes=['consts'])
        for l in range(DEPTH):
            S.dma('sp', self.fv(C_SKIP + 512 * l, 512, parts=1), self.w["hy_skip"][l:l + 1, :], writes=['consts'])
        self.idf = self.fv(C_IDF, 128)
        self.onf = self.fv(C_ONF, 128)
        cb = self.bv(C_IDB, 384)
        self.idb = cb[:, 0:128]
        self.onb = cb[:, 128:192]
        self.onb128 = cb[:, 128:256]
        self.rot = cb[:, 256:384]
        self.mask = cb[:, 384:768]
        self.eps = self.fv(C_EPS, 1)
        S.barrier_all()

    def filter_prologue(self, l):
        S = self.S
        W = self.w
        o = O_HT
        zt = self.fv(o, T, parts=33); o += T
        hA = self.fv(o, T, parts=64); o += T
        hB = self.fv(o, T, parts=64); o += T
        w1 = self.fv(o, 64, parts=33); o += 64
        wi0 = self.fv(o, 64, parts=64); o += 64
        wi1 = self.fv(o, 64, parts=64); o += 64
        wo = self.fv(o, 1024, parts=64); o += 1024
        sm = self.fv(o, 16, parts=64); o += 16
        wrps = [self.fv(o + 512 * i, 512, parts=64) for i in range(2)]; o += 1024
        hf = [self.fv(o + 512 * i, 512) for i in range(2)]; o += 1024
        hb = [self.fv(o + 512 * i, 512) for i in range(2)]; o += 1024
        wn = [self.fv(o + 512 * i, 512) for i in range(2)]; o += 1024
        hsum = self.bv(o, 4096).rearrange("p (a b) -> p a b", b=512); o += 4096
        hdif = self.bv(o, 4096).rearrange("p (a b) -> p a b", b=512); o += 4096
        kout = [self.fv(o + 1024 * i, 1024) for i in range(2)]; o += 2048
        assert o <= AW
        S.dma('sp', zt, self.c_zt, writes=['f_zt'])
        S.dma('sp', w1, W["filt_w1"][l], writes=['f_w'])
        S.dma('sp', wi0, W["filt_w_inner"][l, 0], writes=['f_w'])
        S.dma('sp', wi1, W["filt_w_inner"][l, 1], writes=['f_w'])
        S.dma('sp', wo, W["filt_w_out"][l], writes=['f_w'])
        with self.nc.allow_non_contiguous_dma(reason="tiny param loads"):
            S.dma('sp', sm[:, 0:1], W["filt_b1"][l].rearrange("(p o) -> p o", o=1), writes=['f_sm'])
            S.dma('sp', sm[:, 1:3], W["filt_b_inner"][l].rearrange("i p -> p i"), writes=['f_sm'])
            S.dma('sp', sm[:, 3:4], W["filt_freq"][l].rearrange("(p o) -> p o", o=1), writes=['f_sm'])
        for i in range(3):
            S.op('dve', lambda e, i=i: e.tensor_tensor(out=sm[:, 4 + i:5 + i], in0=sm[:, i:i + 1], in1=sm[:, 3:4],
                                                      op=ALU.mult), reads=['f_sm'], writes=['f_sm'])
        lay = [(w1, 33, zt, 'f_zt'), (wi0, 64, hA, 'hA'), (wi1, 64, hB, 'hB')]
        outs = [(hA, 'hA'), (hB, 'hB'), (hA, 'hA')]
        wi_ = 0
        for li, (wt, kk, src, skey) in enumerate(lay):
            dst, dkey = outs[li]
            for tb in range(4):
                sl = slice(tb * 512, (tb + 1) * 512)
                b, bk = self.bank()
                rk = [(skey, tb)] if li > 0 else ['f_zt']
                S.op('pe', lambda e: e.matmul(b[0:64, :], lhsT=wt[0:kk, :], rhs=src[0:kk, sl], start=True, stop=True),
                     reads=['f_w'] + rk, writes=[bk])
                S.op('dve', lambda e: e.tensor_scalar(out=dst[:, sl], in0=b[0:64, :], scalar1=sm[:, 3:4],
                                                      scalar2=sm[:, 4 + li:5 + li], op0=ALU.mult, op1=ALU.add),
                     reads=['f_sm'], writes=[bk, (dkey, tb)])
                for cmp_op, thr, shift in ((ALU.is_gt, math.pi, -2 * math.pi), (ALU.is_lt, -math.pi, 2 * math.pi)):
                    wr = wrps[wi_ % 2]
                    wkey = ('wrp', wi_ % 2)
                    wi_ += 1
                    S.op('dve', lambda e: e.tensor_scalar(out=wr[:, :], in0=dst[:, sl], scalar1=thr, scalar2=shift,
                                                          op0=cmp_op, op1=ALU.mult),
                         reads=[(dkey, tb)], writes=[wkey])
                    S.op('dve', lambda e: e.tensor_tensor(out=dst[:, sl], in0=dst[:, sl], in1=wr[:, :], op=ALU.add),
                         reads=[wkey], writes=[(dkey, tb)])
                S.op('act', lambda e: e.activation(out=dst[:, sl], in_=dst[:, sl], func=AF.Sin),
                     writes=[(dkey, tb)])
        hid = hA
        skip = self.fv(C_SKIP + 512 * l, 512, parts=1)
        for mc in range(16):
            i = mc % 2
            S.dma('sp', wn[i], self.c_win[mc], writes=[('f_wn', i)])
            for half, dst in ((0, hf[i]), (1, hb[i])):
                b, bk = self.bank()
                S.op('pe', lambda e: e.matmul(b[:, :], lhsT=hid[:, mc * 128:(mc + 1) * 128],
                                              rhs=wo[:, half * 512:(half + 1) * 512], start=True, stop=True),
                     reads=['f_w', ('hA', mc // 4)], writes=[bk])
                S.op('dve', lambda e: e.tensor_tensor(out=dst, in0=b[:, :], in1=wn[i], op=ALU.mult),
                     reads=[('f_wn', i)], writes=[bk, ('f_h', half, i)])
            if mc == 0:
                S.op('dve', lambda e: e.memset(hb[i][0:1, :], 0.0), writes=[('f_h', 1, i)])
                S.op('dve', lambda e: e.tensor_tensor(out=hf[i][0:1, :], in0=hf[i][0:1, :], in1=skip, op=ALU.add),
                     writes=[('f_h', 0, i)])
            S.op('pool', lambda e: e.tensor_tensor(out=hsum[:, mc, :], in0=hf[i], in1=hb[i], op=ALU.add),
                 reads=[('f_h', 0, i), ('f_h', 1, i)], writes=['f_hs'])
            S.op('pool', lambda e: e.tensor_tensor(out=hdif[:, mc, :], in0=hb[i], in1=hf[i], op=ALU.subtract),
                 reads=[('f_h', 0, i), ('f_h', 1, i)], writes=['f_hs'])
        for fc in range(16):
            i = fc % 2
            ko = kout[i]
            mre, kre = self.load_bf(self.c_mf[fc], (16, 128))
            mim, kim = self.load_bf(self.c_mf[16 + fc], (16, 128))
            bre, bkre = self.bank()
            bim, bkim = self.bank()
            for mc in range(16):
                S.op('pe', lambda e: e.matmul(bre[:, :], lhsT=mre[:, mc, :], rhs=hsum[:, mc, :], start=(mc == 0),
                                              stop=(mc == 15)), reads=[kre, 'f_hs'], writes=[bkre], inc=(mc == 15))
            for mc in range(16):
                S.op('pe', lambda e: e.matmul(bim[:, :], lhsT=mim[:, mc, :], rhs=hdif[:, mc, :], start=(mc == 0),
                                              stop=(mc == 15)), reads=[kim, 'f_hs'], writes=[bkim], inc=(mc == 15))
            sf = self.fv(C_SF + fc, 1)
            S.op('act', lambda e: e.activation(out=ko[:, 0:512], in_=bre[:, :], func=AF.Identity, scale=sf),
                 reads=['consts'], writes=[bkre, ('f_ko', i)])
            S.op('act', lambda e: e.activation(out=ko[:, 512:1024], in_=bim[:, :], func=AF.Identity, scale=sf),
                 reads=['consts'], writes=[bkim, ('f_ko', i)])
            if fc == 0:
                bn, bkn = self.bank()
                for mc in range(16):
                    S.op('pe', lambda e: e.matmul(bn[0:1, :], lhsT=mim[:, mc, 0:1], rhs=hsum[:, mc, :],
                                                  start=(mc == 0), stop=(mc == 15)),
                         reads=[kim, 'f_hs'], writes=[bkn], inc=(mc == 15))
                S.op('act', lambda e: e.activation(out=ko[0:1, 512:1024], in_=bn[0:1, :], func=AF.Identity,
                                                   scale=1.0 / NFFT), writes=[bkn, ('f_ko', i)])
            S.dma('act', self.kspec[l, fc], ko, reads=[('f_ko', i)], writes=['kspec'])
        S.barrier_all()

    def xT(self):
        return self.fv(O_X, 8 * T).rearrange("p (c t) -> p c t", t=T)

    def hT(self):
        return self.bv(O_HT, 8192).rearrange("p (c t) -> p c t", t=T)

    def load_x(self, s):
        S = self.S
        xT = self.xT()
        xin = self.fv(O_TMP, 4096).rearrange("p (a d) -> p a d", d=D)
        for tb in range(4):
            S.dma('sp', xin, self.x[s, tb * 512:(tb + 1) * 512, :].rearrange("(a p) d -> p a d", p=128),
                  writes=['xin'])
            for c in range(8):
                b, bk = self.bank()
                for a in range(4):
                    S.op('pe', lambda e: e.transpose(out=b[:, a * 128:(a + 1) * 128],
                                                     in_=xin[:, a, c * 128:(c + 1) * 128], identity=self.idf),
                         reads=['xin', 'consts'], writes=[bk], inc=(a == 3))
                S.op('act', lambda e: e.activation(out=xT[:, c, tb * 512:(tb + 1) * 512], in_=b[:, :], func=AF.Copy),
                     writes=[bk, ('xT', c, tb)])
        S.barrier_all()

    def norm(self, tmp_off, gain):
        S = self.S
        xT, hT = self.xT(), self.hT()
        sq = [self.bv(tmp_off + 512 * i, 256) for i in range(2)]
        rs_ = [self.fv(tmp_off + 1024 + 512 * i, 512) for i in range(4)]
        banks = []
        for tb in range(4):
            sl = slice(tb * 512, (tb + 1) * 512)
            b, bk = self.bank()
            banks.append((b, bk))
            for c in range(8):
                i = c % 2
                S.op('act', lambda e: e.activation(out=sq[i], in_=xT[:, c, sl], func=AF.Square),
                     reads=[('xT', c, tb)], writes=[('sq', i)])
                S.op('pe', lambda e: e.matmul(b[:, :], lhsT=self.onb128, rhs=sq[i], start=(c == 0), stop=(c == 7)),
                     reads=[('sq', i), 'consts'], writes=[bk])
        for tb in range(4):
            b, bk = banks[tb]
            S.op('act', lambda e: e.activation(out=rs_[tb], in_=b[:, :], func=AF.Ln, bias=self.eps, scale=1.0 / D),
                 reads=['consts'], writes=[bk, ('rs', tb)])
        for tb in range(4):
            S.op('act', lambda e: e.activation(out=rs_[tb], in_=rs_[tb], func=AF.Exp, scale=-0.5),
                 writes=[('rs', tb)])
        for tb in range(4):
            sl = slice(tb * 512, (tb + 1) * 512)
            for c in range(8):
                S.op('dve', lambda e: e.scalar_tensor_tensor(out=hT[:, c, sl], in0=xT[:, c, sl], scalar=gain[:, c:c + 1],
                                                             in1=rs_[tb], op0=ALU.mult, op1=ALU.mult),
                     reads=[('xT', c, tb), ('rs', tb), 'consts'], writes=[('hT', tb)])

    def spill_x(self):
        S = self.S
        for c in (2, 3, 4, 5, 6, 7, 0, 1):
            S.dma('sp', self.xs[:, c * T:(c + 1) * T], self.fv(O_X + c * T, T),
                  reads=[('xT', c, tb) for tb in range(4)], writes=[('xs', c)])

    def prefetch_hyena(self, l):
        W = self.w["w_in"][l]
        for j in range(2):
            self.pf(('hy', l, j), lambda: self.load_w(W[:, 256 * j:256 * j + 256], 8, 256))
        for j in range(2):
            self.pf(('hyA', l, j), lambda: self.load_w(W[:, 512 + 256 * j:512 + 256 * j + 256], 8, 256))
            self.pf(('hyB', l, j), lambda: self.load_w(W[:, 1024 + 256 * j:1024 + 256 * j + 256], 8, 256))

    def prefetch_attention(self, l):
        W = self.w["w_in"][l]
        base = 1536
        for which in range(3):
            self.pf(('att', l, 0, 0, which),
                    lambda: self.load_w(W[:, base + 512 * which: base + 512 * which + 128], 8, 128))

    def prefetch_merge(self, l):
        W = self.w["w_in"][l]
        self.pf(('mg', l, 0, 0), lambda: self.load_w(W[:, 6144: 6144 + 128], 8, 128))
        self.pf(('mg', l, 0, 1), lambda: self.load_w(W[:, 7168: 7168 + 128], 8, 128))
        self.pf(('mg', l, 0, 2), lambda: self.load_w(self.w["p_hy"][l][:, 0:128], 4, 128))
        self.pf(('mg', l, 0, 3), lambda: self.load_w(self.w["p_att"][l][:, 0:128], 4, 128))

    def prefetch_wo(self, l):
        for j in range(2):
            self.pf(('wo', l, j), lambda: self.load_w(self.w["w_o"][l][:, j * 256:(j + 1) * 256], 8, 256))

    def prefetch_ffn(self, l):
        W1 = self.w["w_ff1"][l]
        self.pf(('w1', l, 0), lambda: [self.load_w(W1[:, 256 * i: 256 * i + 256], 8, 256) for i in range(2)])

    def hyena(self, l):
        S = self.S
        hT = self.hT()
        W = self.w["w_in"][l]
        gmix = self.fv(C_GMIX + 8 * l, 8)
        cv = self.fv(C_CONV + 48 * l, 48).rearrange("p (c t) -> p c t", t=4)
        o = O_U
        x0 = self.fv(o, 4 * T).rearrange("p (c t) -> p c t", t=T); o += 4 * T
        u = self.bv(o, 4096).rearrange("p (c t) -> p c t", t=T); o += 4096
        o_b = o
        raws = [self.fv(o, T), self.fv(o + T, T)]; o += 2 * T
        rawi = [0]
        cA = self.fv(o, T); o += T
        cB = self.fv(o, T); o += T
        assert o <= AW

        def conv_chunk(wt, wk, col, ch, dst, dkey, extra=()):
            raw = raws[rawi[0] % 2]
            rkey = ('raw', rawi[0] % 2)
            rawi[0] += 1
            for tb in range(4):
                b, bk = self.bank()
                for kc in range(8):
                    S.op('pe', lambda e: e.matmul(b[:, :], lhsT=wt[:, kc, col:col + 128],
                                                  rhs=hT[:, kc, tb * 512:(tb + 1) * 512], start=(kc == 0),
                                                  stop=(kc == 7)), reads=[wk, ('hT', tb)], writes=[bk], inc=(kc == 7))
                S.op('act', lambda e: e.activation(out=raw[:, tb * 512:(tb + 1) * 512], in_=b[:, :], func=AF.Copy),
                     writes=[bk, rkey])
                S.op('act', lambda e: e.activation(out=dst[:, tb * 512:(tb + 1) * 512], in_=b[:, :], func=AF.Identity,
                                                   scale=cv[:, ch, 1:2], bias=cv[:, ch, 3:4]),
                     reads=['consts'], writes=[bk, dkey] + list(extra))
            S.op('dve', lambda e: e.scalar_tensor_tensor(out=dst[:, 1:T], in0=raw[:, 0:T - 1], scalar=cv[:, ch, 0:1],
                                                         in1=dst[:, 1:T], op0=ALU.mult, op1=ALU.add),
                 reads=[rkey, 'consts'], writes=[dkey])
            S.op('dve', lambda e: e.scalar_tensor_tensor(out=dst[:, 0:T - 1], in0=raw[:, 1:T], scalar=cv[:, ch, 2:3],
                                                         in1=dst[:, 0:T - 1], op0=ALU.mult, op1=ALU.add),
                 reads=[rkey, 'consts'], writes=[dkey])

        for j in range(2):
            wt, wk = self.take(('hy', l, j), lambda: self.load_w(W[:, 256 * j:256 * j + 256], 8, 256))
            for i in range(2):
                ch = 2 * j + i
                conv_chunk(wt, wk, 128 * i, ch, x0[:, ch, :], ('x0', ch), [('xT', 2 + ch, t) for t in range(4)])
        for j in range(2):
            wa, wak = self.take(('hyA', l, j), lambda: self.load_w(W[:, 512 + 256 * j:512 + 256 * j + 256], 8, 256))
            wb_, wbk = self.take(('hyB', l, j), lambda: self.load_w(W[:, 1024 + 256 * j:1024 + 256 * j + 256], 8, 256))
            for i in range(2):
                ch = 2 * j + i
                conv_chunk(wa, wak, 128 * i, 4 + ch, cA, 'cA')
                conv_chunk(wb_, wbk, 128 * i, 8 + ch, cB, 'cB')
                S.op('dve', lambda e: e.tensor_tensor(out=u[:, ch, :], in0=cA, in1=cB, op=ALU.mult),
                     reads=['cA', 'cB'], writes=[('u', ch)] + [('xT', 6 + ch // 2, t) for t in range(4)])
        uT = self.bv(O_U + 20480, 4096).rearrange("p (a c) -> p a c", c=512)
        Y = self.bv(O_U + 12288, 8192).rearrange("p (f c) -> p f c", c=512)
        ksp = [self.fv(O_U + 8192 + 1024 * i, 1024) for i in range(2)]
        tt_ = [self.fv(O_U + 10240 + 512 * i, 512) for i in range(4)]
        for tt in range(16):
            b, bk = self.bank()
            bb = b[:, :].bitcast(BF16)
            for cc in range(4):
                S.op('pe', lambda e: e.transpose(out=bb[:, cc * 128:(cc + 1) * 128],
                                                 in_=u[:, cc, tt * 128:(tt + 1) * 128], identity=self.idb),
                     reads=[('u', cc), 'consts'], writes=[bk], inc=(cc == 3))
            S.op('act', lambda e: e.activation(out=uT[:, tt, :], in_=bb[:, 0:512], func=AF.Copy),
                 writes=[bk, 'uT'])
        S.op('dve', lambda e: e.memset(tt_[0][0:1, 0:2], 0.0),
             writes=[('u', c_) for c_ in range(4)] + [('raw', 0), ('raw', 1), 'cA', 'cB', ('ksp', 0), ('ksp', 1),
                                                      't1', 't2', 't3', 't4', 'Y'])
        for fc in range(16):
            i = fc % 2
            mre, kre = self.load_bf(self.c_mf[fc], (16, 128))
            mim, kim = self.load_bf(self.c_mf[16 + fc], (16, 128))
            S.dma('sp', ksp[i], self.kspec[l, fc], reads=['kspec'], writes=[('ksp', i)])
            bre, bkre = self.bank()
            bim, bkim = self.bank()
            for mc in range(16):
                S.op('pe', lambda e: e.matmul(bre[:, :], lhsT=mre[:, mc, :], rhs=uT[:, mc, :], start=(mc == 0),
                                              stop=(mc == 15)), reads=[kre, 'uT'], writes=[bkre], inc=(mc == 15))
            for mc in range(16):
                S.op('pe', lambda e: e.matmul(bim[:, :], lhsT=mim[:, mc, :], rhs=uT[:, mc, :], start=(mc == 0),
                                              stop=(mc == 15)), reads=[kim, 'uT'], writes=[bkim], inc=(mc == 15))
            Ka = ksp[i][:, 0:512]
            Kb = ksp[i][:, 512:1024]
            t1, t2, t3, t4 = tt_
            S.op('dve', lambda e: e.tensor_tensor(out=t1, in0=bre[:, :], in1=Ka, op=ALU.mult),
                 reads=[('ksp', i)], writes=[bkre, 't1'])
            S.op('dve', lambda e: e.tensor_tensor(out=t2, in0=bim[:, :], in1=Kb, op=ALU.mult),
                 reads=[('ksp', i)], writes=[bkim, 't2'])
            S.op('dve', lambda e: e.tensor_tensor(out=t3, in0=bre[:, :], in1=Kb, op=ALU.mult),
                 reads=[('ksp', i)], writes=[bkre, 't3'])
            S.op('dve', lambda e: e.tensor_tensor(out=t4, in0=bim[:, :], in1=Ka, op=ALU.mult),
                 reads=[('ksp', i)], writes=[bkim, 't4'])
            S.op('dve', lambda e: e.tensor_tensor(out=Y[:, fc, :], in0=t1, in1=t2, op=ALU.add),
                 reads=['t1', 't2'], writes=['Y'])
            S.op('dve', lambda e: e.tensor_tensor(out=Y[:, 16 + fc, :], in0=t4, in1=t3, op=ALU.subtract),
                 reads=['t3', 't4'], writes=['Y'])
            if fc == 0:
                S.op('dve', lambda e: e.tensor_copy(out=Y[0:1, 0, :], in_=t1[0:1, :]), reads=['t1'], writes=['Y'])
                S.op('dve', lambda e: e.tensor_copy(out=Y[0:1, 16, :], in_=t2[0:1, :]), reads=['t2'], writes=['Y'])
        yhy = self.bv(O_X, 4096).rearrange("p (c t) -> p c t", t=T)
        for nb in range(4):
            banks = [self.bank() for _ in range(4)]
            for fg in range(8):
                mi, mik = self.load_bf(self.c_mi[nb, fg], (4, 512))
                for j in range(4):
                    fch = fg * 4 + j
                    for cc in range(4):
                        b, bk = banks[cc]
                        S.op('pe', lambda e: e.matmul(b[:, :], lhsT=Y[:, fch, cc * 128:(cc + 1) * 128],
                                                      rhs=mi[:, j, :], start=(fch == 0), stop=(fch == 31)),
                             reads=[mik, 'Y'], writes=[bk], inc=(fch == 31 or j == 3))
            for cc in range(4):
                b, bk = banks[cc]
                S.op('dve', lambda e: e.tensor_tensor(out=yhy[:, cc, nb * 512:(nb + 1) * 512], in0=b[:, :],
                                                      in1=x0[:, cc, nb * 512:(nb + 1) * 512], op=ALU.mult),
                     reads=[('x0', cc)], writes=[bk, 'yhy'] + [('xT', c_, t) for c_ in (0, 1) for t in range(4)])
        self.dump('yhy', self.fv(O_X, 4096), ['yhy'])
        self.prefetch_attention(l)
        S.barrier_all()

    def attention(self, l):
        S = self.S
        hT = self.hT()
        W = self.w["w_in"][l]
        gmix = self.fv(C_GMIX + 8 * l, 8)
        yatt = self.bv(O_U, 4096).rearrange("p (c t) -> p c t", t=T)
        o = O_U + 4096
        cs = self.fv(o, 2 * T); o += 2 * T
        cosT, sinT = cs[:, 0:T], cs[:, T:2 * T]
        accN = self.fv(o, T); o += T
        accD = self.fv(o, T); o += T
        qk = [[self.bv(o + 1024 * (2 * i + j), 1024) for j in range(2)] for i in range(2)]; o += 4096
        vch = [self.bv(o + 2112 * i, 2112).rearrange("p (n f) -> p n f", f=128) for i in range(2)]; o += 4224
        NPT = 6
        pT = [self.bv(o + 128 * i, 128) for i in range(NPT)]; o += 128 * NPT
        qb16 = [self.bv(o + 256 * i, 256) for i in range(2)]; o += 512
        t12 = [[self.fv(o + 512 * (2 * i + j), 512) for j in range(2)] for i in range(2)]; o += 2048
        etmp = [self.fv(o + 512 * i, 512) for i in range(2)]; o += 1024
        eti = [0]
        assert o <= AW, o
        self.bank_rng = (4, 8)
        S.dma('sp', cs, self.c_cs, writes=['cs'])
        it = 0
        pti = 0
        for hp in range(4):
            for g, (window, d) in enumerate(GROUPS):
                n = T // d
                par = it % 2
                it += 1
                qr, kr = qk[par]
                vc = vch[par]
                base = 1536 + g * 1536 + hp * 128
                qkw = [self.take(('att', l, hp, g, which), lambda: self.load_w(
                    W[:, base + 512 * which: base + 512 * which + 128], 8, 128)) for which in range(2)]
                qitems = [(which, tb) for which in range(2) for tb in range(4)]
                qst = {}

                def q_proj(k):
                    which, tb = qitems[k]
                    wt, wk = qkw[which]
                    tp = k % 2
                    j0 = tb * 512
                    bA, bkA = self.bank()
                    for kc in range(8):
                        S.op('pe', lambda e: e.matmul(bA[:, :], lhsT=wt[:, kc, :], rhs=hT[:, kc, j0:j0 + 512],
                                                      start=(kc == 0), stop=(kc == 7)),
                             reads=[wk, ('hT', tb)], writes=[bkA], inc=(kc == 7))
                    S.op('act', lambda e: e.activation(out=qb16[tp], in_=bA[:, :], func=AF.Copy),
                         writes=[bkA, ('qb16', tp)])
                    qst[k] = (bA, bkA)

                def q_rot(k):
                    which, tb = qitems[k]
                    dst = (qr, kr)[which]
                    tp = k % 2
                    j0 = tb * 512
                    bA, bkA = qst.pop(k)
                    bB, bkB = self.bank()
                    S.op('pe', lambda e: e.matmul(bB[:, :], lhsT=self.rot, rhs=qb16[tp], start=True, stop=True),
                         reads=[('qb16', tp), 'consts'], writes=[bkB])
                    t1, t2 = t12[tp]
                    S.op('dve', lambda e: e.tensor_tensor(out=t1, in0=bA[:, :], in1=cosT[:, j0:j0 + 512],
                                                          op=ALU.mult), reads=['cs'], writes=[bkA, ('t1', tp)])
                    S.op('dve', lambda e: e.tensor_tensor(out=t2, in0=bB[:, :], in1=sinT[:, j0:j0 + 512],
                                                          op=ALU.mult), reads=['cs'], writes=[bkB, ('t2', tp)])
                    if d == 1:
                        dv, a1, a2 = dst[:, j0:j0 + 512], t1, t2
                    else:
                        m0, ml = j0 // d, 512 // d
                        dv = dst.rearrange("p (r m) -> p r m", r=d)[:, :, m0:m0 + ml]
                        a1 = t1.rearrange("p (m r) -> p r m", r=d)
                        a2 = t2.rearrange("p (m r) -> p r m", r=d)
                    S.op('pool', lambda e: e.tensor_tensor(out=dv, in0=a1, in1=a2, op=ALU.add),
                         reads=[('t1', tp), ('t2', tp)], writes=[('qk', par, which)])

                q_proj(0)
                for k in range(len(qitems)):
                    if k + 1 < len(qitems):
                        q_proj(k + 1)
                    q_rot(k)
                wt, wk = self.take(('att', l, hp, g, 2), lambda: self.load_w(W[:, base + 1024: base + 1024 + 128], 8, 128))
                nj = n // 128 + 1
                chunks = []
                for r in range(d):
                    if n == 128:
                        chunks.append((r, -1, 0, 128, 0, len(chunks)))
                        continue
                    for j in range(nj):
                        k0 = max(0, 128 * j - 64)
                        k1 = min(n, 128 * j + 64)
                        pb = 64 if j == 0 else 0
                        chunks.append((r, j, k0, k1 - k0, pb, len(chunks)))
                def v_proj(lo, hi, rng):
                    self.bank_rng = rng
                    for c0 in range(lo, hi, 4):
                        b, bk = self.bank()
                        grp = chunks[c0:min(c0 + 4, hi)]
                        for gi, (r, j, k0, nk, pb, idx) in enumerate(grp):
                            for kc in range(8):
                                lhsT = perm_view(hT[:, kc, :], d, r * n + k0, nk)
                                S.op('pe', lambda e: e.matmul(b[pb:pb + nk, gi * 128:(gi + 1) * 128], lhsT=lhsT,
                                                              rhs=wt[:, kc, :], start=(kc == 0), stop=(kc == 7)),
                                     reads=[wk] + [('hT', t) for t in range(4)], writes=[bk],
                                     inc=(kc == 7 and gi == len(grp) - 1))
                        for gi, (r, j, k0, nk, pb, idx) in enumerate(grp):
                            S.op('act', lambda e: e.activation(out=vc[pb:pb + nk, idx, :],
                                                               in_=b[pb:pb + nk, gi * 128:(gi + 1) * 128],
                                                               func=AF.Copy), writes=[bk, ('vc', par)])
                    self.bank_rng = (4, 8)

                nqb = n // 128
                segbanks = {}
                stA = {}

                def stage_a(ci):
                    nonlocal pti
                    (r, j, k0, nk, pb, idx) = chunks[ci]
                    if j == -1:
                        qbs, mcol0 = [0], 256
                    else:
                        qbs = [qb for qb in (j - 1, j) if 0 <= qb < nqb]
                        mcol0 = 0 if qbs[0] == j - 1 else 128
                    q0 = r * n + 128 * qbs[0]
                    nq = 128 * len(qbs)
                    res = []
                    for h in range(2):
                        hs = slice(64 * h, 64 * h + 64)
                        bS, bkS = self.bank()
                        p_ = pT[pti % NPT]
                        pk = ('pT', pti % NPT)
                        pti += 1
                        S.op('pe', lambda e: e.matmul(bS[pb:pb + nk, 0:nq], lhsT=kr[hs, r * n + k0: r * n + k0 + nk],
                                                      rhs=qr[hs, q0:q0 + nq], start=True, stop=True),
                             reads=[('qk', par, 0), ('qk', par, 1)], writes=[bkS])
                        res.append((bS, bkS, p_, pk))
                    for h in range(2):
                        bS, bkS, p_, pk = res[h]
                        S.op('act', lambda e: e.activation(out=p_[pb:pb + nk, 0:nq], in_=bS[pb:pb + nk, 0:nq],
                                                           func=AF.Exp, scale=0.125), writes=[bkS, pk])
                        S.op('dve' if h == 0 else 'pool', lambda e: e.tensor_tensor(out=p_[pb:pb + nk, 0:nq], in0=p_[pb:pb + nk, 0:nq],
                                                              in1=self.mask[pb:pb + nk, mcol0:mcol0 + nq],
                                                              op=ALU.mult), reads=['consts'], writes=[pk])
                    stA[ci] = (qbs, res)

                def stage_b(ci):
                    (r, j, k0, nk, pb, idx) = chunks[ci]
                    qbs, res = stA.pop(ci)
                    for qi, qb in enumerate(qbs):
                        gq = (r * n) // 128 + qb
                        seg = gq // 4
                        if seg not in segbanks:
                            sb = 2 * (seg % 2)
                            segbanks[seg] = ((self.ps[sb], ('ps', sb)), (self.ps[sb + 1], ('ps', sb + 1)))
                        (bN, bkN), (bD, bkD) = segbanks[seg]
                        col = (gq % 4) * 128
                        first = (qb == j) or j == -1
                        last = (qb != j) or j == -1
                        for h in range(2):
                            hs = slice(64 * h, 64 * h + 64)
                            bS, bkS, p_, pk = res[h]
                            S.op('pe', lambda e: e.matmul(bN[hs, col:col + 128], lhsT=vc[pb:pb + nk, idx, hs],
                                                          rhs=p_[pb:pb + nk, qi * 128:(qi + 1) * 128],
                                                          start=first, stop=last),
                                 reads=[pk, ('vc', par)], writes=[bkN])
                        for h in range(2):
                            hs = slice(64 * h, 64 * h + 64)
                            bS, bkS, p_, pk = res[h]
                            S.op('pe', lambda e: e.matmul(bD[hs, col:col + 128], lhsT=self.onb[pb:pb + nk, :],
                                                          rhs=p_[pb:pb + nk, qi * 128:(qi + 1) * 128],
                                                          start=first, stop=last),
                                 reads=[pk, 'consts'], writes=[bkD])
                    if j >= 1 or j == -1:
                        gq = (r * n) // 128 + (j - 1 if j >= 1 else 0)
                        if gq % 4 == 3:
                            seg = gq // 4
                            (bN, bkN), (bD, bkD) = segbanks.pop(seg)
                            for bsrc, bks, acc, ak in ((bN, bkN, accN, 'accN'), (bD, bkD, accD, 'accD')):
                                av = perm_view(acc, d, 512 * seg, 512)
                                src = like(bsrc[:, :], av)
                                if g == 0 and ak == 'accN':
                                    S.op('act', lambda e: e.activation(out=av, in_=src, func=AF.Copy),
                                         writes=[bks, ak])
                                elif g == 0:
                                    S.op('dve', lambda e: e.tensor_copy(out=av, in_=src), writes=[bks, ak])
                                else:
                                    S.op('dve', lambda e: e.tensor_tensor(out=av, in0=src, in1=av, op=ALU.add),
                                         writes=[bks, ak])

                nxt = [(hp_, g_) for hp_ in range(4) for g_ in range(3)]
                ni = nxt.index((hp, g)) + 1
                pend = {}

                def prefetch_step(ci):
                    if ni >= len(nxt):
                        return
                    nhp, ng = nxt[ni]
                    nbase = 1536 + ng * 1536 + nhp * 128

                    def src(which):
                        return W[:, nbase + 512 * which: nbase + 512 * which + 128]
                    if ci == 1:
                        pend[0] = self.load_w_dma(src(0), 8, 128)
                        pend[1] = self.load_w_dma(src(1), 8, 128)
                    elif ci == 6:
                        for which in range(2):
                            self.pre[('att', l, nhp, ng, which)] = self.load_w_cast(pend.pop(which), 'act')
                        pend[2] = self.load_w_dma(src(2), 8, 128)
                    elif ci == 11:
                        self.pre[('att', l, nhp, ng, 2)] = self.load_w_cast(pend.pop(2), 'act')

                vsplit = 8 if len(chunks) > 16 else 4
                v_proj(0, vsplit, (0, 8))
                stage_a(0)
                stage_a(1)
                v_proj(vsplit, len(chunks), (0, 4))
                for ci in range(len(chunks)):
                    if ci + 2 < len(chunks):
                        stage_a(ci + 2)
                    stage_b(ci)
                    prefetch_step(ci)
                assert not segbanks
            S.op('act', lambda e: e.activation(out=accD, in_=accD, func=AF.Ln), writes=['accD'])
            S.op('act', lambda e: e.activation(out=accD, in_=accD, func=AF.Exp, scale=-1.0), writes=['accD'])
            S.op('dve', lambda e: e.tensor_tensor(out=yatt[:, hp, :], in0=accN, in1=accD, op=ALU.mult),
                 reads=['accN', 'accD'], writes=['yatt'])
        self.dump('yatt', self.fv(O_U, 4096), ['yatt'])
        self.bank_rng = (0, 8)
        self.prefetch_merge(l)
        S.barrier_all()

    def merge(self, l):
        S = self.S
        hT = self.hT()
        W = self.w["w_in"][l]
        gmix = self.fv(C_GMIX + 8 * l, 8)
        yhy = self.bv(O_X, 4096).rearrange("p (c t) -> p c t", t=T)
        yatt = self.bv(O_U, 4096).rearrange("p (c t) -> p c t", t=T)
        merged = self.bv(O_U + 17408, 8192).rearrange("p (c t) -> p c t", t=T)
        tmp = [[self.fv(O_U + 13312 + 512 * (4 * i + j), 512) for j in range(4)] for i in range(2)]
        it = 0
        for fc in range(8):
            wg1, k1 = self.take(('mg', l, fc, 0), lambda: self.load_w(W[:, 6144 + fc * 128: 6144 + fc * 128 + 128], 8, 128))
            wg2, k2 = self.take(('mg', l, fc, 1), lambda: self.load_w(W[:, 7168 + fc * 128: 7168 + fc * 128 + 128], 8, 128))
            wp1, k3 = self.take(('mg', l, fc, 2), lambda: self.load_w(self.w["p_hy"][l][:, fc * 128:(fc + 1) * 128], 4, 128))
            wp2, k4 = self.take(('mg', l, fc, 3), lambda: self.load_w(self.w["p_att"][l][:, fc * 128:(fc + 1) * 128], 4, 128))
            for tb in range(4):
                sl = slice(tb * 512, (tb + 1) * 512)
                s1, s2, m1, m2 = tmp[it % 2]
                tk = it % 2
                it += 1
                b1, bk1 = self.bank()
                b2, bk2 = self.bank()
                b3, bk3 = self.bank()
                b4, bk4 = self.bank()
                for kc in range(8):
                    S.op('pe', lambda e: e.matmul(b1[:, :], lhsT=wg1[:, kc, :], rhs=hT[:, kc, sl], start=(kc == 0),
                                                  stop=(kc == 7)), reads=[k1, ('hT', tb)], writes=[bk1], inc=(kc == 7))
                for kc in range(8):
                    S.op('pe', lambda e: e.matmul(b2[:, :], lhsT=wg2[:, kc, :], rhs=hT[:, kc, sl], start=(kc == 0),
                                                  stop=(kc == 7)), reads=[k2, ('hT', tb)], writes=[bk2], inc=(kc == 7))
                for kc in range(4):
                    S.op('pe', lambda e: e.matmul(b3[:, :], lhsT=wp1[:, kc, :], rhs=yhy[:, kc, sl], start=(kc == 0),
                                                  stop=(kc == 3)), reads=[k3, 'yhy'], writes=[bk3], inc=(kc == 3))
                for kc in range(4):
                    S.op('pe', lambda e: e.matmul(b4[:, :], lhsT=wp2[:, kc, :], rhs=yatt[:, kc, sl], start=(kc == 0),
                                                  stop=(kc == 3)), reads=[k4, 'yatt'], writes=[bk4], inc=(kc == 3))
                S.op('act', lambda e: e.activation(out=s1, in_=b1[:, :], func=AF.Sigmoid), writes=[bk1, ('s1', tk)])
                S.op('act', lambda e: e.activation(out=s2, in_=b2[:, :], func=AF.Sigmoid), writes=[bk2, ('s2', tk)])
                S.op('dve', lambda e: e.tensor_tensor(out=m1, in0=b3[:, :], in1=s1, op=ALU.mult),
                     reads=[('s1', tk)], writes=[bk3, ('m1', tk)])
                S.op('dve', lambda e: e.tensor_tensor(out=m2, in0=b4[:, :], in1=s2, op=ALU.mult),
                     reads=[('s2', tk)], writes=[bk4, ('m2', tk)])
                S.op('dve', lambda e: e.tensor_tensor(out=merged[:, fc, sl], in0=m1, in1=m2, op=ALU.add),
                     reads=[('m1', tk), ('m2', tk)], writes=[('merged', tb)])
        self.prefetch_wo(l)
        S.barrier_all()
        xT = self.xT()
        for c in range(8):
            S.dma('sp', self.fv(O_X + c * T, T), self.xs[:, c * T:(c + 1) * T], reads=[('xs', c)],
                  writes=[('xT', c, tb) for tb in range(4)])
        wos = [self.take(('wo', l, j), lambda: self.load_w(self.w["w_o"][l][:, j * 256:(j + 1) * 256], 8, 256))
               for j in range(4)]
        gffn = self.fv(C_GFFN + 8 * l, 8)
        hT = self.hT()
        sq = [self.bv(O_TMP + 512 * i, 256) for i in range(2)]
        rs_ = [self.fv(O_TMP + 1024 + 512 * i, 512) for i in range(4)]
        def stats(tb):
            sl = slice(tb * 512, (tb + 1) * 512)
            sb, sbk = self.ps[4 + tb], ('ps', 4 + tb)
            for c in range(8):
                i = c % 2
                S.op('act', lambda e: e.activation(out=sq[i], in_=xT[:, c, sl], func=AF.Square),
                     reads=[('xT', c, tb)], writes=[('sq', i)])
                S.op('pe', lambda e: e.matmul(sb[:, :], lhsT=self.onb128, rhs=sq[i], start=(c == 0), stop=(c == 7)),
                     reads=[('sq', i), 'consts'], writes=[sbk])

        self.bank_rng = (0, 4)
        for tb in range(4):
            sl = slice(tb * 512, (tb + 1) * 512)
            for oc in range(8):
                wo, ko = wos[oc // 2]
                b, bk = self.bank()
                for kc in range(8):
                    S.op('pe', lambda e: e.matmul(b[:, :], lhsT=wo[:, kc, (oc % 2) * 128:(oc % 2) * 128 + 128],
                                                  rhs=merged[:, kc, sl], start=(kc == 0), stop=(kc == 7)),
                         reads=[ko, ('merged', tb)], writes=[bk], inc=(kc == 7))
                S.op('dve', lambda e: e.tensor_tensor(out=xT[:, oc, sl], in0=b[:, :], in1=xT[:, oc, sl], op=ALU.add),
                     writes=[bk, ('xT', oc, tb)])
            if tb >= 1:
                stats(tb - 1)
        stats(3)
        self.bank_rng = (0, 8)
        for tb in range(4):
            sb, sbk = self.ps[4 + tb], ('ps', 4 + tb)
            S.op('act', lambda e: e.activation(out=rs_[tb], in_=sb[:, :], func=AF.Ln, bias=self.eps, scale=1.0 / D),
                 reads=['consts'], writes=[sbk, ('rs', tb)])
        for tb in range(4):
            S.op('act', lambda e: e.activation(out=rs_[tb], in_=rs_[tb], func=AF.Exp, scale=-0.5),
                 writes=[('rs', tb)])
        for tb in range(4):
            sl = slice(tb * 512, (tb + 1) * 512)
            for c in range(8):
                S.op('dve', lambda e: e.scalar_tensor_tensor(out=hT[:, c, sl], in0=xT[:, c, sl], scalar=gffn[:, c:c + 1],
                                                             in1=rs_[tb], op0=ALU.mult, op1=ALU.mult),
                     reads=[('xT', c, tb), ('rs', tb), 'consts'], writes=[('hT', tb)])
        self.prefetch_ffn(l)

    def ffn(self, l):
        S = self.S
        xT, hT = self.xT(), self.hT()
        gffn = self.fv(C_GFFN + 8 * l, 8)
        a_ = [self.bv(O_TMP + 3072 + 1024 * i, 1024).rearrange("p (f t) -> p f t", t=512) for i in range(2)]
        r_ = [self.fv(O_TMP + 5120 + 512 * i, 512) for i in range(2)]
        W1 = self.w["w_ff1"][l]
        W2 = self.w["w_ff2"][l]
        S.op('dve', lambda e: e.memset(r_[0][0:1, 0:2], 0.0),
             writes=[('r', 0), ('r', 1)] + [('merged', t) for t in range(4)])
        ri = [0]
        wts = {}

        def get_w1(fg):
            if ('w1', fg) not in wts:
                wts[('w1', fg)] = self.take(('w1', l, fg), lambda: [
                    self.load_w(W1[:, fg * 512 + 256 * i: fg * 512 + 256 * i + 256], 8, 256) for i in range(2)])
            return wts[('w1', fg)]

        def get_w2(fg):
            if ('w2', fg) not in wts:
                wts[('w2', fg)] = [self.load_w(W2[fg * 512:(fg + 1) * 512, 512 * i: 512 * i + 512], 4, 512)
                                   for i in range(2)]
            return wts[('w2', fg)]

        items = [(fg, tb) for fg in range(8) for tb in range(4)]

        def stage1(k):
            fg, tb = items[k]
            sl = slice(tb * 512, (tb + 1) * 512)
            w1t = get_w1(fg)
            if tb == 0 and fg == 0:
                get_w2(fg)
            if tb == 2 and fg + 1 < 8:
                get_w1(fg + 1)
            if tb == 3 and fg + 1 < 8:
                get_w2(fg + 1)
            a = a_[k % 2]
            for f in range(4):
                wt, wk = w1t[f // 2]
                b, bk = self.bank()
                rr = r_[ri[0] % 2]
                rk = ('r', ri[0] % 2)
                ri[0] += 1
                for kc in range(8):
                    S.op('pe', lambda e: e.matmul(b[:, :], lhsT=wt[:, kc, (f % 2) * 128:(f % 2) * 128 + 128],
                                                  rhs=hT[:, kc, sl], start=(kc == 0), stop=(kc == 7)),
                         reads=[wk, ('hT', tb)], writes=[bk], inc=(kc == 7))
                S.op('act', lambda e: e.activation(out=rr, in_=b[:, :], func=AF.Relu), writes=[bk, rk])
                S.op('dve', lambda e: e.tensor_tensor(out=a[:, f, :], in0=rr, in1=rr, op=ALU.mult),
                     reads=[rk], writes=[('a', k % 2)])

        def stage2(k):
            fg, tb = items[k]
            sl = slice(tb * 512, (tb + 1) * 512)
            w2t = get_w2(fg)
            a = a_[k % 2]
            for dc in range(8):
                wt, wk = w2t[dc // 4]
                b, bk = self.bank()
                for f in range(4):
                    S.op('pe', lambda e: e.matmul(b[:, :], lhsT=wt[:, f, (dc % 4) * 128:(dc % 4) * 128 + 128],
                                                  rhs=a[:, f, :], start=(f == 0), stop=(f == 3)),
                         reads=[wk, ('a', k % 2)], writes=[bk], inc=(f == 3))
                S.op('dve', lambda e: e.tensor_tensor(out=xT[:, dc, sl], in0=b[:, :], in1=xT[:, dc, sl],
                                                      op=ALU.add), writes=[bk, ('xT', dc, tb)])

        stage1(0)
        for k in range(len(items)):
            if k + 1 < len(items):
                stage1(k + 1)
            stage2(k)
        S.barrier_all()

    def final(self, s):
        S = self.S
        xT = self.xT()
        gfin = self.fv(C_GFIN, 8)
        tmp = O_TMP
        sq = [self.bv(tmp + 512 * i, 256) for i in range(2)]
        rst = [self.fv(tmp + 1024 + 512 * i, 512) for i in range(4)]
        yn_ = [self.fv(tmp + 3072 + 4096 * i, 4096).rearrange("p (c t) -> p c t", t=512) for i in range(2)]
        yo = [self.fv(tmp + 11264 + 1024 * i, 1024) for i in range(2)]
        oi = [0]

        def stats_all():
            banks = []
            for tb in range(4):
                sl = slice(tb * 512, (tb + 1) * 512)
                b, bk = self.bank(0, 4)
                banks.append((b, bk))
                for c in range(8):
                    i = c % 2
                    S.op('act', lambda e: e.activation(out=sq[i], in_=xT[:, c, sl], func=AF.Square),
                         reads=[('xT', c, tb)], writes=[('sq', i)])
                    S.op('pe', lambda e: e.matmul(b[:, :], lhsT=self.onb128, rhs=sq[i], start=(c == 0), stop=(c == 7)),
                         reads=[('sq', i), 'consts'], writes=[bk])
            for tb in range(4):
                b, bk = banks[tb]
                S.op('act', lambda e: e.activation(out=rst[tb], in_=b[:, :], func=AF.Ln, bias=self.eps,
                                                   scale=1.0 / D), reads=['consts'], writes=[bk, ('rs', tb)])
            for tb in range(4):
                S.op('act', lambda e: e.activation(out=rst[tb], in_=rst[tb], func=AF.Exp, scale=-0.5),
                     writes=[('rs', tb)])

        def scale(tb):
            sl = slice(tb * 512, (tb + 1) * 512)
            yn = yn_[tb % 2]
            ynk = ('yn', tb % 2)
            for c in range(8):
                S.op('dve', lambda e: e.scalar_tensor_tensor(out=yn[:, c, :], in0=xT[:, c, sl], scalar=gfin[:, c:c + 1],
                                                             in1=rst[tb], op0=ALU.mult, op1=ALU.mult),
                     reads=[('xT', c, tb), ('rs', tb), 'consts'], writes=[ynk])

        def store(tb):
            yn = yn_[tb % 2]
            ynk = ('yn', tb % 2)
            for a in range(4):
                y = yo[oi[0] % 2]
                yk = ('yo', oi[0] % 2)
                oi[0] += 1
                for half in range(2):
                    b2, bk2 = self.bank()
                    for cc in range(4):
                        c = half * 4 + cc
                        S.op('pe', lambda e: e.transpose(out=b2[:, cc * 128:(cc + 1) * 128],
                                                         in_=yn[:, c, a * 128:(a + 1) * 128], identity=self.idf),
                             reads=[ynk, 'consts'], writes=[bk2], inc=(cc == 3))
                    S.op('act', lambda e: e.activation(out=y[:, half * 512:(half + 1) * 512], in_=b2[:, :],
                                                       func=AF.Copy), writes=[bk2, yk])
                t0 = tb * 512 + a * 128
                S.dma('act', self.out[s, t0:t0 + 128, :], y, reads=[yk], writes=['out'])

        stats_all()
        scale(0)
        self.bank_rng = (4, 8)
        for tb in range(4):
            if tb + 1 < 4:
                scale(tb + 1)
            store(tb)
        self.bank_rng = (0, 8)
        S.barrier_all()

    def build(self):
        S = self.S
        self.load_consts()
        for l in range(self.depth):
            self.filter_prologue(l)
        for s in range(self.nseq):
            self.load_x(s)
            for l in range(self.depth):
                self.norm(AW - 3072, self.fv(C_GMIX + 8 * l, 8))
                self.prefetch_hyena(l)
                self.spill_x()
                self.hyena(l)
                self.attention(l)
                self.merge(l)
                self.ffn(l)
            self.final(s)
        S.barrier_all()


_CONSTS = None


def make_consts():
    global _CONSTS
    if _CONSTS is not None:
        return _CONSTS
    bf = ml_dtypes.bfloat16
    c = {}
    ident = np.eye(128, dtype=np.float32)
    c["c_f32"] = np.concatenate([ident, np.ones((128, 128), np.float32)], axis=1)
    rot = np.zeros((128, 128), np.float32)
    for hb in (0, 64):
        for e in range(32):
            rot[hb + e + 32, hb + e] = -1.0
            rot[hb + e, hb + e + 32] = 1.0
    k = np.arange(128)[:, None]
    q = np.arange(128)[None, :]
    mask = np.concatenate([(k <= q), (k >= q), (np.abs(k - q) <= 64)], axis=1).astype(np.float32)
    c["c_bf"] = np.concatenate([ident, np.ones((128, 128), np.float32), rot, mask], axis=1).astype(bf)
    half = 32
    inv_freq = (10000.0 ** (-np.arange(half, dtype=np.float32) / half)).astype(np.float32)
    pos = np.arange(T, dtype=np.float32)
    ang = (pos[None, :] * inv_freq[:, None]).astype(np.float32)
    ang128 = np.tile(ang, (4, 1))
    c["c_cs"] = np.concatenate([np.cos(ang128), np.sin(ang128)], axis=1).astype(np.float32)
    a = np.arange(T, dtype=np.int64)[:, None]
    f = np.arange(T, dtype=np.int64)[None, :]
    ph = ((a * f) % NFFT).astype(np.float64) * (2 * np.pi / NFFT)
    M = np.concatenate([np.cos(ph), np.sin(ph)], axis=1)
    M[:, 2048] = np.where(np.arange(T) % 2 == 0, 1.0, -1.0)
    M = M.astype(np.float32)
    mf = M.reshape(16, 128, 32, 128).transpose(2, 1, 0, 3).reshape(32, 128, 2048)
    c["c_mf"] = np.ascontiguousarray(mf).astype(bf)
    MT = M.T.reshape(8, 4, 128, 4, 512).transpose(3, 0, 2, 1, 4).reshape(4, 8, 128, 2048)
    c["c_mi"] = np.ascontiguousarray(MT).astype(bf)
    L = T
    nn = np.arange(L, dtype=np.float32)
    t = (nn / np.float32(L - 1)).astype(np.float32)
    bands = np.linspace(1e-4, 15, 16, dtype=np.float32)
    angf = (np.float32(2.0 * math.pi / L) * nn[:, None] * bands[None, :]).astype(np.float32)
    z = np.concatenate([t[:, None], np.cos(angf), -np.sin(angf)], axis=-1).astype(np.float32)
    c["c_zt"] = np.ascontiguousarray(z.T)
    max_decay = math.log(1e-2) / 0.3
    min_decay = math.log(1e-2) / 1.5
    deltas = np.abs(np.linspace(min_decay, max_decay, HW, dtype=np.float32))
    window = np.exp(-t[:, None] * deltas[None, :]).astype(np.float32)
    c["c_win"] = np.ascontiguousarray(window.reshape(16, 128, 512))
    sf = np.full((128, 16), 2.0 / NFFT, np.float32)
    sf[0, 0] = 1.0 / NFFT
    c["c_sf"] = sf
    _CONSTS = c
    return c


_NC_CACHE = {}


def get_nc(nseq, depth, dbg=None):
    key = (nseq, depth, tuple(sorted((dbg or {}).items())))
    if key not in _NC_CACHE:
        nc = bass.Bass("TRN2", target_bir_lowering=False)
        with ExitStack() as es:
            p = Prog(nc, es, nseq, depth, dbg)
            p.build()
        _NC_CACHE[key] = nc
    return _NC_CACHE[key]


WNAMES = ["norm_mix", "w_in", "conv_w", "conv_b", "filt_w1", "filt_b1", "filt_w_inner", "filt_b_inner",
          "filt_w_out", "filt_freq", "hy_skip", "p_hy", "p_att", "w_o", "norm_ffn", "w_ff1", "w_ff2", "norm_final"]


def kernel(**inputs):
    x = np.ascontiguousarray(np.asarray(inputs["x"], dtype=np.float32))
    consts = make_consts()
    base = {k: np.ascontiguousarray(np.asarray(inputs[k], dtype=np.float32)) for k in WNAMES}
    base.update(consts)
    nc = get_nc(NSEQ, DEPTH)
    in_maps = []
    for c in range(NCORE):
        m = dict(base)
        m["x"] = x[c * NSEQ:(c + 1) * NSEQ]
        in_maps.append(m)
    res = run_bass_kernel_spmd(nc, in_maps, core_ids=list(range(NCORE)))
    out = np.concatenate([res.results[c]["out"] for c in range(NCORE)], axis=0)
    return out.astype(np.float32)
```
